# Optimizing a Trainium2 kernel written in Bass

```python
import jax
import jax.numpy as jnp
from jax import lax
import numpy as np

D_MODEL = 1024
BATCH = 8
SEQ = 4096
DEPTH = 2

GRID_W = 64
CTX_LEN = 256
POOL_WINDOWS = (2, 4, 8, 16)
POOL_GROUP = D_MODEL // 8
POOL_WIDTH = POOL_GROUP * len(POOL_WINDOWS)
CONV_WIDTH = D_MODEL - POOL_WIDTH
CONV_K = 31
MLSTM_HEADS = 4
MLSTM_INNER = 2 * D_MODEL
MLSTM_HEAD_DIM = MLSTM_INNER // MLSTM_HEADS
MLSTM_CONV_K = 3
MLSTM_CHUNK = 64
FFN_HIDDEN = ((8 * D_MODEL + 3 * 256 - 1) // (3 * 256)) * 256
N_EVEN = (DEPTH + 1) // 2
N_ODD = DEPTH // 2
DEEPNORM_ALPHA = (2.0 * DEPTH) ** 0.25
DEEPNORM_BETA = (8.0 * DEPTH) ** -0.25
LN_EPS = 1e-5

kernel_name = "hybrid_pool_conv_mlstm_dit_block"


def layer_norm(x, g, b):
    xf = x.astype(jnp.float32)
    mu = jnp.mean(xf, axis=-1, keepdims=True)
    var = jnp.mean(jnp.square(xf - mu), axis=-1, keepdims=True)
    return ((xf - mu) * lax.rsqrt(var + LN_EPS) * g + b).astype(x.dtype)


def post_norm(x, branch, g, b):
    return layer_norm(DEEPNORM_ALPHA * x + branch, g, b)


def modulate(x, shift, scale):
    return x * (1 + scale) + shift


def swiglu(u, w_in, w_out):
    a, g = jnp.split(u @ w_in, 2, axis=-1)
    return (jax.nn.silu(a) * g) @ w_out


def depthwise_conv1d(x, w, b):
    K = w.shape[0]
    y = lax.conv_general_dilated(x, w[:, None, :], window_strides=(1,),
                                 padding=[(K // 2, K - 1 - K // 2)],
                                 dimension_numbers=('NWC', 'WIO', 'NWC'),
                                 feature_group_count=x.shape[-1])
    return y + b


def box_mean(x, w, axis):
    L = x.shape[axis]
    pad = [(0, 0)] * x.ndim
    pad[axis] = (1, 0)
    cs = jnp.pad(jnp.cumsum(x, axis=axis), pad)
    t = jnp.arange(L)
    lo = jnp.clip(t - w // 2, 0, L)
    hi = jnp.clip(t + w - w // 2, 0, L)
    s = jnp.take(cs, hi, axis=axis) - jnp.take(cs, lo, axis=axis)
    shape = [1] * x.ndim
    shape[axis] = L
    return s / (hi - lo).astype(jnp.float32).reshape(shape)


def pool_conv_mixer(u, rows, w_in, pool_w, pool_b, pool_scale, conv_w, conv_b, norm_g, norm_b, w_out):
    B, L, _ = u.shape
    p = u @ w_in
    a = p[..., :POOL_WIDTH]
    val = p[..., POOL_WIDTH:POOL_WIDTH + CONV_WIDTH]
    gate = p[..., POOL_WIDTH + CONV_WIDTH:]
    diffs = []
    for gi, w in enumerate(POOL_WINDOWS):
        seg = a[..., gi * POOL_GROUP:(gi + 1) * POOL_GROUP].astype(jnp.float32)
        if rows is None:
            pooled = box_mean(seg, w, 1)
        else:
            g2 = seg.reshape(B, rows, GRID_W, POOL_GROUP)
            pooled = box_mean(box_mean(g2, w, 1), w, 2).reshape(B, L, POOL_GROUP)
        diffs.append((pooled - seg).astype(u.dtype))
    d = jnp.stack(diffs, axis=2)
    y_a = (jnp.einsum('blgc,gcd->blgd', d, pool_w).reshape(B, L, POOL_WIDTH) + pool_b) * pool_scale
    glu = val * jax.nn.sigmoid(gate)
    y_b = jax.nn.silu(layer_norm(depthwise_conv1d(glu, conv_w, conv_b), norm_g, norm_b))
    return jnp.concatenate([y_a, y_b], axis=-1) @ w_out


def mlstm_zero_state(B):
    H, DH = MLSTM_HEADS, MLSTM_HEAD_DIM
    return (jnp.zeros((B, H, DH, DH), jnp.float32), jnp.zeros((B, H, DH), jnp.float32),
            jnp.zeros((B, H), jnp.float32))


def mlstm_features(xm, conv_w, conv_b, wq, wk, wv, w_gate, b_gate):
    B, L, E = xm.shape
    H, DH = MLSTM_HEADS, MLSTM_HEAD_DIM
    xc = jax.nn.silu(depthwise_conv1d(xm, conv_w, conv_b))
    xch = xc.reshape(B, L, H, DH)
    q = jnp.einsum('blhd,hde->blhe', xch, wq)
    k = jnp.einsum('blhd,hde->blhe', xch, wk)
    v = jnp.einsum('blhd,hde->blhe', xm.reshape(B, L, H, DH), wv)
    g = (jnp.einsum('ble,zeg->zblg', q.reshape(B, L, E), w_gate[:, 0])
         + jnp.einsum('ble,zeg->zblg', k.reshape(B, L, E), w_gate[:, 1])
         + jnp.einsum('ble,zeg->zblg', v.reshape(B, L, E), w_gate[:, 2])).astype(jnp.float32)
    g = g + b_gate[:, None, None, :].astype(jnp.float32)
    ig = g[..., :H]
    lf = jax.nn.log_sigmoid(g[..., H:])
    return xc, q, k, v, ig, lf


def mlstm_chunk_scan(k, v, ig, lf, state, q=None):
    B, L, H, _ = k.shape
    T = MLSTM_CHUNK
    nc = L // T

    def chunks(a):
        a = a.reshape((B, nc, T, H) + a.shape[3:])
        return jnp.moveaxis(jnp.moveaxis(a, 1, 0), 3, 2)

    def update(carry, kc, vc, ic, fc):
        C, n, m = carry
        b = jnp.cumsum(fc, axis=-1)
        b_last = b[..., -1]
        w_end = b_last[..., None] - b + ic
        m_new = jnp.maximum(b_last + m, jnp.max(w_end, axis=-1))
        decay = jnp.exp(b_last + m - m_new)
        kw = kc * jnp.exp(w_end - m_new[..., None])[..., None]
        C_new = decay[..., None, None] * C + jnp.einsum('bhtk,bhtv->bhkv', kw, vc)
        n_new = decay[..., None] * n + jnp.sum(kw, axis=2)
        return (C_new, n_new, m_new), b

    if q is None:
        def step_state(carry, xs):
            new, _ = update(carry, *xs)
            return new, None
        final, _ = lax.scan(step_state, state, (chunks(k), chunks(v), chunks(ig), chunks(lf)))
        return None, final

    causal = jnp.tril(jnp.ones((T, T), dtype=bool))

    def step(carry, xs):
        qc, kc, vc, ic, fc = xs
        C, n, m = carry
        new, b = update(carry, kc, vc, ic, fc)
        dmat = jnp.where(causal, b[..., :, None] - b[..., None, :] + ic[..., None, :], -jnp.inf)
        inter = b + m[..., None]
        m_t = jnp.maximum(inter, jnp.max(dmat, axis=-1))
        s = jnp.einsum('bhtk,bhsk->bhts', qc, kc) * jnp.exp(dmat - m_t[..., None])
        iw = jnp.exp(inter - m_t)
        num = iw[..., None] * jnp.einsum('bhtk,bhkv->bhtv', qc, C) + jnp.einsum('bhts,bhsv->bhtv', s, vc)
        den = iw * jnp.einsum('bhtk,bhk->bht', qc, n) + jnp.sum(s, axis=-1)
        h = num / jnp.maximum(jnp.abs(den), jnp.exp(-m_t))[..., None]
        return new, h

    final, hs = lax.scan(step, state, (chunks(q), chunks(k), chunks(v), chunks(ig), chunks(lf)))
    hs = jnp.moveaxis(jnp.moveaxis(hs, 0, 1), 2, 3).reshape(B, L, H, -1)
    return hs, final


def _flip(a, d):
    return jnp.flip(a, axis=1) if d == 1 else a


def mlstm_bidir(q, k, v, ig, lf, init_states, with_output):
    kf = k.astype(jnp.float32) * (MLSTM_HEAD_DIM ** -0.5)
    vf = v.astype(jnp.float32)
    qf = q.astype(jnp.float32) if with_output else None
    hs, finals = [], []
    for d in range(2):
        h, st = mlstm_chunk_scan(_flip(kf, d), _flip(vf, d), _flip(ig[d], d), _flip(lf[d], d),
                                 init_states[d], _flip(qf, d) if with_output else None)
        finals.append(st)
        if with_output:
            hs.append(_flip(h, d))
    h_sum = hs[0] + hs[1] if with_output else None
    return h_sum, (finals[0], finals[1])


def mlstm_output(h, xc, z, norm_g, skip, w_out):
    B, L = h.shape[:2]
    mu = jnp.mean(h, axis=-1, keepdims=True)
    var = jnp.mean(jnp.square(h - mu), axis=-1, keepdims=True)
    hn = ((h - mu) * lax.rsqrt(var + LN_EPS)).reshape(B, L, MLSTM_INNER) * norm_g
    y = (hn.astype(xc.dtype) + skip * xc) * jax.nn.silu(z)
    return y @ w_out


def mlstm_mixer(u, uc, w_in, conv_w, conv_b, wq, wk, wv, w_gate, b_gate, norm_g, skip, w_out, ctx_out):
    E = MLSTM_INNER
    feat = (conv_w, conv_b, wq, wk, wv, w_gate, b_gate)
    zero = mlstm_zero_state(u.shape[0])
    if ctx_out:
        xm_c, z_c = jnp.split(uc @ w_in, 2, axis=-1)
    else:
        xm_c = uc @ w_in[:, :E]
    xc_c, q_c, k_c, v_c, ig_c, lf_c = mlstm_features(xm_c, *feat)
    h_c, ctx_states = mlstm_bidir(q_c, k_c, v_c, ig_c, lf_c, (zero, zero), ctx_out)
    xm, z = jnp.split(u @ w_in, 2, axis=-1)
    xc, q, k, v, ig, lf = mlstm_features(xm, *feat)
    h_l, _ = mlstm_bidir(q, k, v, ig, lf, ctx_states, True)
    y = mlstm_output(h_l, xc, z, norm_g, skip, w_out)
    y_c = mlstm_output(h_c, xc_c, z_c, norm_g, skip, w_out) if ctx_out else None
    return y, y_c


def setup_inputs(seed: int = 0) -> dict:
    key = jax.random.key(seed)
    ks = jax.random.split(key, 32)
    D, E, H, DH, F = D_MODEL, MLSTM_INNER, MLSTM_HEADS, MLSTM_HEAD_DIM, FFN_HIDDEN

    def nrm(k, shape, s):
        return jax.random.normal(k, shape, jnp.float32) * s

    f_bias = jnp.broadcast_to(jnp.linspace(3.0, 6.0, H, dtype=jnp.float32), (N_ODD, 2, H))
    ml_b_gate = jnp.concatenate([nrm(ks[22], (N_ODD, 2, H), 0.1),
                                 f_bias + nrm(ks[23], (N_ODD, 2, H), 0.1)], axis=-1)
    return {
        'x': nrm(ks[0], (BATCH, SEQ, D), 1.0),
        'c': nrm(ks[1], (BATCH, D), 1.0),
        'ctx': nrm(ks[2], (BATCH, CTX_LEN, D), 1.0),
        'c_ctx': nrm(ks[3], (D,), 1.0),
        'mod_w': nrm(ks[4], (DEPTH, D, 6 * D), 0.3 * D ** -0.5),
        'mod_b': nrm(ks[5], (DEPTH, 6 * D), 0.02),
        'ln_g': 1.0 + nrm(ks[6], (DEPTH, 2, D), 0.05),
        'ln_b': nrm(ks[7], (DEPTH, 2, D), 0.02),
        'ab_w_in': nrm(ks[8], (N_EVEN, D, POOL_WIDTH + 2 * CONV_WIDTH), D ** -0.5),
        'ab_pool_w': nrm(ks[9], (N_EVEN, len(POOL_WINDOWS), POOL_GROUP, POOL_GROUP), POOL_GROUP ** -0.5),
        'ab_pool_b': nrm(ks[10], (N_EVEN, POOL_WIDTH), 0.02),
        'ab_pool_scale': 1.0 + nrm(ks[11], (N_EVEN, POOL_WIDTH), 0.1),
        'ab_conv_w': nrm(ks[12], (N_EVEN, CONV_K, CONV_WIDTH), CONV_K ** -0.5),
        'ab_conv_b': nrm(ks[13], (N_EVEN, CONV_WIDTH), 0.02),
        'ab_norm_g': 1.0 + nrm(ks[14], (N_EVEN, CONV_WIDTH), 0.05),
        'ab_norm_b': nrm(ks[15], (N_EVEN, CONV_WIDTH), 0.02),
        'ab_w_out': nrm(ks[16], (N_EVEN, POOL_WIDTH + CONV_WIDTH, D), DEEPNORM_BETA * (POOL_WIDTH + CONV_WIDTH) ** -0.5),
        'ml_w_in': nrm(ks[17], (N_ODD, D, 2 * E), D ** -0.5),
        'ml_conv_w': nrm(ks[18], (N_ODD, MLSTM_CONV_K, E), MLSTM_CONV_K ** -0.5),
        'ml_conv_b': nrm(ks[19], (N_ODD, E), 0.02),
        'ml_wq': nrm(ks[20], (N_ODD, H, DH, DH), DH ** -0.5),
        'ml_wk': nrm(ks[21], (N_ODD, H, DH, DH), DH ** -0.5),
        'ml_wv': nrm(ks[24], (N_ODD, H, DH, DH), DH ** -0.5),
        'ml_w_gate': nrm(ks[25], (N_ODD, 2, 3, E, 2 * H), 0.1 * (3 * E) ** -0.5),
        'ml_b_gate': ml_b_gate,
        'ml_norm_g': 1.0 + nrm(ks[26], (N_ODD, E), 0.05),
        'ml_skip': 1.0 + nrm(ks[27], (N_ODD, E), 0.05),
        'ml_w_out': nrm(ks[28], (N_ODD, E, D), DEEPNORM_BETA * E ** -0.5),
        'ffn_w_in': nrm(ks[29], (DEPTH, D, 2 * F), D ** -0.5),
        'ffn_w_out': nrm(ks[30], (DEPTH, F, D), DEEPNORM_BETA * F ** -0.5),
    }


def reference(x, c, ctx, c_ctx, mod_w, mod_b, ln_g, ln_b,
              ab_w_in, ab_pool_w, ab_pool_b, ab_pool_scale, ab_conv_w, ab_conv_b, ab_norm_g, ab_norm_b, ab_w_out,
              ml_w_in, ml_conv_w, ml_conv_b, ml_wq, ml_wk, ml_wv, ml_w_gate, ml_b_gate, ml_norm_g, ml_skip, ml_w_out,
              ffn_w_in, ffn_w_out):
    rows = x.shape[1] // GRID_W
    silu_c = jax.nn.silu(c)
    silu_cc = jax.nn.silu(c_ctx)
    h, hc = x, ctx
    for l in range(DEPTH):
        last = l == DEPTH - 1
        even = l % 2 == 0
        j = l // 2
        mods = (silu_c @ mod_w[l] + mod_b[l])[:, None, :]
        sh_m, sc_m, g_m, sh_f, sc_f, g_f = jnp.split(mods, 6, axis=-1)
        ctx_mixer_needed = (not last) or (not even)
        if ctx_mixer_needed:
            n_cols = 6 * D_MODEL if not last else 2 * D_MODEL
            cmods = jnp.split(silu_cc @ mod_w[l][:, :n_cols] + mod_b[l][:n_cols], n_cols // D_MODEL)
            uc = modulate(hc, cmods[0], cmods[1])
        u = modulate(h, sh_m, sc_m)
        if even:
            ab = (ab_w_in[j], ab_pool_w[j], ab_pool_b[j], ab_pool_scale[j], ab_conv_w[j], ab_conv_b[j],
                  ab_norm_g[j], ab_norm_b[j], ab_w_out[j])
            y = pool_conv_mixer(u, rows, *ab)
            y_c = pool_conv_mixer(uc, None, *ab) if not last else None
        else:
            y, y_c = mlstm_mixer(u, uc, ml_w_in[j], ml_conv_w[j], ml_conv_b[j], ml_wq[j], ml_wk[j], ml_wv[j],
                                 ml_w_gate[j], ml_b_gate[j], ml_norm_g[j], ml_skip[j], ml_w_out[j],
                                 ctx_out=not last)
        h = post_norm(h, g_m * y, ln_g[l, 0], ln_b[l, 0])
        h = post_norm(h, g_f * swiglu(modulate(h, sh_f, sc_f), ffn_w_in[l], ffn_w_out[l]), ln_g[l, 1], ln_b[l, 1])
        if not last:
            hc = post_norm(hc, cmods[2] * y_c, ln_g[l, 0], ln_b[l, 0])
            hc = post_norm(hc, cmods[5] * swiglu(modulate(hc, cmods[3], cmods[4]), ffn_w_in[l], ffn_w_out[l]),
                           ln_g[l, 1], ln_b[l, 1])
    return h
```

```python
import numpy as np
import concourse.bass as bass
import concourse.mybir as mybir
from concourse.bass_utils import run_bass_kernel_spmd
from contextlib import ExitStack

F32 = mybir.dt.float32
BF16 = mybir.dt.bfloat16
ALU = mybir.AluOpType
AF = mybir.ActivationFunctionType
AX = mybir.AxisListType

D = 1024
T = 4096
TC = 256
NT = T + TC
FH = 2816
E = 2048
NH = 4
DH = 512
CH = 64
NCH = NT // CH
ALPHA = (2.0 * 2) ** 0.25
EPS = 1e-5
POOL_W = (2, 4, 8, 16)


class Buf:
    __slots__ = ('name', 'w', 'r', 'dsem', 'dtot')

    def __init__(self, name):
        self.name = name
        self.w = None
        self.r = {}
        self.dsem = None
        self.dtot = 0


class K:
    def __init__(self, nc):
        self.nc = nc
        self.engs = {'pe': nc.tensor, 'act': nc.scalar, 'dve': nc.vector, 'pool': nc.gpsimd, 'sp': nc.sync}
        self.sem = {e: nc.alloc_semaphore("s_" + e) for e in ('pe', 'act', 'dve', 'pool')}
        self.cnt = {e: 0 for e in self.sem}
        self.known = {e: {} for e in self.engs}
        self.free_dsems = []
        self.live = []
        self.nsem = 0
        self.noreuse = set()

    def _need(self, eng, dep, waits):
        if dep is None:
            return
        if dep[0] == 'e':
            src, c = dep[1], dep[2]
            if src == eng and eng == 'pe':
                return
            key = ('e', src)
            sem = self.sem[src]
        else:
            sem, c = dep[1], dep[2]
            key = ('d', id(sem))
        if self.known[eng].get(key, 0) >= c:
            return
        if key in waits and waits[key][1] >= c:
            return
        waits[key] = (sem, c)

    def _deps(self, eng, reads, writes):
        waits = {}
        for b in reads:
            self._need(eng, b.w, waits)
        for b in writes:
            self._need(eng, b.w, waits)
            for v in b.r.values():
                self._need(eng, v, waits)
        e = self.engs[eng]
        for key, (sem, c) in waits.items():
            self.known[eng][key] = c
            e.wait_ge(sem, c)

    def op(self, eng, fns, reads=(), writes=()):
        if not isinstance(fns, (list, tuple)):
            fns = [fns]
        self._deps(eng, reads, writes)
        e = self.engs[eng]
        for f in fns[:-1]:
            f(e)
        self.cnt[eng] += 1
        c = self.cnt[eng]
        fns[-1](e).then_inc(self.sem[eng], 1)
        dep = ('e', eng, c)
        for b in writes:
            b.w = dep
            b.r = {}
        for b in reads:
            b.r[('e', eng)] = dep

    def _dsem(self, sb):
        if sb.dsem is None:
            if self.free_dsems:
                sb.dsem, sb.dtot = self.free_dsems.pop()
            else:
                sb.dsem = self.nc.alloc_semaphore("d%d" % self.nsem)
                self.nsem += 1
                sb.dtot = 0
            self.live.append(sb)

    def dma(self, qeng, out_ap, in_ap, sb, reads=(), writes=(), **kw):
        if qeng == 'pool':
            assert sb.dsem is None
            sb.dsem = self.nc.alloc_semaphore("w%d" % self.nsem)
            self.nsem += 1
            sb.dtot = 0
            self.live.append(sb)
            self.noreuse.add(id(sb))
        self._dsem(sb)
        e = self.engs[qeng]
        if sb.dtot > 0:
            key = ('d', id(sb.dsem))
            if self.known[qeng].get(key, 0) < sb.dtot:
                self.known[qeng][key] = sb.dtot
                e.wait_ge(sb.dsem, sb.dtot)
        self._deps(qeng, reads, writes)
        sb.dtot += 16
        e.dma_start(out=out_ap, in_=in_ap, **kw).then_inc(sb.dsem, 16)
        dep = ('d', sb.dsem, sb.dtot)
        for b in writes:
            b.w = dep
            b.r = {}
        for b in reads:
            b.r[('d', id(sb.dsem))] = dep

    def barrier(self, release=()):
        for eng, e in self.engs.items():
            for src, sem in self.sem.items():
                c = self.cnt[src]
                key = ('e', src)
                if c > 0 and self.known[eng].get(key, 0) < c:
                    self.known[eng][key] = c
                    e.wait_ge(sem, c)
            for b in self.live:
                key = ('d', id(b.dsem))
                if b.dtot > 0 and self.known[eng].get(key, 0) < b.dtot:
                    self.known[eng][key] = b.dtot
                    e.wait_ge(b.dsem, b.dtot)
        for b in release:
            if b.dsem is not None:
                self.live.remove(b)
                if id(b) not in self.noreuse:
                    self.free_dsems.append((b.dsem, b.dtot))
                b.dsem = None


class Ctx:
    def __init__(self, nc):
        self.nc = nc
        self.k = K(nc)
        self.uid = 0

    def tile(self, es, shape, dt, name=None, bufs=None):
        self.uid += 1
        nm = "%s_%d" % (name or "t", self.uid)
        t = es.enter_context(self.nc.sbuf_tensor(nm, list(shape), dt))
        b = Buf(nm)
        if bufs is not None:
            bufs.append(b)
        return t, b

    def psum(self, es, shape, dt, name=None):
        self.uid += 1
        nm = "%s_%d" % (name or "ps", self.uid)
        t = es.enter_context(self.nc.psum_tensor(nm, list(shape), dt))
        return t, Buf(nm)


def build(dbg=None):
    dbg = dbg or {}
    nc = bass.Bass("TRN2", target_bir_lowering=False)
    cx = Ctx(nc)
    k = cx.k

    def din(name, shape, dt=F32):
        return nc.dram_tensor(name, list(shape), dt, kind="ExternalInput").ap()

    def dscr(name, shape, dt=F32):
        kind = "ExternalOutput" if name in dbg.get('outs', ()) else "Internal"
        return nc.dram_tensor(name, list(shape), dt, kind=kind).ap()

    x_d = din("x", [T, D])
    ctx_d = din("ctx", [TC, D])
    cvec_d = din("cvec", [128, 8, 2])
    modw_d = din("mod_w", [2, D, 6 * D])
    modbc_d = din("modb_col", [128, 2, 48, 2])
    modbb_d = din("modb_bc", [128, 2, 6 * D])
    lng_d = din("lng_bc", [128, 4, D])
    lnb_d = din("lnb_bc", [128, 4, D])
    ident_d = din("ident", [128, 128])
    ffn_win_d = din("ffn_w_in", [2, D, 2 * FH])
    ffn_wout_d = din("ffn_w_out", [2, FH, D])
    out_d = nc.dram_tensor("out", [T, D], F32, kind="ExternalOutput").ap()
    ab_win_d = din("ab_w_in", [D, 1536])
    ab_wout_d = din("ab_w_out", [D, D])
    ab_pw_d = din("ab_pool_w", [4, 128, 128])
    poolm_d = din("poolm", [128, 31, 512])
    poolc_d = din("poolc", [128, 8, 256])
    invc_d = din("invc", [128, 4, T])
    invcc_d = din("invcc", [128, 4, TC])
    abv_d = din("abvec", [128, 4, 36])

    ml_win_d = din("ml_w_in", [D, 2 * E])
    ml_wq_d = din("ml_wq", [NH, DH, DH])
    ml_wk_d = din("ml_wk", [NH, DH, DH])
    ml_wv_d = din("ml_wv", [NH, DH, DH])
    ml_wout_d = din("ml_w_out", [E, D])
    wg_d = din("wg", [128, 3, 16, 16])
    bg_d = din("bg_bc", [128, 16])
    mlv_d = din("mlvec", [128, 16, 6])
    masks_d = din("masks", [128, 2, 128])
    NTP = NT + 4
    XMT_d = dscr("XMT", [128, 16, NTP])
    P1_d = dscr("P1", [128, 16, T])
    P2_d = dscr("P2", [128, 16, T])
    QT_d = dscr("QT", [NT // 128, 128, 16 * 128], BF16)
    KT_d = dscr("KT", [NT // 128, 128, 16 * 128], BF16)
    Ktm_d = dscr("Ktm", [NT, E], BF16)
    Vtm_d = dscr("Vtm", [NT, E])
    G_d = dscr("G", [NT, 16])
    Gb_d = dscr("Gb", [NT, 16])
    HF_d = dscr("HF", [T, E])
    HB_d = dscr("HB", [T, E])
    H3_d = dscr("H3", [T, D])
    gbc_d = dscr("gbc", [6, 128, D])
    H1_d = din("H1", [NT, D]) if dbg.get('h1_in') else dscr("H1", [NT, D])
    H2_d = din("H2", [NT, D]) if dbg.get('h2_in') else dscr("H2", [NT, D])

    dram_bufs = {}

    def DB(name):
        if name not in dram_bufs:
            dram_bufs[name] = Buf(name)
        return dram_bufs[name]

    with ExitStack() as top:
        identF, B_identF = cx.tile(top, [128, 128], F32, "identF")
        identB, B_identB = cx.tile(top, [128, 128], BF16, "identB")
        modT, B_modT = cx.tile(top, [128, 2, 48, 2], F32, "modT")
        epsT, B_eps = cx.tile(top, [128, 1], F32, "eps")
        k.dma('sp', identF[:], ident_d[:, :], B_identF, writes=[B_identF])
        k.op('dve', lambda e: e.tensor_copy(out=identB[:], in_=identF[:]), reads=[B_identF], writes=[B_identB])
        k.op('pool', lambda e: e.memset(epsT[:], EPS), writes=[B_eps])

        def phase_mods():
            with ExitStack() as es:
                loc = []
                cv, B_cv = cx.tile(es, [128, 8, 2], F32, "cv", loc)
                scv, B_scv = cx.tile(es, [128, 8, 2], F32, "scv", loc)
                scb = [cx.tile(es, [128, 8, 128], F32, "scb", loc) for _ in range(2)]
                mbc, B_mbc = cx.tile(es, [128, 2, 48, 2], F32, "mbc", loc)
                wt = [cx.tile(es, [128, 8, 512], F32, "wt", loc) for _ in range(4)]
                mbb = [cx.tile(es, [128, 512], F32, "mbb", loc) for _ in range(2)]
                gts = {}
                pcs = [cx.psum(es, [128, 512], F32, "pc") for _ in range(2)]
                pgs = [cx.psum(es, [128, 512], F32, "pg") for _ in range(2)]
                k.dma('sp', cv[:], cvec_d[:, :, :], B_cv, writes=[B_cv])
                k.dma('sp', mbc[:], modbc_d[:, :, :, :], B_mbc, writes=[B_mbc])
                k.op('act', lambda e: e.activation(out=scv[:], in_=cv[:], func=AF.Silu), reads=[B_cv], writes=[B_scv])
                for r in range(2):
                    for kc in range(8):
                        k.op('dve', lambda e: e.tensor_copy(out=scb[r][0][:, kc, :],
                                                           in_=scv[:, kc, r:r + 1].to_broadcast([128, 128])),
                             reads=[B_scv], writes=[scb[r][1]])
                it = 0
                ig = 0

                def mods_load(q):
                    lq, jq = q // 12, q % 12
                    wvq = modw_d[lq].rearrange("(kc p) n -> p kc n", p=128)
                    k.dma('sp', wt[q % 4][0][:], wvq[:, :, jq * 512:(jq + 1) * 512], wt[q % 4][1], writes=[wt[q % 4][1]])

                for q in range(3):
                    mods_load(q)
                for l in range(2):
                    for jb in range(12):
                        w, B_w = wt[it % 4]
                        pc, B_pc = pcs[it % 2]
                        if it + 3 < 24:
                            mods_load(it + 3)
                        it += 1
                        fns = []
                        for jj in range(4):
                            for kc in range(8):
                                fns.append(lambda e, jj=jj, kc=kc: e.matmul(
                                    pc[:, jj * 2:(jj + 1) * 2], lhsT=w[:, kc, jj * 128:(jj + 1) * 128],
                                    rhs=scv[:, kc, 0:2], start=(kc == 0), stop=(kc == 7)))
                        k.op('pe', fns, reads=[B_w, B_scv], writes=[B_pc])
                        k.op('dve', lambda e: e.tensor_tensor(
                            out=modT[:, l, jb * 4:(jb + 1) * 4, :],
                            in0=pc[:, 0:8].rearrange("p (a b) -> p a b", b=2),
                            in1=mbc[:, l, jb * 4:(jb + 1) * 4, :], op=ALU.add),
                            reads=[B_mbc], writes=[B_pc, B_modT])
                        ch = jb // 2
                        if ch in (2, 5):
                            for r in range(2):
                                if r == 1 and l == 1:
                                    continue
                                key = (l, ch, r)
                                if key not in gts:
                                    gts[key] = cx.tile(es, [128, D], F32, "gt", loc)
                                gt, B_gt = gts[key]
                                pg, B_pg = pgs[ig % 2]
                                mb, B_mb = mbb[ig % 2]
                                ig += 1
                                k.dma('sp', mb[:], modbb_d[:, l, jb * 512:(jb + 1) * 512], B_mb, writes=[B_mb])
                                fns = [lambda e, kc=kc: e.matmul(pg[:], lhsT=scb[r][0][:, kc, :], rhs=w[:, kc, :],
                                                                 start=(kc == 0), stop=(kc == 7)) for kc in range(8)]
                                k.op('pe', fns, reads=[B_w, scb[r][1]], writes=[B_pg])
                                hf = jb % 2
                                k.op('dve', lambda e: e.tensor_tensor(out=gt[:, hf * 512:(hf + 1) * 512], in0=pg[:],
                                                                     in1=mb[:], op=ALU.add),
                                     reads=[B_mb], writes=[B_pg, B_gt])
                                if hf == 1:
                                    idx = {(0, 2, 0): 0, (0, 5, 0): 1, (0, 2, 1): 2, (0, 5, 1): 3,
                                           (1, 2, 0): 4, (1, 5, 0): 5}[key]
                                    k.dma('sp', gbc_d[idx], gt[:], B_gt, reads=[B_gt], writes=[DB("gbc%d" % idx)])
                    for c0 in (8, 32):
                        k.op('dve', lambda e: e.tensor_scalar(out=modT[:, l, c0:c0 + 8, :], in0=modT[:, l, c0:c0 + 8, :],
                                                             scalar1=1.0, scalar2=None, op0=ALU.add),
                             reads=[B_modT], writes=[B_modT])
                k.barrier(release=loc)

        def post_norm_g(psA, B_psA, psB, B_psB, res, B_res, gate, B_gate, lng, B_lng, lnb, B_lnb,
                        t1, B_t1, sm, B_sm, out_ap, out_buf):
            k.op('dve', lambda e: e.tensor_tensor(out=t1[:, 0:512], in0=psA, in1=gate[:, 0:512], op=ALU.mult),
                 reads=[B_gate], writes=[B_psA, B_t1])
            k.op('dve', lambda e: e.tensor_tensor(out=t1[:, 512:1024], in0=psB, in1=gate[:, 512:1024], op=ALU.mult),
                 reads=[B_gate], writes=[B_psB, B_t1])
            yield
            k.op('dve', lambda e: e.scalar_tensor_tensor(out=t1[:], in0=res, scalar=ALPHA, in1=t1[:],
                                                        op0=ALU.mult, op1=ALU.add),
                 reads=[B_res], writes=[B_t1])
            yield
            k.op('dve', [lambda e: e.bn_stats(out=sm[:, 0:6], in_=t1[:, 0:512]),
                         lambda e: e.bn_stats(out=sm[:, 6:12], in_=t1[:, 512:1024])],
                 reads=[B_t1], writes=[B_sm])
            yield
            k.op('dve', lambda e: e.bn_aggr(out=sm[:, 12:14], in_=sm[:, 0:12]), reads=[B_sm], writes=[B_sm])
            yield
            k.op('act', lambda e: e.activation(out=sm[:, 14:15], in_=sm[:, 13:14], func=AF.Sqrt, bias=epsT[:, 0:1],
                                               scale=1.0), reads=[B_sm, B_eps], writes=[B_sm])
            yield
            k.op('dve', lambda e: e.reciprocal(out=sm[:, 14:15], in_=sm[:, 14:15]), reads=[B_sm], writes=[B_sm])
            yield
            k.op('dve', lambda e: e.scalar_tensor_tensor(out=sm[:, 15:16], in0=sm[:, 12:13], scalar=-1.0,
                                                        in1=sm[:, 14:15], op0=ALU.mult, op1=ALU.mult),
                 reads=[B_sm], writes=[B_sm])
            yield
            k.op('act', lambda e: e.activation(out=t1[:], in_=t1[:], func=AF.Identity, scale=sm[:, 14:15],
                                               bias=sm[:, 15:16]), reads=[B_sm, B_t1], writes=[B_t1])
            yield
            k.op('dve', lambda e: e.tensor_tensor(out=t1[:], in0=t1[:], in1=lng, op=ALU.mult),
                 reads=[B_lng], writes=[B_t1])
            yield
            k.op('dve', lambda e: e.tensor_tensor(out=t1[:], in0=t1[:], in1=lnb, op=ALU.add),
                 reads=[B_lnb], writes=[B_t1])
            yield
            k.dma('sp', out_ap, t1[:], B_t1, reads=[B_t1], writes=[out_buf])

        def post_norm(*args):
            for _ in post_norm_g(*args):
                pass

        def phase_ffn(l, Hin, Hin_name, Hout, Hout_name, groups):
            NTK = 256
            with ExitStack() as es:
                loc = []
                fwin = [cx.tile(es, [128, 8, 1408], BF16, "fwin", loc) for _ in range(4)]
                fwout = [cx.tile(es, [128, 11, D], BF16, "fwout", loc) for _ in range(2)]
                wi = ffn_win_d[l].rearrange("(kc p) n -> p kc n", p=128)
                wo = ffn_wout_d[l].rearrange("(f p) n -> p f n", p=128)
                for pi in (0, 2, 1, 3):
                    k.dma('pool', fwin[pi][0][:], wi[:, :, pi * 1408:(pi + 1) * 1408], fwin[pi][1],
                          writes=[fwin[pi][1]])
                for pi in range(2):
                    k.dma('pool', fwout[pi][0][:], wo[:, pi * 11:(pi + 1) * 11, :], fwout[pi][1],
                          writes=[fwout[pi][1]])
                gates = {}
                for r in sorted(set(g[2] for g in groups)):
                    gt, B_gt = cx.tile(es, [128, D], F32, "gf", loc)
                    idx = {(0, 0): 1, (0, 1): 3, (1, 0): 5}[(l, r)]
                    k.dma('sp', gt[:], gbc_d[idx], B_gt, reads=[DB("gbc%d" % idx)], writes=[B_gt])
                    gates[r] = (gt, B_gt)
                lng, B_lng = cx.tile(es, [128, D], F32, "lng", loc)
                lnb, B_lnb = cx.tile(es, [128, D], F32, "lnb", loc)
                k.dma('sp', lng[:], lng_d[:, l * 2 + 1, :], B_lng, writes=[B_lng])
                k.dma('sp', lnb[:], lnb_d[:, l * 2 + 1, :], B_lnb, writes=[B_lnb])
                xts = [cx.tile(es, [128, D], F32, "xt", loc) for _ in range(4)]
                t1s = [cx.tile(es, [128, D], F32, "t1", loc) for _ in range(2)]
                sms = [cx.tile(es, [128, 16], F32, "sm", loc) for _ in range(2)]
                uTs = [cx.tile(es, [128, 8, NTK], BF16, "uT", loc) for _ in range(2)]
                actT, B_actT = cx.tile(es, [128, 22, NTK], BF16, "actT", loc)
                sAs = [cx.tile(es, [128, NTK], F32, "sA", loc) for _ in range(4)]
                ptA = cx.psum(es, [128, 512], F32, "ptA")
                ptB = cx.psum(es, [128, 512], F32, "ptB")
                pabs = [cx.psum(es, [128, 512], F32, "pab") for _ in range(4)]
                pos = [cx.psum(es, [128, 512], F32, "po") for _ in range(2)]
                ix = 0
                ip = 0
                def ffn_loads(gq):
                    rin_ = groups[gq][0]
                    for s in range(2):
                        xt, B_xt = xts[(2 * gq + s) % 4]
                        k.dma('sp', xt[:], Hin[rin_ + s * 128: rin_ + (s + 1) * 128, :], B_xt,
                              reads=[DB(Hin_name)], writes=[B_xt])

                cnf = dict(ip=0)

                def ffn_group(gi):
                    rin, rout, r = groups[gi]
                    uT, B_uT = uTs[gi % 2]
                    ffn_loads(gi)
                    yield
                    xs = []
                    for s in range(2):
                        xt, B_xt = xts[(2 * gi + s) % 4]
                        xs.append((xt, B_xt))
                        for (pt, B_pt), k0 in ((ptA, 0), (ptB, 4)):
                            k.op('pe', [lambda e, kk=kk: e.transpose(out=pt[:, kk * 128:(kk + 1) * 128],
                                                                    in_=xt[:, (k0 + kk) * 128:(k0 + kk + 1) * 128],
                                                                    identity=identF[:]) for kk in range(4)],
                                 reads=[B_xt, B_identF], writes=[B_pt])
                            yield
                            for kk in range(4):
                                kc = k0 + kk
                                k.op('act', lambda e: e.activation(
                                    out=uT[:, kc, s * 128:(s + 1) * 128], in_=pt[:, kk * 128:(kk + 1) * 128],
                                    func=AF.Identity, scale=modT[:, l, 32 + kc, r:r + 1],
                                    bias=modT[:, l, 24 + kc, r:r + 1]),
                                    reads=[B_modT], writes=[B_pt, B_uT])
                                yield
                    while gi > 0 and not cnf.get(('out', gi - 1)):
                        yield
                    for f in range(22):
                        ip = cnf['ip']
                        pab, B_pa = pabs[ip % 4]
                        B_pg = B_pa
                        pa = pab[:, 0:256]
                        pg = pab[:, 256:512]
                        sA, B_sA = sAs[ip % 4]
                        cnf['ip'] += 1
                        pi = f // 11
                        c0 = (f % 11) * 128
                        wa, B_wa = fwin[pi]
                        wg, B_wg = fwin[2 + pi]
                        k.op('pe', [lambda e, kc=kc: e.matmul(pa[:, 0:NTK], lhsT=wa[:, kc, c0:c0 + 128],
                                                              rhs=uT[:, kc, :], start=(kc == 0), stop=(kc == 7))
                                    for kc in range(8)] +
                                   [lambda e, kc=kc: e.matmul(pg[:, 0:NTK], lhsT=wg[:, kc, c0:c0 + 128],
                                                              rhs=uT[:, kc, :], start=(kc == 0), stop=(kc == 7))
                                    for kc in range(8)], reads=[B_wa, B_wg, B_uT], writes=[B_pa])
                        yield
                        k.op('act', lambda e: e.activation(out=sA[:], in_=pa[:, 0:NTK], func=AF.Silu),
                             writes=[B_pa, B_sA])
                        yield
                        k.op('dve', lambda e: e.tensor_tensor(out=actT[:, f, :], in0=pg[:, 0:NTK], in1=sA[:],
                                                             op=ALU.mult), reads=[B_sA], writes=[B_pg, B_actT])
                        yield
                    for s in range(2):
                        for hf in range(2):
                            po, B_po = pos[hf]
                            k.op('pe', [lambda e, f=f: e.matmul(po[:], lhsT=actT[:, f, s * 128:(s + 1) * 128],
                                                                rhs=fwout[f // 11][0][:, f % 11, hf * 512:(hf + 1) * 512],
                                                                start=(f == 0), stop=(f == 21)) for f in range(22)],
                                 reads=[B_actT, fwout[0][1], fwout[1][1]], writes=[B_po])
                            yield
                        if s == 1:
                            cnf[('out', gi)] = True
                        t1, B_t1 = t1s[s]
                        sm, B_sm = sms[s]
                        gt, B_gt = gates[r]
                        yield from post_norm_g(pos[0][0][:], pos[0][1], pos[1][0][:], pos[1][1], xs[s][0][:], xs[s][1],
                                               gt, B_gt, lng[:], B_lng, lnb[:], B_lnb, t1, B_t1, sm, B_sm,
                                               Hout[rout + s * 128: rout + (s + 1) * 128, :], DB(Hout_name))

                run_pipelined(ffn_group, len(groups), depth=2, skew=70)
                k.barrier(release=loc)

        def xrows(n0, nrows):
            if n0 < TC:
                return ctx_d[n0:n0 + nrows, :]
            return x_d[n0 - TC:n0 - TC + nrows, :]

        def make_uT_g(l, r, jsh, jsc, src_ap, B_src_dram, xt, B_xt, ptA, ptB, uT, B_uT, s):
            if src_ap is not None:
                k.dma('sp', xt[:], src_ap, B_xt, reads=B_src_dram, writes=[B_xt])
            for (pt, B_pt), k0 in ((ptA, 0), (ptB, 4)):
                k.op('pe', [lambda e, kk=kk: e.transpose(out=pt[:, kk * 128:(kk + 1) * 128],
                                                        in_=xt[:, (k0 + kk) * 128:(k0 + kk + 1) * 128],
                                                        identity=identF[:]) for kk in range(4)],
                     reads=[B_xt, B_identF], writes=[B_pt])
                yield
                for kk in range(4):
                    kc = k0 + kk
                    k.op('act', lambda e: e.activation(
                        out=uT[:, kc, s * 128:(s + 1) * 128], in_=pt[:, kk * 128:(kk + 1) * 128],
                        func=AF.Identity, scale=modT[:, l, jsc + kc, r:r + 1], bias=modT[:, l, jsh + kc, r:r + 1]),
                        reads=[B_modT], writes=[B_pt, B_uT])
                    yield

        def make_uT(*args):
            for _ in make_uT_g(*args):
                pass

        L0_GROUPS = [(0, 2, 1)] + [(TC + g * 512, 4, 0) for g in range(8)]
        POOL_D = {0: list(range(-1, 4)), 1: list(range(-1, 5)), 2: list(range(-2, 6)), 3: list(range(-4, 8))}
        POOL_IDX = {}
        for gi in range(4):
            for dl in POOL_D[gi]:
                POOL_IDX[(gi, dl)] = len(POOL_IDX)

        def layer0_mixer(groups):
            with ExitStack() as l0:
                keep = []
                A_tm, B_A = cx.tile(l0, [128, NT // 128, 512], BF16, "A_tm", keep)
                YB, B_YB = cx.tile(l0, [128, 4, NT], BF16, "YB", keep)
                abv, B_abv = cx.tile(l0, [128, 4, 36], F32, "abv", keep)
                k.dma('sp', abv[:], abv_d[:, :, :], B_abv, writes=[B_abv])
                lg = ExitStack()
                keepg = []
                with lg:
                    GLU, B_GLU = cx.tile(lg, [128, 4, T + 30], BF16, "GLU", keepg)
                    GLUc, B_GLUc = cx.tile(lg, [128, 4, TC + 30], BF16, "GLUc", keepg)
                    k.op('pool', lambda e: e.memset(GLU[:], 0.0), writes=[B_GLU])
                    k.op('pool', lambda e: e.memset(GLUc[:], 0.0), writes=[B_GLUc])
                    with ExitStack() as es:
                        loc = []
                        win, B_win = cx.tile(es, [128, 8, 1536], BF16, "win", loc)
                        k.dma('pool', win[:], ab_win_d.rearrange("(kc p) n -> p kc n", p=128), B_win, writes=[B_win])
                        xts = [cx.tile(es, [128, D], F32, "xt", loc) for _ in range(8)]
                        uTs = [cx.tile(es, [128, 8, 512], BF16, "uT", loc) for _ in range(2)]
                        sgs = [cx.tile(es, [128, 512], F32, "sg", loc) for _ in range(2)]

                        def a0_loads(gq):
                            nq, nsq, _ = groups[gq]
                            for sq in range(nsq):
                                k.dma('sp', xts[(gq % 2) * 4 + sq][0][:], xrows(nq + sq * 128, 128), xts[(gq % 2) * 4 + sq][1],
                                      writes=[xts[(gq % 2) * 4 + sq][1]])

                        ptA = cx.psum(es, [128, 512], F32, "ptA")
                        ptB = cx.psum(es, [128, 512], F32, "ptB")
                        pas = [cx.psum(es, [128, 512], F32, "pa") for _ in range(2)]
                        pvs = [cx.psum(es, [128, 512], F32, "pv") for _ in range(2)]
                        pgs = [cx.psum(es, [128, 512], F32, "pg") for _ in range(2)]
                        cna = dict(ia=0, iv=0)

                        def a0_group(gidx):
                            n0, nsub, r = groups[gidx]
                            ntok = nsub * 128
                            uT, B_uT = uTs[gidx % 2]
                            a0_loads(gidx)
                            yield
                            yield from acquire(cna, 'st1')
                            for s in range(nsub):
                                xt, B_xt = xts[(gidx % 2) * 4 + s]
                                yield from make_uT_g(0, r, 0, 8, None, [], xt, B_xt, ptA, ptB, uT, B_uT, s)
                            cna['st1'] = False
                            yield from acquire(cna, 'st2')
                            for s in range(nsub):
                                pa, B_pa = pas[cna['ia'] % 2]
                                cna['ia'] += 1
                                k.op('pe', [lambda e, kc=kc: e.matmul(pa[:], lhsT=uT[:, kc, s * 128:(s + 1) * 128],
                                                                      rhs=win[:, kc, 0:512], start=(kc == 0), stop=(kc == 7))
                                            for kc in range(8)], reads=[B_uT, B_win], writes=[B_pa])
                                yield
                                ti = n0 // 128 + s
                                k.op('act', lambda e: e.copy(out=A_tm[:, ti, :], in_=pa[:]), writes=[B_pa, B_A])
                                yield
                            Gt, B_Gt, goff = (GLUc, B_GLUc, 15 + n0) if r == 1 else (GLU, B_GLU, 15 + n0 - TC)
                            for jc in range(4):
                                iv = cna['iv']
                                pv, B_pv = pvs[iv % 2]
                                pg, B_pg = pgs[iv % 2]
                                sg, B_sg = sgs[iv % 2]
                                cna['iv'] += 1
                                k.op('pe', [lambda e, kc=kc: e.matmul(pv[:, 0:ntok], lhsT=win[:, kc, 512 + jc * 128:512 + (jc + 1) * 128],
                                                                      rhs=uT[:, kc, 0:ntok], start=(kc == 0), stop=(kc == 7))
                                            for kc in range(8)], reads=[B_uT, B_win], writes=[B_pv])
                                k.op('pe', [lambda e, kc=kc: e.matmul(pg[:, 0:ntok], lhsT=win[:, kc, 1024 + jc * 128:1024 + (jc + 1) * 128],
                                                                      rhs=uT[:, kc, 0:ntok], start=(kc == 0), stop=(kc == 7))
                                            for kc in range(8)], reads=[B_uT, B_win], writes=[B_pg])
                                yield
                                k.op('act', lambda e: e.activation(out=sg[:, 0:ntok], in_=pg[:, 0:ntok], func=AF.Sigmoid),
                                     writes=[B_pg, B_sg])
                                yield
                                k.op('dve', lambda e: e.tensor_tensor(out=Gt[:, jc, goff:goff + ntok], in0=pv[:, 0:ntok],
                                                                     in1=sg[:, 0:ntok], op=ALU.mult),
                                     reads=[B_sg], writes=[B_pv, B_Gt])
                                yield
                            cna['st2'] = False

                        run_pipelined(a0_group, len(groups), depth=2, skew=1)
                        k.barrier(release=loc)
                    with ExitStack() as es:
                        loc = []
                        DG, B_DG = cx.tile(es, [128, 4 * 31, 128], BF16, "DG", loc)
                        for jc in range(4):
                            k.op('dve', lambda e: e.tensor_tensor(
                                out=DG[:, jc * 31:(jc + 1) * 31, :], in0=identF[:].unsqueeze(1).to_broadcast([128, 31, 128]),
                                in1=abv[:, jc, 0:31].unsqueeze(2).to_broadcast([128, 31, 128]),
                                op=ALU.mult), reads=[B_identF, B_abv], writes=[B_DG])
                        onesF, B_ones = cx.tile(es, [128, 128], F32, "ones", loc)
                        k.op('pool', lambda e: e.memset(onesF[:], 1.0), writes=[B_ones])
                        CVs = [cx.tile(es, [128, 4, 512], F32, "CV", loc) for _ in range(2)]
                        SQs = [cx.tile(es, [128, 4, 512], F32, "SQ", loc) for _ in range(2)]
                        MEs = [cx.tile(es, [128, 512], F32, "ME", loc) for _ in range(2)]
                        M2s = [cx.tile(es, [128, 512], F32, "M2", loc) for _ in range(2)]
                        VARs = [cx.tile(es, [128, 512], F32, "VAR", loc) for _ in range(2)]
                        xcs = [cx.tile(es, [128, 512], F32, "xc", loc) for _ in range(2)]
                        pcvs = [cx.psum(es, [128, 512], F32, "pcv") for _ in range(2)]
                        pS = cx.psum(es, [128, 512], F32, "pS")
                        pSS = cx.psum(es, [128, 512], F32, "pSS")
                        cnb = dict(ic=0)

                        def b0a_group(gidx):
                            n0, nsub, r = groups[gidx]
                            ntok = nsub * 128
                            CV, B_CV = CVs[gidx % 2]
                            SQ, B_SQ = SQs[gidx % 2]
                            ME, B_ME = MEs[gidx % 2]
                            M2, B_M2 = M2s[gidx % 2]
                            VAR, B_VAR = VARs[gidx % 2]
                            Gt, B_Gt, goff = (GLUc, B_GLUc, n0) if r == 1 else (GLU, B_GLU, n0 - TC)
                            yield from acquire(cnb, 'st1')
                            for jc in range(4):
                                pcv, B_pcv = pcvs[cnb['ic'] % 2]
                                cnb['ic'] += 1
                                k.op('pe', [lambda e, kk=kk: e.matmul(pcv[:, 0:ntok], lhsT=DG[:, jc * 31 + kk, :],
                                                                      rhs=Gt[:, jc, goff + kk:goff + kk + ntok],
                                                                      start=(kk == 0), stop=(kk == 30)) for kk in range(31)],
                                     reads=[B_DG, B_Gt], writes=[B_pcv])
                                yield
                                k.op('act', lambda e: e.activation(out=CV[:, jc, 0:ntok], in_=pcv[:, 0:ntok], func=AF.Identity,
                                                                   bias=abv[:, jc, 31:32], scale=1.0),
                                     reads=[B_abv], writes=[B_pcv, B_CV])
                                yield
                                k.op('dve', lambda e: e.tensor_tensor(out=SQ[:, jc, 0:ntok], in0=CV[:, jc, 0:ntok],
                                                                      in1=CV[:, jc, 0:ntok], op=ALU.mult),
                                     reads=[B_CV], writes=[B_SQ])
                                yield
                            cnb['st1'] = False
                            yield from acquire(cnb, 'st2')
                            k.op('pe', [lambda e, jc=jc: e.matmul(pS[0][:, 0:ntok], lhsT=onesF[:], rhs=CV[:, jc, 0:ntok],
                                                                  start=(jc == 0), stop=(jc == 3)) for jc in range(4)],
                                 reads=[B_ones, B_CV], writes=[pS[1]])
                            k.op('pe', [lambda e, jc=jc: e.matmul(pSS[0][:, 0:ntok], lhsT=onesF[:], rhs=SQ[:, jc, 0:ntok],
                                                                  start=(jc == 0), stop=(jc == 3)) for jc in range(4)],
                                 reads=[B_ones, B_SQ], writes=[pSS[1]])
                            yield
                            k.op('dve', lambda e: e.tensor_scalar(out=ME[:, 0:ntok], in0=pS[0][:, 0:ntok], scalar1=1.0 / 512,
                                                                 scalar2=None, op0=ALU.mult), writes=[pS[1], B_ME])
                            yield
                            k.op('dve', lambda e: e.tensor_tensor(out=M2[:, 0:ntok], in0=ME[:, 0:ntok], in1=ME[:, 0:ntok],
                                                                  op=ALU.mult), reads=[B_ME], writes=[B_M2])
                            yield
                            k.op('dve', lambda e: e.scalar_tensor_tensor(out=VAR[:, 0:ntok], in0=pSS[0][:, 0:ntok],
                                                                        scalar=1.0 / 512, in1=M2[:, 0:ntok],
                                                                        op0=ALU.mult, op1=ALU.subtract),
                                 reads=[B_M2], writes=[pSS[1], B_VAR])
                            yield
                            k.op('act', lambda e: e.activation(out=VAR[:, 0:ntok], in_=VAR[:, 0:ntok], func=AF.Sqrt,
                                                               bias=epsT[:, 0:1], scale=1.0), reads=[B_eps], writes=[B_VAR])
                            yield
                            k.op('dve', lambda e: e.reciprocal(out=VAR[:, 0:ntok], in_=VAR[:, 0:ntok]), writes=[B_VAR])
                            yield
                            for jc in range(4):
                                xc, B_xc = xcs[jc % 2]
                                k.op('dve', lambda e: e.tensor_tensor(out=xc[:, 0:ntok], in0=CV[:, jc, 0:ntok], in1=ME[:, 0:ntok],
                                                                     op=ALU.subtract), reads=[B_CV, B_ME], writes=[B_xc])
                                yield
                                k.op('dve', lambda e: e.tensor_tensor(out=xc[:, 0:ntok], in0=xc[:, 0:ntok], in1=VAR[:, 0:ntok],
                                                                      op=ALU.mult), reads=[B_VAR], writes=[B_xc])
                                yield
                                k.op('act', lambda e: e.activation(out=YB[:, jc, n0:n0 + ntok], in_=xc[:, 0:ntok], func=AF.Silu,
                                                                   scale=abv[:, jc, 32:33], bias=abv[:, jc, 33:34]),
                                     reads=[B_xc, B_abv], writes=[B_YB])
                                yield
                            cnb['st2'] = False

                        run_pipelined(b0a_group, len(groups), depth=2, skew=1)
                        k.barrier(release=loc)
                    k.barrier(release=keepg)
                with ExitStack() as es:
                    loc = []
                    poolm, B_poolm = cx.tile(es, [128, 31, 512], BF16, "poolm", loc)
                    poolc, B_poolc = cx.tile(es, [128, 8, 256], BF16, "poolc", loc)
                    pw, B_pw = cx.tile(es, [128, 4, 128], BF16, "pw", loc)
                    wout, B_wout = cx.tile(es, [128, 8, D], BF16, "wout", loc)
                    k.dma('pool', poolm[:], poolm_d[:, :, :], B_poolm, writes=[B_poolm])
                    k.dma('pool', poolc[:], poolc_d[:, :, :], B_poolc, writes=[B_poolc])
                    k.dma('pool', pw[:], ab_pw_d.rearrange("g c d -> c g d"), B_pw, writes=[B_pw])
                    k.dma('pool', wout[:], ab_wout_d.rearrange("(kc p) n -> p kc n", p=128), B_wout, writes=[B_wout])
                    gates = {}
                    for r in sorted(set(g[2] for g in groups)):
                        gt, B_gt = cx.tile(es, [128, D], F32, "gm", loc)
                        idx = {0: 0, 1: 2}[r]
                        k.dma('sp', gt[:], gbc_d[idx], B_gt, reads=[DB("gbc%d" % idx)], writes=[B_gt])
                        gates[r] = (gt, B_gt)
                    lng, B_lng = cx.tile(es, [128, D], F32, "lng", loc)
                    lnb, B_lnb = cx.tile(es, [128, D], F32, "lnb", loc)
                    k.dma('sp', lng[:], lng_d[:, 0, :], B_lng, writes=[B_lng])
                    k.dma('sp', lnb[:], lnb_d[:, 0, :], B_lnb, writes=[B_lnb])
                    invs = [cx.tile(es, [128, 4, 512], F32, "inv", loc) for _ in range(2)]
                    YAs = [cx.tile(es, [128, 4, 512], BF16, "YA", loc) for _ in range(2)]
                    dTs = [cx.tile(es, [128, 4, 512], BF16, "dT", loc) for _ in range(2)]
                    tqs = [cx.tile(es, [128, 512], F32, "tq", loc) for _ in range(2)]
                    sgs = [cx.tile(es, [128, 512], F32, "sg", loc) for _ in range(2)]
                    xts = [cx.tile(es, [128, D], F32, "xt", loc) for _ in range(2)]
                    t1s = [cx.tile(es, [128, D], F32, "t1", loc) for _ in range(2)]
                    sms = [cx.tile(es, [128, 16], F32, "sm", loc) for _ in range(2)]
                    pbs = [cx.psum(es, [128, 512], F32, "pb") for _ in range(2)]
                    psgs = [cx.psum(es, [128, 512], F32, "psg") for _ in range(2)]
                    pws = [cx.psum(es, [128, 512], F32, "pw") for _ in range(2)]
                    pos = [cx.psum(es, [128, 512], F32, "po") for _ in range(2)]
                    ib = 0
                    ix = 0
                    def b0b_loads(gq):
                        nq, nsq, rq = groups[gq]
                        inv_, B_inv_ = invs[gq % 2]
                        if rq == 1:
                            k.dma('sp', inv_[:, :, 0:nsq * 128], invcc_d[:, :, :], B_inv_, writes=[B_inv_])
                        else:
                            k.dma('sp', inv_[:, :, 0:nsq * 128], invc_d[:, :, nq - TC:nq - TC + nsq * 128], B_inv_, writes=[B_inv_])

                    cnp = dict(ib=0, ix=0)

                    def b0b_group(gidx):
                        n0, nsub, r = groups[gidx]
                        ntok = nsub * 128
                        inv, B_inv = invs[gidx % 2]
                        YA, B_YA = YAs[gidx % 2]
                        dT, B_dT = dTs[gidx % 2]
                        b0b_loads(gidx)
                        yield
                        yield from acquire(cnp, 'st1')
                        for gi in range(4):
                            ib = cnp['ib']
                            pb, B_pb = pbs[ib % 2]
                            psg, B_psg = psgs[ib % 2]
                            pwp, B_pwp = pws[ib % 2]
                            tq, B_tq = tqs[ib % 2]
                            sg, B_sg = sgs[ib % 2]
                            cnp['ib'] += 1
                            if r == 1:
                                srcs = [(st, poolc[:, gi * 2 + st, 0:ntok]) for st in range(2)]
                                Bm = B_poolc
                            else:
                                g = (n0 - TC) // 512
                                srcs = []
                                for dl in POOL_D[gi]:
                                    st = 4 * g + dl
                                    if 0 <= st < 32:
                                        srcs.append((2 + st, poolm[:, POOL_IDX[(gi, dl)], 0:ntok]))
                                Bm = B_poolm
                            k.op('pe', [lambda e, i=i: e.matmul(pb[:, 0:ntok], lhsT=A_tm[:, srcs[i][0], gi * 128:(gi + 1) * 128],
                                                                rhs=srcs[i][1], start=(i == 0), stop=(i == len(srcs) - 1))
                                        for i in range(len(srcs))], reads=[B_A, Bm], writes=[B_pb])
                            k.op('pe', [lambda e, s=s: e.matmul(psg[:, s * 128:(s + 1) * 128],
                                                                lhsT=A_tm[:, n0 // 128 + s, gi * 128:(gi + 1) * 128],
                                                                rhs=identB[:], start=True, stop=True) for s in range(nsub)],
                                 reads=[B_A, B_identB], writes=[B_psg])
                            yield
                            k.op('act', lambda e: e.activation(out=sg[:, 0:ntok], in_=psg[:, 0:ntok], func=AF.Copy, scale=-1.0),
                                 writes=[B_psg, B_sg])
                            k.op('dve', lambda e: e.tensor_tensor(out=tq[:, 0:ntok], in0=pb[:, 0:ntok], in1=inv[:, gi, 0:ntok],
                                                                 op=ALU.mult), reads=[B_inv], writes=[B_pb, B_tq])
                            yield
                            k.op('dve', lambda e: e.tensor_tensor(out=dT[:, gi, 0:ntok], in0=tq[:, 0:ntok], in1=sg[:, 0:ntok],
                                                                  op=ALU.add), reads=[B_tq, B_sg], writes=[B_dT])
                            yield
                            k.op('pe', lambda e: e.matmul(pwp[:, 0:ntok], lhsT=pw[:, gi, :], rhs=dT[:, gi, 0:ntok],
                                                          start=True, stop=True), reads=[B_pw, B_dT], writes=[B_pwp])
                            yield
                            k.op('dve', lambda e: e.tensor_scalar(out=YA[:, gi, 0:ntok], in0=pwp[:, 0:ntok],
                                                                 scalar1=abv[:, gi, 34:35], scalar2=abv[:, gi, 35:36],
                                                                 op0=ALU.add, op1=ALU.mult),
                                 reads=[B_abv], writes=[B_pwp, B_YA])
                            yield
                        cnp['st1'] = False
                        yield from acquire(cnp, 'st2')
                        for s in range(nsub):
                            ix = cnp['ix']
                            xt, B_xt = xts[ix % 2]
                            t1, B_t1 = t1s[ix % 2]
                            sm, B_sm = sms[ix % 2]
                            cnp['ix'] += 1
                            k.dma('sp', xt[:], xrows(n0 + s * 128, 128), B_xt, writes=[B_xt])
                            for hf in range(2):
                                po, B_po = pos[hf]
                                fns = []
                                for kc in range(8):
                                    lh = YA[:, kc, s * 128:(s + 1) * 128] if kc < 4 else \
                                        YB[:, kc - 4, n0 + s * 128:n0 + (s + 1) * 128]
                                    fns.append(lambda e, kc=kc, lh=lh: e.matmul(po[:], lhsT=lh,
                                                                               rhs=wout[:, kc, hf * 512:(hf + 1) * 512],
                                                                               start=(kc == 0), stop=(kc == 7)))
                                k.op('pe', fns, reads=[B_YA, B_YB, B_wout], writes=[B_po])
                                yield
                            gt, B_gt = gates[r]
                            yield from post_norm_g(pos[0][0][:], pos[0][1], pos[1][0][:], pos[1][1], xt[:], B_xt,
                                                   gt, B_gt, lng[:], B_lng, lnb[:], B_lnb, t1, B_t1, sm, B_sm,
                                                   H1_d[n0 + s * 128:n0 + (s + 1) * 128, :], DB("H1"))
                        cnp['st2'] = False

                    run_pipelined(b0b_group, len(groups), depth=2, skew=1)
                    k.barrier(release=loc)
                k.barrier(release=keep)

        def pcol(n):
            return 1 + n if n < TC else n + 3

        def jb_of(c):
            return 3 - c if c < 4 else 71 - c

        SCL = float(DH) ** -0.5

        def phase_a1a(groups):
            with ExitStack() as es:
                loc = []
                wins = [cx.tile(es, [128, 8, 1024], BF16, "mwin", loc) for _ in range(4)]
                wv_ = ml_win_d.rearrange("(kc p) n -> p kc n", p=128)
                for pi in range(4):
                    k.dma('pool', wins[pi][0][:], wv_[:, :, pi * 1024:(pi + 1) * 1024], wins[pi][1], writes=[wins[pi][1]])
                zt, B_zt = cx.tile(es, [128, 16, 2], F32, "zt", loc)
                k.op('pool', lambda e: e.memset(zt[:], 0.0), writes=[B_zt])
                k.dma('sp', XMT_d[:, :, 0:1], zt[:, :, 0:1], B_zt, reads=[B_zt], writes=[DB("XMT")], allow_slow_non_contiguous=True)
                k.dma('sp', XMT_d[:, :, 257:259], zt[:, :, 0:2], B_zt, reads=[B_zt], writes=[DB("XMT")], allow_slow_non_contiguous=True)
                k.dma('sp', XMT_d[:, :, NTP - 1:NTP], zt[:, :, 0:1], B_zt, reads=[B_zt], writes=[DB("XMT")], allow_slow_non_contiguous=True)
                xts = [cx.tile(es, [128, D], F32, "xt", loc) for _ in range(8)]
                uTs = [cx.tile(es, [128, 8, 512], BF16, "uT", loc) for _ in range(2)]
                sts = [cx.tile(es, [128, 512], F32, "st", loc) for _ in range(4)]

                def a1a_loads(gq):
                    nq, nsq, _ = groups[gq]
                    for sq in range(nsq):
                        k.dma('sp', xts[(gq % 2) * 4 + sq][0][:], H2_d[nq + sq * 128:nq + (sq + 1) * 128, :],
                              xts[(gq % 2) * 4 + sq][1], reads=[DB("H2")], writes=[xts[(gq % 2) * 4 + sq][1]])

                ptA = cx.psum(es, [128, 512], F32, "ptA")
                ptB = cx.psum(es, [128, 512], F32, "ptB")
                pms = [cx.psum(es, [128, 512], F32, "pm") for _ in range(3)]
                cnm = dict(im=0)

                def a1a_group(gidx):
                    n0, nsub, r = groups[gidx]
                    ntok = nsub * 128
                    uT, B_uT = uTs[gidx % 2]
                    a1a_loads(gidx)
                    yield
                    yield from acquire(cnm, 'st1')
                    for s_ in range(nsub):
                        xt, B_xt = xts[(gidx % 2) * 4 + s_]
                        yield from make_uT_g(1, r, 0, 8, None, [DB("H2")], xt, B_xt, ptA, ptB, uT, B_uT, s_)
                    cnm['st1'] = False
                    yield from acquire(cnm, 'st2')
                    for ec in range(32 if r == 0 else 16):
                        im = cnm['im']
                        pm, B_pm = pms[im % 3]
                        st, B_st = sts[im % 4]
                        cnm['im'] += 1
                        w, B_w = wins[ec // 8]
                        c0 = (ec % 8) * 128
                        k.op('pe', [lambda e, kc=kc: e.matmul(pm[:, 0:ntok], lhsT=w[:, kc, c0:c0 + 128], rhs=uT[:, kc, 0:ntok],
                                                              start=(kc == 0), stop=(kc == 7)) for kc in range(8)],
                             reads=[B_w, B_uT], writes=[B_pm])
                        yield
                        if ec < 16:
                            k.op('act', lambda e: e.copy(out=st[:, 0:ntok], in_=pm[:, 0:ntok]), writes=[B_pm, B_st])
                            k.dma('sp', XMT_d[:, ec, pcol(n0):pcol(n0) + ntok], st[:, 0:ntok], B_st, reads=[B_st],
                                  writes=[DB("XMT")])
                        else:
                            k.op('act', lambda e: e.activation(out=st[:, 0:ntok], in_=pm[:, 0:ntok], func=AF.Silu),
                                 writes=[B_pm, B_st])
                            k.dma('sp', P2_d[:, ec - 16, n0 - TC:n0 - TC + ntok], st[:, 0:ntok], B_st, reads=[B_st],
                                  writes=[DB("P2")])
                        yield
                    cnm['st2'] = False

                run_pipelined(a1a_group, len(groups), depth=2, skew=1)
                k.barrier(release=loc)

        def phase_a1b(groups):
            with ExitStack() as es:
                loc = []
                wq, B_wq = cx.tile(es, [128, 4, 4, 512], BF16, "wq", loc)
                wk, B_wk = cx.tile(es, [128, 4, 4, 512], BF16, "wk", loc)
                wv, B_wv = cx.tile(es, [128, 4, 4, 512], BF16, "wv", loc)
                for (w_, B_w, src) in ((wq, B_wq, ml_wq_d), (wk, B_wk, ml_wk_d), (wv, B_wv, ml_wv_d)):
                    k.dma('pool', w_[:], src.rearrange("h (dc p) n -> p h dc n", p=128), B_w, writes=[B_w])
                wgf, B_wgf = cx.tile(es, [128, 3, 16, 16], F32, "wgf", loc)
                wg, B_wg = cx.tile(es, [128, 3, 16, 16], BF16, "wg", loc)
                bg, B_bg = cx.tile(es, [128, 16], F32, "bg", loc)
                mlv, B_mlv = cx.tile(es, [128, 16, 6], F32, "mlv", loc)
                k.dma('sp', wgf[:], wg_d[:, :, :, :], B_wgf, writes=[B_wgf])
                k.dma('sp', bg[:], bg_d[:, :], B_bg, writes=[B_bg])
                k.dma('sp', mlv[:], mlv_d[:, :, :], B_mlv, writes=[B_mlv])
                k.op('dve', lambda e: e.tensor_scalar(out=wgf[:, 1, :, :], in0=wgf[:, 1, :, :], scalar1=1.0 / SCL,
                                                     scalar2=None, op0=ALU.mult), reads=[B_wgf], writes=[B_wgf])
                k.op('dve', lambda e: e.tensor_copy(out=wg[:], in_=wgf[:]), reads=[B_wgf], writes=[B_wg])
                XMs = [cx.tile(es, [128, 16, 258], F32, "XM", loc) for _ in range(2)]
                XMbs = [cx.tile(es, [128, 16, 256], BF16, "XMb", loc) for _ in range(2)]
                XCbs = [cx.tile(es, [128, 16, 256], BF16, "XCb", loc) for _ in range(2)]
                qT2 = [cx.tile(es, [128, 16, 256], BF16, "qT", loc) for _ in range(2)]
                kT2 = [cx.tile(es, [128, 16, 256], BF16, "kT", loc) for _ in range(2)]
                vT2 = [cx.tile(es, [128, 16, 256], BF16, "vT", loc) for _ in range(2)]
                tcs = [cx.tile(es, [128, 256], F32, "tc", loc) for _ in range(4)]
                xcfs = [cx.tile(es, [128, 256], F32, "xcf", loc) for _ in range(4)]
                p1s = [cx.tile(es, [128, 256], F32, "p1s", loc) for _ in range(3)]
                tms = [cx.tile(es, [128, 512], F32, "tm", loc) for _ in range(4)]
                tmbs = [cx.tile(es, [128, 512], BF16, "tmb", loc) for _ in range(4)]
                gss = [cx.tile(es, [128, 16], F32, "gs", loc) for _ in range(2)]
                pfs = [cx.psum(es, [128, 512], F32, "pf") for _ in range(3)]
                pts = [cx.psum(es, [128, 512], F32, "ptm") for _ in range(3)]
                pgt = cx.psum(es, [128, 512], F32, "pgt")
                i1 = 0
                ipf = 0
                ipt = 0
                itm = 0
                igs = 0
                def a1b_loads(gq):
                    pcq = pcol(groups[gq][0])
                    k.dma('sp', XMs[gq % 2][0][:], XMT_d[:, :, pcq - 1:pcq + 257], XMs[gq % 2][1], reads=[DB("XMT")],
                          writes=[XMs[gq % 2][1]])

                cn = dict(i1=0, ipf=0, ipt=0, itm=0, igs=0)

                def a1b_group(gidx):
                    n0, r = groups[gidx]
                    XM, B_XM = XMs[gidx % 2]
                    XMb, B_XMb = XMbs[gidx % 2]
                    XCb, B_XCb = XCbs[gidx % 2]
                    qT, B_qT = qT2[gidx % 2]
                    kT, B_kT = kT2[gidx % 2]
                    vT, B_vT = vT2[gidx % 2]
                    a1b_loads(gidx)
                    yield
                    yield from acquire(cn, 'st1')
                    k.op('act', lambda e: e.copy(out=XMb[:], in_=XM[:, :, 1:257]), reads=[B_XM], writes=[B_XMb])

                    def conv_tail(ec):
                        xcf, B_xcf = xcfs[ec % 4]
                        k.op('dve', lambda e: e.tensor_copy(out=XCb[:, ec, :], in_=xcf[:]), reads=[B_xcf], writes=[B_XCb])
                        if r == 0:
                            p1, B_p1 = p1s[cn['i1'] % 3]
                            cn['i1'] += 1
                            k.op('dve', lambda e: e.tensor_scalar(out=p1[:], in0=xcf[:], scalar1=mlv[:, ec, 4:5], scalar2=None,
                                                                  op0=ALU.mult), reads=[B_xcf, B_mlv], writes=[B_p1])
                            k.dma('sp', P1_d[:, ec, n0 - TC:n0 - TC + 256], p1[:], B_p1, reads=[B_p1], writes=[DB("P1")])

                    for ec in range(16):
                        tc_, B_tc = tcs[ec % 4]
                        xcf, B_xcf = xcfs[ec % 4]
                        k.op('dve', lambda e: e.tensor_scalar(out=tc_[:], in0=XM[:, ec, 0:256], scalar1=mlv[:, ec, 0:1],
                                                             scalar2=None, op0=ALU.mult),
                             reads=[B_XM, B_mlv], writes=[B_tc])
                        yield
                        k.op('dve', lambda e: e.scalar_tensor_tensor(out=tc_[:], in0=XM[:, ec, 1:257], scalar=mlv[:, ec, 1:2],
                                                                    in1=tc_[:], op0=ALU.mult, op1=ALU.add),
                             reads=[B_XM, B_mlv], writes=[B_tc])
                        yield
                        k.op('dve', lambda e: e.scalar_tensor_tensor(out=tc_[:], in0=XM[:, ec, 2:258], scalar=mlv[:, ec, 2:3],
                                                                    in1=tc_[:], op0=ALU.mult, op1=ALU.add),
                             reads=[B_XM, B_mlv], writes=[B_tc])
                        yield
                        k.op('act', lambda e: e.activation(out=xcf[:], in_=tc_[:], func=AF.Silu, bias=mlv[:, ec, 3:4], scale=1.0),
                             reads=[B_tc, B_mlv], writes=[B_xcf])
                        if ec >= 2:
                            conv_tail(ec - 2)
                        yield
                    conv_tail(14)
                    yield
                    conv_tail(15)
                    yield
                    cn['st1'] = False
                    yield from acquire(cn, 'st2')
                    for (dst, B_dst, w_, B_w, src, B_src, scl) in ((qT, B_qT, wq, B_wq, XCb, B_XCb, 1.0),
                                                                  (kT, B_kT, wk, B_wk, XCb, B_XCb, SCL),
                                                                  (vT, B_vT, wv, B_wv, XMb, B_XMb, 1.0)):
                        for h in range(4):
                            for op_ in range(2):
                                pf, B_pf = pfs[cn['ipf'] % 3]
                                cn['ipf'] += 1
                                fns = []
                                for o2 in range(2):
                                    oc = op_ * 2 + o2
                                    for dc in range(4):
                                        fns.append(lambda e, dc=dc, oc=oc, o2=o2: e.matmul(
                                            pf[:, o2 * 256:(o2 + 1) * 256], lhsT=w_[:, h, dc, oc * 128:(oc + 1) * 128],
                                            rhs=src[:, h * 4 + dc, :], start=(dc == 0), stop=(dc == 3)))
                                k.op('pe', fns, reads=[B_w, B_src], writes=[B_pf])
                                yield
                                dv = dst[:, h * 4 + op_ * 2:h * 4 + op_ * 2 + 2, :]
                                pv_ = pf[:, 0:512].rearrange("p (a t) -> p a t", a=2)
                                if cn['ipf'] % 2:
                                    k.op('act', lambda e: e.activation(out=dv, in_=pv_, func=AF.Copy, scale=scl),
                                         writes=[B_pf, B_dst])
                                else:
                                    k.op('dve', lambda e: e.tensor_scalar(out=dv, in0=pv_, scalar1=scl,
                                                                         scalar2=None, op0=ALU.mult), writes=[B_pf, B_dst])
                                yield
                    for s_ in range(2):
                        for (w_, B_w, src, B_src, scl, dd, dn) in ((wk, B_wk, XCb, B_XCb, SCL, Ktm_d, "Ktm"),
                                                                  (wv, B_wv, XMb, B_XMb, 1.0, Vtm_d, "Vtm")):
                            for h in range(4):
                                pt, B_pt = pts[cn['ipt'] % 3]
                                cn['ipt'] += 1
                                tm, B_tm = (tmbs if dn == "Ktm" else tms)[cn['itm'] % 4]
                                cn['itm'] += 1
                                k.op('pe', [lambda e, dc=dc: e.matmul(pt[:], lhsT=src[:, h * 4 + dc, s_ * 128:(s_ + 1) * 128],
                                                                      rhs=w_[:, h, dc, :], start=(dc == 0), stop=(dc == 3))
                                            for dc in range(4)], reads=[B_w, B_src], writes=[B_pt])
                                yield
                                if cn['itm'] % 2:
                                    k.op('act', lambda e: e.activation(out=tm[:], in_=pt[:], func=AF.Copy, scale=scl),
                                         writes=[B_pt, B_tm])
                                else:
                                    k.op('dve', lambda e: e.tensor_scalar(out=tm[:], in0=pt[:], scalar1=scl, scalar2=None,
                                                                         op0=ALU.mult), writes=[B_pt, B_tm])
                                yield
                                k.dma('sp', dd[n0 + s_ * 128:n0 + (s_ + 1) * 128, h * 512:(h + 1) * 512], tm[:], B_tm,
                                      reads=[B_tm], writes=[DB(dn)])
                        gs, B_gs = gss[cn['igs'] % 2]
                        cn['igs'] += 1
                        fns = []
                        for j, (src, B_src) in enumerate(((qT, B_qT), (kT, B_kT), (vT, B_vT))):
                            for ec in range(16):
                                fns.append(lambda e, j=j, ec=ec, src=src: e.matmul(
                                    pgt[0][:, 0:16], lhsT=src[:, ec, s_ * 128:(s_ + 1) * 128], rhs=wg[:, j, ec, :],
                                    start=(j == 0 and ec == 0), stop=(j == 2 and ec == 15)))
                        k.op('pe', fns, reads=[B_qT, B_kT, B_vT, B_wg], writes=[pgt[1]])
                        yield
                        k.op('dve', lambda e: e.tensor_tensor(out=gs[:], in0=pgt[0][:, 0:16], in1=bg[:], op=ALU.add),
                             reads=[B_bg], writes=[pgt[1], B_gs])
                        yield
                        nn = n0 + s_ * 128
                        k.dma('sp', G_d[nn:nn + 128, :], gs[:], B_gs, reads=[B_gs], writes=[DB("G")])
                        cblk = n0 // 256
                        jb = 0 if cblk == 0 else NT // 256 - cblk
                        k.dma('sp', Gb_d[jb * 256 + s_ * 128:jb * 256 + (s_ + 1) * 128, :], gs[:], B_gs, reads=[B_gs],
                              writes=[DB("Gb")])
                    if r == 0:
                        for s_ in range(2):
                            u = n0 // 128 + s_
                            k.dma('sp', QT_d[u].rearrange("p (a t) -> p a t", t=128), qT[:, :, s_ * 128:(s_ + 1) * 128], B_qT,
                                  reads=[B_qT], writes=[DB("QT")])
                            k.dma('sp', KT_d[u].rearrange("p (a t) -> p a t", t=128), kT[:, :, s_ * 128:(s_ + 1) * 128], B_kT,
                                  reads=[B_kT], writes=[DB("KT")])
                    cn['st2'] = False

                run_pipelined(a1b_group, len(groups), depth=2, skew=1)
                k.barrier(release=loc)

        CB = 256
        NBK = NT // CB

        def cmapB(d, j):
            if d == 0 or j == 0:
                return j
            return NBK - j

        def phase_g1(d, COL, B_COL, DECB, B_DECB, ABF, B_ABF, EKB, B_EKB):
            NP = NBK
            with ExitStack() as es:
                loc = []
                KEEP, B_KEEP = cx.tile(es, [NP, 4, CB], F32, "KEEP", loc)
                RST, B_RST = cx.tile(es, [NP, 4, CB], F32, "RST", loc)
                onesF, B_ones = cx.tile(es, [NP, 128], F32, "ones", loc)
                oneT, B_one = cx.tile(es, [NP, 1], F32, "one", loc)
                k.op('pool', lambda e: e.memset(KEEP[:], 1.0), writes=[B_KEEP])
                k.op('pool', lambda e: e.memset(KEEP[:, :, 0:1], 0.0), writes=[B_KEEP])
                k.op('pool', lambda e: e.memset(RST[:], 0.0), writes=[B_RST])
                k.op('pool', lambda e: e.memset(RST[:, :, 0:1], -1e30), writes=[B_RST])
                k.op('pool', lambda e: e.memset(onesF[:], 1.0), writes=[B_ones])
                k.op('pool', lambda e: e.memset(oneT[:], 1.0), writes=[B_one])
                GC, B_GC = cx.tile(es, [NP, CB, 16], F32, "GC", loc)
                IG, B_IG = cx.tile(es, [NP, 4, CB], F32, "IG", loc)
                FG, B_FG = cx.tile(es, [NP, 4, CB], F32, "FG", loc)
                NB, B_NB = cx.tile(es, [NP, 4, CB], F32, "NB", loc)
                Gg, B_Gg = cx.tile(es, [NP, 4, CB], F32, "Gg", loc)
                WE, B_WE = cx.tile(es, [NP, 4, CB], F32, "WE", loc)
                Mx, B_Mx = cx.tile(es, [NP, 4, CB], F32, "Mx", loc)
                Rr, B_Rr = cx.tile(es, [NP, 4, CB], F32, "Rr", loc)
                X5, B_X5 = cx.tile(es, [NP, 20, CB], F32, "X5", loc)
                Y5, B_Y5 = cx.tile(es, [NP, 20, CB], F32, "Y5", loc)
                sm, B_sm = cx.tile(es, [NP, 8, 4], F32, "smg", loc)
                rows, B_rows = cx.tile(es, [4, 4, NP], F32, "rows", loc)
                tmpd, B_tmpd = cx.tile(es, [NP, NP], F32, "tmpd", loc)
                pA = cx.psum(es, [128, 512], F32, "pA")
                pB = cx.psum(es, [128, 512], F32, "pB")
                pC = cx.psum(es, [128, 512], F32, "pC")
                idn = identF[0:NP, 0:NP]
                src = (G_d if d == 0 else Gb_d).rearrange("(c t) j -> c t j", t=CB)
                k.dma('sp', GC[:], src, B_GC, reads=[DB("G"), DB("Gb")], writes=[B_GC])
                if d == 0:
                    vi = GC[:, :, 0:4].rearrange("p t h -> p h t")
                    vf = GC[:, :, 4:8].rearrange("p t h -> p h t")
                else:
                    vi = GC[:, ::-1, 8:12].rearrange("p t h -> p h t")
                    vf = GC[:, ::-1, 12:16].rearrange("p t h -> p h t")
                k.op('dve', lambda e: e.tensor_copy(out=IG[:], in_=vi), reads=[B_GC], writes=[B_IG])
                k.op('dve', lambda e: e.tensor_copy(out=FG[:], in_=vf), reads=[B_GC], writes=[B_FG])
                k.op('act', lambda e: e.activation(out=FG[:], in_=FG[:], func=AF.Exp, scale=-1.0), writes=[B_FG])
                k.op('act', lambda e: e.activation(out=FG[:], in_=FG[:], func=AF.Ln, bias=oneT[:, 0:1], scale=1.0),
                     reads=[B_one], writes=[B_FG])
                fl = lambda t_: t_[:].rearrange("p h t -> p (h t)")
                k.op('dve', lambda e: e.tensor_tensor_scan(out=fl(NB), data0=fl(KEEP), data1=fl(FG), initial=0.0,
                                                          op0=ALU.mult, op1=ALU.add),
                     reads=[B_KEEP, B_FG], writes=[B_NB])
                k.op('dve', lambda e: e.tensor_tensor(out=Gg[:], in0=IG[:], in1=NB[:], op=ALU.add),
                     reads=[B_IG, B_NB], writes=[B_Gg])
                k.op('dve', lambda e: e.tensor_tensor(out=WE[:], in0=Gg[:], in1=NB[:, :, CB - 1:CB].to_broadcast([NP, 4, CB]),
                                                     op=ALU.subtract), reads=[B_Gg, B_NB], writes=[B_WE])
                k.op('dve', lambda e: e.tensor_reduce(out=sm[:, 1, :], in_=WE[:], axis=AX.X, op=ALU.max),
                     reads=[B_WE], writes=[B_sm])
                k.op('dve', lambda e: e.tensor_scalar(out=sm[:, 0, :], in0=NB[:, :, CB - 1], scalar1=-1.0, scalar2=None,
                                                     op0=ALU.mult), reads=[B_NB], writes=[B_sm])
                k.op('dve', lambda e: e.tensor_tensor_scan(out=fl(Mx), data0=fl(RST), data1=fl(Gg), initial=0.0,
                                                          op0=ALU.add, op1=ALU.max),
                     reads=[B_RST, B_Gg], writes=[B_Mx])
                k.op('pe', [lambda e: e.transpose(out=pA[0][0:4, 0:NP], in_=sm[:, 0, :], identity=idn),
                            lambda e: e.transpose(out=pA[0][0:4, NP:2 * NP], in_=sm[:, 1, :], identity=idn)],
                     reads=[B_sm, B_identF], writes=[pA[1]])
                k.op('dve', lambda e: e.tensor_copy(out=rows[:, 0:2, :],
                                                   in_=pA[0][0:4, 0:2 * NP].rearrange("p (a c) -> p a c", a=2)),
                     writes=[pA[1], B_rows])
                k.op('dve', lambda e: e.tensor_tensor_scan(out=rows[:, 2, :], data0=rows[:, 0, :], data1=rows[:, 1, :],
                                                          initial=0.0, op0=ALU.add, op1=ALU.max), writes=[B_rows])
                k.op('dve', [lambda e: e.memset(rows[:, 3, 0:1], 0.0),
                             lambda e: e.tensor_copy(out=rows[:, 3, 1:NP], in_=rows[:, 2, 0:NP - 1])], writes=[B_rows])
                k.op('pe', [lambda e: e.transpose(out=pB[0][0:NP, 0:4], in_=rows[:, 2, :], identity=identF[0:4, 0:4]),
                            lambda e: e.transpose(out=pB[0][0:NP, 4:8], in_=rows[:, 3, :], identity=identF[0:4, 0:4])],
                     reads=[B_rows, B_identF], writes=[pB[1]])
                k.op('dve', lambda e: e.tensor_copy(out=sm[:, 2:4, :], in_=pB[0][0:NP, 0:8].rearrange("p (a c) -> p a c", a=2)),
                     writes=[pB[1], B_sm])
                k.op('dve', lambda e: e.tensor_tensor(out=sm[:, 4, :], in0=sm[:, 0, :], in1=sm[:, 3, :], op=ALU.add),
                     writes=[B_sm])
                k.op('dve', lambda e: e.tensor_tensor(out=sm[:, 4, :], in0=sm[:, 4, :], in1=sm[:, 2, :], op=ALU.subtract),
                     writes=[B_sm])
                k.op('act', lambda e: e.activation(out=sm[:, 4, :], in_=sm[:, 4, :], func=AF.Exp), writes=[B_sm])
                bc = lambda a_: a_.unsqueeze(2).to_broadcast([NP, 4, CB])
                x5 = lambda i: X5[:, i * 4:(i + 1) * 4, :]
                lastb = lambda t_: t_[:, :, CB - 1:CB].to_broadcast([NP, 4, CB])
                k.op('dve', lambda e: e.tensor_tensor(out=x5(0), in0=WE[:], in1=bc(sm[:, 2, :]), op=ALU.subtract),
                     reads=[B_WE, B_sm], writes=[B_X5])
                k.op('dve', lambda e: e.tensor_tensor(out=Rr[:], in0=Mx[:], in1=bc(sm[:, 3, :]), op=ALU.max),
                     reads=[B_Mx, B_sm], writes=[B_Rr])
                k.op('dve', lambda e: e.tensor_tensor(out=x5(1), in0=Gg[:], in1=lastb(Rr), op=ALU.subtract),
                     reads=[B_Gg, B_Rr], writes=[B_X5])
                k.op('dve', lambda e: e.tensor_tensor(out=x5(2), in0=bc(sm[:, 3, :]), in1=Rr[:], op=ALU.subtract),
                     reads=[B_Rr, B_sm], writes=[B_X5])
                k.op('dve', lambda e: e.tensor_tensor(out=x5(3), in0=lastb(Rr), in1=Rr[:], op=ALU.subtract),
                     reads=[B_Rr], writes=[B_X5])
                k.op('dve', lambda e: e.tensor_tensor(out=x5(4), in0=NB[:], in1=Rr[:], op=ALU.subtract),
                     reads=[B_NB, B_Rr], writes=[B_X5])
                k.op('act', lambda e: e.activation(out=X5[:], in_=X5[:], func=AF.Exp), writes=[B_X5])
                if d == 0:
                    Ysrc, B_Ysrc = X5, B_X5
                else:
                    k.op('dve', lambda e: e.tensor_copy(out=Y5[:], in_=X5[:, :, ::-1]), reads=[B_X5], writes=[B_Y5])
                    Ysrc, B_Ysrc = Y5, B_Y5
                for bk, (pb_, B_pb) in enumerate((pA, pB)):
                    qs = list(range(bk * 10, bk * 10 + 10))
                    fns = []
                    for q in qs:
                        for ti in range(2):
                            o = ((q - bk * 10) * 2 + ti) * NP
                            fns.append(lambda e, q=q, ti=ti, o=o: e.transpose(out=pb_[:, o:o + NP],
                                                                             in_=Ysrc[:, q, ti * 128:(ti + 1) * 128], identity=idn))
                    k.op('pe', fns, reads=[B_Ysrc, B_identF], writes=[B_pb])
                    k.op('dve', lambda e: e.tensor_copy(
                        out=COL[:, qs[0]:qs[-1] + 1, :, :],
                        in_=pb_[:, 0:20 * NP].rearrange("p (a b c) -> p a b c", b=2, c=NP)),
                        writes=[B_pb, B_COL])
                k.op('dve', lambda e: e.tensor_copy(out=EKB[:], in_=COL[:, 0:4, :, :]), reads=[B_COL], writes=[B_EKB])
                k.op('dve', lambda e: e.tensor_copy(out=ABF[:], in_=COL[:, 4:8, :, :]), reads=[B_COL], writes=[B_ABF])
                for h in range(4):
                    k.op('dve', lambda e: e.tensor_scalar(out=tmpd[:], in0=idn, scalar1=sm[:, 4, h:h + 1], scalar2=None,
                                                         op0=ALU.mult), reads=[B_sm, B_identF], writes=[B_tmpd])
                    k.op('pe', lambda e: e.matmul(pC[0][:, h * NP:(h + 1) * NP], lhsT=onesF[:], rhs=tmpd[:], start=True,
                                                  stop=True), reads=[B_ones, B_tmpd], writes=[pC[1]])
                k.op('dve', lambda e: e.tensor_copy(out=DECB[:], in_=pC[0][:, 0:4 * NP].rearrange("p (a c) -> p a c", c=NP)),
                     writes=[pC[1], B_DECB])
                k.barrier(release=loc)

        def acquire(lk, name):
            while lk.get(name):
                yield
            lk[name] = True

        def run_pipelined(make_gen, n, depth, skew):
            active = []
            nxt = 0
            tick = 0
            last_start = -10 ** 9
            while nxt < n or active:
                if nxt < n and len(active) < depth and tick - last_start >= skew:
                    active.append(make_gen(nxt))
                    nxt += 1
                    last_start = tick
                for g in list(active):
                    try:
                        next(g)
                    except StopIteration:
                        active.remove(g)
                tick += 1

        def run_interleaved(gens):
            active = list(gens)
            while active:
                for g in list(active):
                    try:
                        next(g)
                    except StopIteration:
                        active.remove(g)

        def phase_scan_all():
            with ExitStack() as es:
                loc = []
                G1 = []
                for d in range(2):
                    COL, B_COL = cx.tile(es, [128, 20, 2, NBK], F32, "COL", loc)
                    DECB, B_DECB = cx.tile(es, [128, 4, NBK], F32, "DECB", loc)
                    ABF, B_ABF = cx.tile(es, [128, 4, 2, NBK], BF16, "ABF", loc)
                    EKB, B_EKB = cx.tile(es, [128, 4, 2, NBK], BF16, "EKB", loc)
                    msk, B_msk = cx.tile(es, [128, 128], F32, "msk", loc)
                    G1.append((COL, B_COL, DECB, B_DECB, ABF, B_ABF, EKB, B_EKB, msk, B_msk))
                for d in range(2):
                    COL, B_COL, DECB, B_DECB, ABF, B_ABF, EKB, B_EKB, msk, B_msk = G1[d]
                    phase_g1(d, COL, B_COL, DECB, B_DECB, ABF, B_ABF, EKB, B_EKB)
                    k.dma('sp', msk[:], masks_d[:, d, :], B_msk, writes=[B_msk])
                for hs0 in (0, 2):
                    with ExitStack() as es2:
                        loc2 = []
                        gens = [scan_dir(es2, loc2, d, hs0, G1[d]) for d in range(2)]
                        run_interleaved(gens)
                        k.barrier(release=loc2)
                k.barrier(release=loc)

        def scan_dir(es, loc, d, hs0, g1, nblocks=NBK):
            COL, B_COL, DECB, B_DECB, ABF, B_ABF, EKB, B_EKB, msk, B_msk = g1
            NHP = 2
            HW = NHP * 512
            C_, _ = cx.tile(es, [128, NHP * 4, 512], F32, "C", loc)
            Cb_, _ = cx.tile(es, [128, NHP * 4, 512], BF16, "Cb", loc)
            n_, B_n = cx.tile(es, [128, NHP * 4], F32, "n", loc)
            nb_, B_nb = cx.tile(es, [128, NHP * 4], BF16, "nb", loc)
            BC = [Buf("C_%d" % i) for i in range(NHP * 4)]
            BCb = [Buf("Cb_%d" % i) for i in range(NHP * 4)]
            k.op('pool', lambda e: e.memset(C_[:], 0.0), writes=BC)
            k.op('pool', lambda e: e.memset(Cb_[:], 0.0), writes=BCb)
            k.op('pool', lambda e: e.memset(n_[:], 0.0), writes=[B_n])
            k.op('pool', lambda e: e.memset(nb_[:], 0.0), writes=[B_nb])
            kbs = [[cx.tile(es, [128, HW], BF16, "kb", loc) for _ in range(2)] for _ in range(2)]
            kTs = [[cx.tile(es, [128, NHP * 4, 128], BF16, "kTt", loc) for _ in range(2)] for _ in range(2)]
            vws = [[cx.tile(es, [128, HW], BF16, "vw", loc) for _ in range(2)] for _ in range(2)]
            vas = [[cx.tile(es, [128, HW], BF16, "va", loc) for _ in range(2)] for _ in range(2)]
            qTs = [[cx.tile(es, [128, NHP * 4, 128], BF16, "qTt", loc) for _ in range(2)] for _ in range(2)]
            vts = [cx.tile(es, [128, HW], F32, "vt", loc) for _ in range(2)]
            PTs = [cx.tile(es, [128, NHP, 128], BF16, "PT", loc) for _ in range(3)]
            HOs = [cx.tile(es, [128, 512], F32, "HO", loc) for _ in range(2)]
            Tqs = [cx.tile(es, [128, 512], F32, "Tq", loc) for _ in range(2)]
            smS = [cx.tile(es, [128, 40], F32, "smS", loc) for _ in range(2)]
            pks = cx.psum(es, [128, 512], F32, "pks")
            bank3 = [cx.psum(es, [128, 512], F32, "pb3") for _ in range(3)]
            SM0 = 256
            cnt = dict(vt=0, pt=0, ho=0, tq=0, ss=0, bk=0)
            yield

            def loads(j):
                c = cmapB(d, j)
                lat = c >= 1
                tl = [2 * c, 2 * c + 1] if d == 0 else [2 * c + 1, 2 * c]
                for i, u in enumerate(tl):
                    kb, B_kb = kbs[j % 2][i]
                    k.dma('sp', kb[:], Ktm_d[u * 128:(u + 1) * 128, hs0 * 512:hs0 * 512 + HW], B_kb, reads=[DB("Ktm")], writes=[B_kb])
                    if lat:
                        kT, B_kT = kTs[j % 2][i]
                        qT, B_qT = qTs[j % 2][i]
                        k.dma('sp', kT[:], KT_d[u][:, hs0 * 512:hs0 * 512 + NHP * 512].rearrange("p (a t) -> p a t", t=128),
                              B_kT, reads=[DB("KT")], writes=[B_kT])
                        k.dma('sp', qT[:], QT_d[u][:, hs0 * 512:hs0 * 512 + NHP * 512].rearrange("p (a t) -> p a t", t=128),
                              B_qT, reads=[DB("QT")], writes=[B_qT])

            loads(0)
            yield
            for j in range(nblocks):
                c = cmapB(d, j)
                lat = c >= 1
                tl = [2 * c, 2 * c + 1] if d == 0 else [2 * c + 1, 2 * c]
                col = lambda q, h, ti: COL[:, q * 4 + h, ti, j:j + 1]
                for i, u in enumerate(tl):
                    ti = u - 2 * c
                    vt, B_vt = vts[cnt['vt'] % 2]
                    cnt['vt'] += 1
                    vw, B_vw = vws[j % 2][i]
                    va, B_va = vas[j % 2][i]
                    k.dma('sp', vt[:], Vtm_d[u * 128:(u + 1) * 128, hs0 * 512:hs0 * 512 + HW], B_vt, reads=[DB("Vtm")], writes=[B_vt])
                    for hl in range(NHP):
                        h = hs0 + hl
                        k.op('act', lambda e: e.activation(out=vw[:, hl * 512:(hl + 1) * 512], in_=vt[:, hl * 512:(hl + 1) * 512],
                                                           func=AF.Copy, scale=col(0, h, ti)),
                             reads=[B_vt, B_COL], writes=[B_vw])
                        if lat:
                            k.op('act', lambda e: e.activation(out=va[:, hl * 512:(hl + 1) * 512], in_=vt[:, hl * 512:(hl + 1) * 512],
                                                               func=AF.Copy, scale=col(1, h, ti)),
                                 reads=[B_vt, B_COL], writes=[B_va])
                    yield
                if j + 1 < nblocks:
                    loads(j + 1)
                if lat:
                    for i, u in enumerate(tl):
                        ti = u - 2 * c
                        qT, B_qT = qTs[j % 2][i]
                        sS, B_sS = smS[cnt['ss'] % 2]
                        cnt['ss'] += 1
                        PTl = []
                        for w in range(i + 1):
                            kT, B_kT = kTs[j % 2][w]
                            PT, B_PT = PTs[cnt['pt'] % 3]
                            cnt['pt'] += 1
                            PTl.append((PT, B_PT, w))
                            fns = []
                            for hl in range(NHP):
                                for kc in range(4):
                                    fns.append(lambda e, hl=hl, kc=kc: e.matmul(pks[0][:, hl * 128:(hl + 1) * 128],
                                                                               lhsT=kT[:, hl * 4 + kc, :], rhs=qT[:, hl * 4 + kc, :],
                                                                               start=(kc == 0), stop=(kc == 3)))
                            k.op('pe', fns, reads=[B_qT, B_kT], writes=[pks[1]])
                            yield
                            kqv = pks[0][:, 0:NHP * 128].rearrange("p (h t) -> p h t", h=NHP)
                            if w == i:
                                k.op('dve', lambda e: e.tensor_tensor(out=PT[:], in0=kqv,
                                                                     in1=msk[:].unsqueeze(1).to_broadcast([128, NHP, 128]), op=ALU.mult),
                                     reads=[B_msk], writes=[pks[1], B_PT])
                            else:
                                k.op('dve', lambda e: e.tensor_copy(out=PT[:], in_=kqv), writes=[pks[1], B_PT])
                            yield
                        fns = []
                        rds = [B_qT, B_nb, B_ABF]
                        for hl in range(NHP):
                            h = hs0 + hl
                            for wi, (PT, B_PT, w) in enumerate(PTl):
                                tw = tl[w] - 2 * c
                                fns.append(lambda e, hl=hl, h=h, PT=PT, tw=tw, wi=wi: e.matmul(
                                    pks[0][:, SM0 + 2 * hl:SM0 + 2 * hl + 1], lhsT=PT[:, hl, :], rhs=ABF[:, h, tw, j:j + 1],
                                    start=(wi == 0), stop=(wi == len(PTl) - 1)))
                                rds.append(B_PT)
                            for kc in range(4):
                                fns.append(lambda e, hl=hl, kc=kc: e.matmul(pks[0][:, SM0 + 2 * hl + 1:SM0 + 2 * hl + 2],
                                                                           lhsT=qT[:, hl * 4 + kc, :],
                                                                           rhs=nb_[:, hl * 4 + kc:hl * 4 + kc + 1],
                                                                           start=(kc == 0), stop=(kc == 3)))
                        k.op('pe', fns, reads=rds, writes=[pks[1]])
                        yield
                        NS = 2 * NHP
                        k.op('act', lambda e: e.copy(out=sS[:, 0:NS], in_=pks[0][:, SM0:SM0 + NS]), writes=[pks[1], B_sS])
                        yield
                        sv = lambda a_: sS[:, 8 * a_:8 * a_ + NHP]
                        ev = sS[:, 0:NS].rearrange("p (h two) -> p two h", two=2)
                        cq = lambda q: COL[:, q * 4 + hs0:q * 4 + hs0 + NHP, ti, j]
                        k.op('dve', lambda e: e.tensor_tensor(out=sv(1), in0=ev[:, 0, :], in1=cq(3), op=ALU.mult),
                             reads=[B_COL], writes=[B_sS])
                        k.op('dve', lambda e: e.tensor_tensor(out=sv(2), in0=ev[:, 1, :], in1=cq(2), op=ALU.mult),
                             reads=[B_COL], writes=[B_sS])
                        yield
                        k.op('dve', lambda e: e.tensor_tensor(out=sv(1), in0=sv(1), in1=sv(2), op=ALU.add), writes=[B_sS])
                        yield
                        k.op('dve', lambda e: e.tensor_scalar(out=sv(2), in0=sv(1), scalar1=-1.0, scalar2=None, op0=ALU.mult),
                             writes=[B_sS])
                        yield
                        k.op('dve', lambda e: e.tensor_tensor(out=sv(1), in0=sv(1), in1=sv(2), op=ALU.max), writes=[B_sS])
                        yield
                        k.op('dve', lambda e: e.tensor_tensor(out=sv(1), in0=sv(1), in1=cq(4), op=ALU.max),
                             reads=[B_COL], writes=[B_sS])
                        yield
                        k.op('dve', lambda e: e.reciprocal(out=sv(1), in_=sv(1)), writes=[B_sS])
                        yield
                        k.op('dve', lambda e: e.tensor_tensor(out=sv(2), in0=sv(1), in1=cq(3), op=ALU.mult),
                             reads=[B_COL], writes=[B_sS])
                        k.op('dve', lambda e: e.tensor_tensor(out=sv(3), in0=sv(1), in1=cq(2), op=ALU.mult),
                             reads=[B_COL], writes=[B_sS])
                        yield
                        Hd, Hn = (HF_d, "HF") if d == 0 else (HB_d, "HB")
                        for hl in range(NHP):
                            h = hs0 + hl
                            Tq, B_Tq = Tqs[cnt['tq'] % 2]
                            cnt['tq'] += 1
                            HO, B_HO = HOs[cnt['ho'] % 2]
                            cnt['ho'] += 1
                            pnum_, B_pnum = bank3[cnt['bk'] % 3]
                            pqc_, B_pqc = bank3[(cnt['bk'] + 1) % 3]
                            cnt['bk'] += 2
                            fns = []
                            rds = []
                            for wi, (PT, B_PT, w) in enumerate(PTl):
                                va, B_va = vas[j % 2][w]
                                fns.append(lambda e, PT=PT, va=va, wi=wi: e.matmul(pnum_[:], lhsT=PT[:, hl, :],
                                                                                  rhs=va[:, hl * 512:(hl + 1) * 512],
                                                                                  start=(wi == 0), stop=(wi == len(PTl) - 1)))
                                rds += [B_PT, B_va]
                            k.op('pe', fns, reads=rds, writes=[B_pnum])
                            k.op('pe', [lambda e, kc=kc: e.matmul(pqc_[:], lhsT=qT[:, hl * 4 + kc, :], rhs=Cb_[:, hl * 4 + kc, :],
                                                                  start=(kc == 0), stop=(kc == 3)) for kc in range(4)],
                                 reads=[B_qT] + BCb[hl * 4:hl * 4 + 4], writes=[B_pqc])
                            yield
                            k.op('act', lambda e: e.activation(out=Tq[:], in_=pqc_[:], func=AF.Copy, scale=sS[:, 24 + hl:25 + hl]),
                                 reads=[B_sS], writes=[B_pqc, B_Tq])
                            yield
                            k.op('dve', lambda e: e.scalar_tensor_tensor(out=HO[:], in0=pnum_[:], scalar=sS[:, 16 + hl:17 + hl],
                                                                        in1=Tq[:], op0=ALU.mult, op1=ALU.add),
                                 reads=[B_sS, B_Tq], writes=[B_pnum, B_HO])
                            yield
                            r0 = u * 128 - TC
                            k.dma('sp', Hd[r0:r0 + 128, h * 512:(h + 1) * 512], HO[:], B_HO, reads=[B_HO], writes=[DB(Hn)])
                if j == nblocks - 1:
                    continue
                fns = []
                for hl in range(NHP):
                    h = hs0 + hl
                    for kc in range(4):
                        for i, u in enumerate(tl):
                            ti = u - 2 * c
                            kb, B_kb = kbs[j % 2][i]
                            fns.append(lambda e, hl=hl, h=h, kc=kc, kb=kb, ti=ti, i=i: e.matmul(
                                pks[0][:, SM0 + 32 + hl * 4 + kc:SM0 + 33 + hl * 4 + kc],
                                lhsT=kb[:, hl * 512 + kc * 128:hl * 512 + (kc + 1) * 128], rhs=EKB[:, h, ti, j:j + 1],
                                start=(i == 0), stop=(i == 1)))
                k.op('pe', fns, reads=[B_EKB, kbs[j % 2][0][1], kbs[j % 2][1][1]], writes=[pks[1]])
                yield
                k.op('dve', lambda e: e.tensor_tensor(out=n_[:].rearrange("p (h c) -> p h c", h=NHP),
                                                     in0=n_[:].rearrange("p (h c) -> p h c", h=NHP),
                                                     in1=DECB[:, hs0:hs0 + NHP, j:j + 1].to_broadcast([128, NHP, 4]), op=ALU.mult),
                     reads=[B_DECB, B_nb], writes=[B_n])
                yield
                k.op('dve', lambda e: e.tensor_tensor(out=n_[:], in0=n_[:], in1=pks[0][:, SM0 + 32:SM0 + 32 + NHP * 4], op=ALU.add),
                     writes=[pks[1], B_n])
                yield
                k.op('dve', lambda e: e.tensor_copy(out=nb_[:], in_=n_[:]), reads=[B_n], writes=[B_nb])
                for hl in range(NHP):
                    h = hs0 + hl
                    for kc in range(4):
                        ii = hl * 4 + kc
                        pdc_, B_pdc = bank3[cnt['bk'] % 3]
                        cnt['bk'] += 1
                        fns = []
                        rds = []
                        for i in range(2):
                            kb, B_kb = kbs[j % 2][i]
                            vw, B_vw = vws[j % 2][i]
                            fns.append(lambda e, kb=kb, vw=vw, i=i: e.matmul(pdc_[:], lhsT=kb[:, hl * 512 + kc * 128:hl * 512 + (kc + 1) * 128],
                                                                           rhs=vw[:, hl * 512:(hl + 1) * 512], start=(i == 0), stop=(i == 1)))
                            rds += [B_kb, B_vw]
                        k.op('pe', fns, reads=rds, writes=[B_pdc])
                        k.op('act', lambda e: e.activation(out=C_[:, ii, :], in_=C_[:, ii, :], func=AF.Copy,
                                                           scale=DECB[:, h, j:j + 1]),
                             reads=[B_DECB, BCb[ii]], writes=[BC[ii]])
                        yield
                        k.op('dve', lambda e: e.tensor_tensor(out=C_[:, ii, :], in0=C_[:, ii, :], in1=pdc_[:], op=ALU.add),
                             writes=[B_pdc, BC[ii]])
                        yield
                        k.op('dve', lambda e: e.tensor_copy(out=Cb_[:, ii, :], in_=C_[:, ii, :]), reads=[BC[ii]], writes=[BCb[ii]])
                        yield

        def phase_c1(ntiles=T // 128):
            with ExitStack() as es:
                loc = []
                mwout, B_mwout = cx.tile(es, [128, 16, D], BF16, "mwout", loc)
                k.dma('pool', mwout[:], ml_wout_d.rearrange("(ec p) n -> p ec n", p=128), B_mwout, writes=[B_mwout])
                mlv, B_mlv = cx.tile(es, [128, 16, 6], F32, "mlv", loc)
                k.dma('sp', mlv[:], mlv_d[:, :, :], B_mlv, writes=[B_mlv])
                gt, B_gt = cx.tile(es, [128, D], F32, "gm1", loc)
                k.dma('sp', gt[:], gbc_d[4], B_gt, reads=[DB("gbc4")], writes=[B_gt])
                lng, B_lng = cx.tile(es, [128, D], F32, "lng", loc)
                lnb, B_lnb = cx.tile(es, [128, D], F32, "lnb", loc)
                k.dma('sp', lng[:], lng_d[:, 2, :], B_lng, writes=[B_lng])
                k.dma('sp', lnb[:], lnb_d[:, 2, :], B_lnb, writes=[B_lnb])
                ND = 3
                hfs = [cx.tile(es, [128, E], F32, "hf", loc) for _ in range(ND)]
                hbs = [cx.tile(es, [128, E], F32, "hb", loc) for _ in range(ND)]
                hns = [cx.tile(es, [128, E], BF16, "hn", loc) for _ in range(ND)]
                p1s = [cx.tile(es, [128, 16, 128], F32, "p1t", loc) for _ in range(ND)]
                p2s = [cx.tile(es, [128, 16, 128], F32, "p2t", loc) for _ in range(ND)]
                yTs = [cx.tile(es, [128, 16, 128], BF16, "yT", loc) for _ in range(ND)]
                sts = [cx.tile(es, [128, 48], F32, "stc", loc) for _ in range(ND)]
                xts = [cx.tile(es, [128, D], F32, "xt", loc) for _ in range(ND)]
                t1s = [cx.tile(es, [128, D], F32, "t1", loc) for _ in range(ND)]
                sms = [cx.tile(es, [128, 16], F32, "sm", loc) for _ in range(ND)]
                pths = [cx.psum(es, [128, 1024], BF16, "pth") for _ in range(4)]
                pos = [cx.psum(es, [128, 512], F32, "po") for _ in range(4)]
                def c1_loads(iq):
                    tq0 = iq * 128
                    k.dma('sp', hfs[iq % ND][0][:], HF_d[tq0:tq0 + 128, :], hfs[iq % ND][1], reads=[DB("HF")], writes=[hfs[iq % ND][1]])
                    k.dma('sp', hbs[iq % ND][0][:], HB_d[tq0:tq0 + 128, :], hbs[iq % ND][1], reads=[DB("HB")], writes=[hbs[iq % ND][1]])
                    k.dma('sp', p1s[iq % ND][0][:], P1_d[:, :, tq0:tq0 + 128], p1s[iq % ND][1], reads=[DB("P1")], writes=[p1s[iq % ND][1]])
                    k.dma('sp', p2s[iq % ND][0][:], P2_d[:, :, tq0:tq0 + 128], p2s[iq % ND][1], reads=[DB("P2")], writes=[p2s[iq % ND][1]])
                    k.dma('sp', xts[iq % ND][0][:], H2_d[TC + tq0:TC + tq0 + 128, :], xts[iq % ND][1], reads=[DB("H2")], writes=[xts[iq % ND][1]])

                def c1_tile(i):
                    t0 = i * 128
                    hf, B_hf = hfs[i % ND]
                    hb, B_hb = hbs[i % ND]
                    hn, B_hn = hns[i % ND]
                    p1, B_p1 = p1s[i % ND]
                    p2, B_p2 = p2s[i % ND]
                    yT, B_yT = yTs[i % ND]
                    st, B_st = sts[i % ND]
                    xt, B_xt = xts[i % ND]
                    t1, B_t1 = t1s[i % ND]
                    sm, B_sm = sms[i % ND]
                    c1_loads(i)
                    yield
                    k.op('dve', lambda e: e.tensor_tensor(out=hf[:], in0=hf[:], in1=hb[:], op=ALU.add), reads=[B_hb], writes=[B_hf])
                    yield
                    k.op('dve', [lambda e, h=h: e.bn_stats(out=st[:, h * 6:(h + 1) * 6], in_=hf[:, h * 512:(h + 1) * 512])
                                 for h in range(4)], reads=[B_hf], writes=[B_st])
                    yield
                    for h in range(4):
                        k.op('dve', lambda e: e.bn_aggr(out=st[:, 24 + 2 * h:26 + 2 * h], in_=st[:, h * 6:(h + 1) * 6]), writes=[B_st])
                        yield
                    mvv = st[:, 24:32].rearrange("p (h two) -> p two h", two=2)
                    k.op('act', lambda e: e.activation(out=st[:, 32:36], in_=mvv[:, 1, :], func=AF.Sqrt, bias=epsT[:, 0:1], scale=1.0),
                         reads=[B_eps], writes=[B_st])
                    yield
                    k.op('dve', lambda e: e.reciprocal(out=st[:, 32:36], in_=st[:, 32:36]), writes=[B_st])
                    yield
                    k.op('dve', lambda e: e.scalar_tensor_tensor(out=st[:, 36:40], in0=mvv[:, 0, :], scalar=-1.0, in1=st[:, 32:36],
                                                                op0=ALU.mult, op1=ALU.mult), writes=[B_st])
                    yield
                    for h in range(4):
                        k.op('act', lambda e: e.activation(out=hn[:, h * 512:(h + 1) * 512], in_=hf[:, h * 512:(h + 1) * 512],
                                                           func=AF.Identity, scale=st[:, 32 + h:33 + h], bias=st[:, 36 + h:37 + h]),
                             reads=[B_st, B_hf], writes=[B_hn])
                        yield
                    for half in range(2):
                        pth, B_pth = pths[(2 * i + half) % 4]
                        k.op('pe', [lambda e, q=q: e.transpose(out=pth[:, q * 128:(q + 1) * 128],
                                                              in_=hn[:, (half * 8 + q) * 128:(half * 8 + q + 1) * 128],
                                                              identity=identB[:]) for q in range(8)],
                             reads=[B_hn, B_identB], writes=[B_pth])
                        yield
                        for q in range(8):
                            ec = half * 8 + q
                            k.op('dve', lambda e: e.scalar_tensor_tensor(out=p1[:, ec, :], in0=pth[:, q * 128:(q + 1) * 128],
                                                                        scalar=mlv[:, ec, 5:6], in1=p1[:, ec, :],
                                                                        op0=ALU.mult, op1=ALU.add),
                                 reads=[B_mlv], writes=[B_pth, B_p1])
                            yield
                    k.op('dve', lambda e: e.tensor_tensor(out=yT[:], in0=p1[:], in1=p2[:], op=ALU.mult),
                         reads=[B_p1, B_p2], writes=[B_yT])
                    yield
                    po0, po1 = pos[(2 * i) % 4], pos[(2 * i + 1) % 4]
                    for hfx, (po, B_po) in enumerate((po0, po1)):
                        k.op('pe', [lambda e, ec=ec: e.matmul(po[:], lhsT=yT[:, ec, :], rhs=mwout[:, ec, hfx * 512:(hfx + 1) * 512],
                                                              start=(ec == 0), stop=(ec == 15)) for ec in range(16)],
                             reads=[B_yT, B_mwout], writes=[B_po])
                        yield
                    yield from post_norm_g(po0[0][:], po0[1], po1[0][:], po1[1], xt[:], B_xt,
                                           gt, B_gt, lng[:], B_lng, lnb[:], B_lnb, t1, B_t1, sm, B_sm,
                                           H3_d[t0:t0 + 128, :], DB("H3"))

                run_pipelined(c1_tile, ntiles, depth=ND, skew=18)
                k.barrier(release=loc)

        phase_mods()
        if dbg.get('full'):
            layer0_mixer(L0_GROUPS)
            phase_ffn(0, H1_d, "H1", H2_d, "H2", [(i * 256, i * 256, 1 if i == 0 else 0) for i in range(NT // 256)])
            phase_a1a(L0_GROUPS)
            phase_a1b([(i * 256, 1 if i == 0 else 0) for i in range(NT // 256)])
            phase_scan_all()
            phase_c1()
            phase_ffn(1, H3_d, "H3", out_d, "out", [(i * 256, i * 256, 0) for i in range(T // 256)])
        if dbg.get('a1'):
            ng = dbg.get('ngroups', 9)
            phase_a1a(L0_GROUPS[:ng])
            nt256 = (256 + (ng - 1) * 512) // 256
            phase_a1b([(i * 256, 1 if i == 0 else 0) for i in range(nt256)])
        if dbg.get('scan'):
            phase_scan_all()
        if dbg.get('c1'):
            phase_c1(dbg.get('c1tiles', T // 128))
        if dbg.get('l0mix'):
            layer0_mixer(L0_GROUPS[:dbg.get('ngroups', 9)])
        if dbg.get('ffn0'):
            groups = [(i * 256, i * 256, 1 if i == 0 else 0) for i in range(dbg.get('ngroups', 17))]
            phase_ffn(0, H1_d, "H1", H2_d, "H2", groups)
        k.barrier()
    return nc


POOL_W = (2, 4, 8, 16)
POOL_D = {0: list(range(-1, 4)), 1: list(range(-1, 5)), 2: list(range(-2, 6)), 3: list(range(-4, 8))}

def pool_consts():
    mats = []
    for gi, w in enumerate(POOL_W):
        h = w // 2
        for dl in POOL_D[gi]:
            m = np.zeros((2, 64, 8, 64), np.float32)
            for rr in range(2):
                srow = 2 * dl + rr
                for rho in range(8):
                    if rho - h <= srow <= rho + h - 1:
                        for c in range(64):
                            lo = max(c - h, 0); hi = min(c + h - 1, 63)
                            m[rr, lo:hi + 1, rho, c] = 1.0
            mats.append(m.reshape(128, 512))
    poolm = np.ascontiguousarray(np.stack(mats, 1))
    pc = np.zeros((128, 8, 256), np.float32)
    for gi, w in enumerate(POOL_W):
        h = w // 2
        for st in range(2):
            for p in range(128):
                n_src = st * 128 + p
                for n in range(256):
                    if n - h <= n_src <= n + h - 1:
                        pc[p, gi * 2 + st, n] = 1.0
    def cnt(L, w):
        t = np.arange(L)
        lo = np.clip(t - w // 2, 0, L); hi = np.clip(t + w - w // 2, 0, L)
        return (hi - lo).astype(np.float32)
    invc = np.zeros((4, 64, 64), np.float32); invcc = np.zeros((4, 256), np.float32)
    for gi, w in enumerate(POOL_W):
        c64 = cnt(64, w)
        invc[gi] = (np.float32(1.0) / c64)[:, None] * (np.float32(1.0) / c64)[None, :]
        invcc[gi] = np.float32(1.0) / cnt(256, w)
    invc = np.ascontiguousarray(np.broadcast_to(invc.reshape(1, 4, 4096), (128, 4, 4096)))
    invcc = np.ascontiguousarray(np.broadcast_to(invcc.reshape(1, 4, 256), (128, 4, 256)))
    return poolm, pc, invc, invcc

def abvec(inp):
    v = np.zeros((128, 4, 36), np.float32)
    v[:, :, 0:31] = inp['ab_conv_w'][0].reshape(31, 4, 128).transpose(2, 1, 0)
    v[:, :, 31] = inp['ab_conv_b'][0].reshape(4, 128).T
    v[:, :, 32] = inp['ab_norm_g'][0].reshape(4, 128).T
    v[:, :, 33] = inp['ab_norm_b'][0].reshape(4, 128).T
    v[:, :, 34] = inp['ab_pool_b'][0].reshape(4, 128).T
    v[:, :, 35] = inp['ab_pool_scale'][0].reshape(4, 128).T
    return v

def ml_consts(inp):
    wgt = inp['ml_w_gate'][0]
    wg = np.ascontiguousarray(wgt.reshape(2, 3, 16, 128, 8).transpose(3, 1, 2, 0, 4).reshape(128, 3, 16, 16)).astype(np.float32)
    bg = np.ascontiguousarray(np.broadcast_to(inp['ml_b_gate'][0].reshape(1, 16), (128, 16))).astype(np.float32)
    mlv = np.zeros((128, 16, 6), np.float32)
    for i in range(3):
        mlv[:, :, i] = inp['ml_conv_w'][0][i].reshape(16, 128).T
    mlv[:, :, 3] = inp['ml_conv_b'][0].reshape(16, 128).T
    mlv[:, :, 4] = inp['ml_skip'][0].reshape(16, 128).T
    mlv[:, :, 5] = inp['ml_norm_g'][0].reshape(16, 128).T
    masks = np.zeros((128, 2, 128), np.float32)
    s = np.arange(128)[:, None]; t = np.arange(128)[None, :]
    masks[:, 0, :] = (s <= t)
    masks[:, 1, :] = (s >= t)
    return dict(wg=wg, bg_bc=bg, mlvec=mlv, masks=masks, ml_w_in=inp['ml_w_in'][0], ml_wq=inp['ml_wq'][0], ml_wk=inp['ml_wk'][0],
                ml_wv=inp['ml_wv'][0], ml_w_out=inp['ml_w_out'][0])


def host_inputs(inp, b):
    c = inp['c'][b]
    cc = inp['c_ctx']
    cvec = np.stack([c.reshape(8, 128).T, cc.reshape(8, 128).T], axis=-1).astype(np.float32)
    mb = inp['mod_b']
    modb_col = np.ascontiguousarray(np.broadcast_to(mb.reshape(2, 48, 128).transpose(2, 0, 1)[..., None], (128, 2, 48, 2))).astype(np.float32)
    modb_bc = np.ascontiguousarray(np.broadcast_to(mb[None], (128, 2, 6144))).astype(np.float32)
    lng = np.ascontiguousarray(np.broadcast_to(inp['ln_g'].reshape(1, 4, 1024), (128, 4, 1024))).astype(np.float32)
    lnb = np.ascontiguousarray(np.broadcast_to(inp['ln_b'].reshape(1, 4, 1024), (128, 4, 1024))).astype(np.float32)
    return dict(x=np.ascontiguousarray(inp['x'][b]), ctx=np.ascontiguousarray(inp['ctx'][b]), cvec=np.ascontiguousarray(cvec),
                modb_col=modb_col, modb_bc=modb_bc, lng_bc=lng, lnb_bc=lnb)


_NC_CACHE = {}


def kernel(**inputs):
    inp = {k_: np.asarray(v, dtype=np.float32) for k_, v in inputs.items()}
    if 'nc' not in _NC_CACHE:
        _NC_CACHE['nc'] = build(dict(full=True))
    nc = _NC_CACHE['nc']
    poolm, pc, invc, invcc = pool_consts()
    shared = dict(mod_w=inp['mod_w'], ident=np.eye(128, dtype=np.float32), ffn_w_in=inp['ffn_w_in'], ffn_w_out=inp['ffn_w_out'],
                  ab_w_in=inp['ab_w_in'][0], ab_w_out=inp['ab_w_out'][0], ab_pool_w=inp['ab_pool_w'][0],
                  poolm=poolm, poolc=pc, invc=invc, invcc=invcc, abvec=abvec(inp))
    shared.update(ml_consts(inp))
    shared = {k_: np.ascontiguousarray(v, dtype=np.float32) for k_, v in shared.items()}
    in_maps = []
    for b in range(8):
        m = dict(shared)
        m.update(host_inputs(inp, b))
        in_maps.append(m)
    res = run_bass_kernel_spmd(nc, in_maps, core_ids=list(range(8)))
    return np.stack([np.asarray(r["out"], dtype=np.float32) for r in res.results], axis=0)
```

```python
import numpy as np
import concourse.bass as bass
import concourse.mybir as mybir
from concourse.bass_utils import run_bass_kernel_spmd
from contextlib import ExitStack

F32 = mybir.dt.float32
BF16 = mybir.dt.bfloat16
ALU = mybir.AluOpType
AF = mybir.ActivationFunctionType
AX = mybir.AxisListType

D = 1024
T = 4096
TC = 256
NT = T + TC
FH = 2816
E = 2048
NH = 4
DH = 512
CH = 64
NCH = NT // CH
ALPHA = (2.0 * 2) ** 0.25
EPS = 1e-5
POOL_W = (2, 4, 8, 16)


class Buf:
    __slots__ = ('name', 'w', 'r', 'dsem', 'dtot')

    def __init__(self, name):
        self.name = name
        self.w = None
        self.r = {}
        self.dsem = None
        self.dtot = 0


class K:
    def __init__(self, nc):
        self.nc = nc
        self.engs = {'pe': nc.tensor, 'act': nc.scalar, 'dve': nc.vector, 'pool': nc.gpsimd, 'sp': nc.sync}
        self.sem = {e: nc.alloc_semaphore("s_" + e) for e in ('pe', 'act', 'dve', 'pool')}
        self.cnt = {e: 0 for e in self.sem}
        self.known = {e: {} for e in self.engs}
        self.free_dsems = []
        self.live = []
        self.nsem = 0
        self.noreuse = set()

    def _need(self, eng, dep, waits):
        if dep is None:
            return
        if dep[0] == 'e':
            src, c = dep[1], dep[2]
            if src == eng and eng == 'pe':
                return
            key = ('e', src)
            sem = self.sem[src]
        else:
            sem, c = dep[1], dep[2]
            key = ('d', id(sem))
        if self.known[eng].get(key, 0) >= c:
            return
        if key in waits and waits[key][1] >= c:
            return
        waits[key] = (sem, c)

    def _deps(self, eng, reads, writes):
        waits = {}
        for b in reads:
            self._need(eng, b.w, waits)
        for b in writes:
            self._need(eng, b.w, waits)
            for v in b.r.values():
                self._need(eng, v, waits)
        e = self.engs[eng]
        for key, (sem, c) in waits.items():
            self.known[eng][key] = c
            e.wait_ge(sem, c)

    def op(self, eng, fns, reads=(), writes=()):
        if not isinstance(fns, (list, tuple)):
            fns = [fns]
        self._deps(eng, reads, writes)
        e = self.engs[eng]
        for f in fns[:-1]:
            f(e)
        self.cnt[eng] += 1
        c = self.cnt[eng]
        fns[-1](e).then_inc(self.sem[eng], 1)
        dep = ('e', eng, c)
        for b in writes:
            b.w = dep
            b.r = {}
        for b in reads:
            b.r[('e', eng)] = dep

    def _dsem(self, sb):
        if sb.dsem is None:
            if self.free_dsems:
                sb.dsem, sb.dtot = self.free_dsems.pop()
            else:
                sb.dsem = self.nc.alloc_semaphore("d%d" % self.nsem)
                self.nsem += 1
                sb.dtot = 0
            self.live.append(sb)

    def dma(self, qeng, out_ap, in_ap, sb, reads=(), writes=(), **kw):
        if qeng == 'pool':
            assert sb.dsem is None
            sb.dsem = self.nc.alloc_semaphore("w%d" % self.nsem)
            self.nsem += 1
            sb.dtot = 0
            self.live.append(sb)
            self.noreuse.add(id(sb))
        self._dsem(sb)
        e = self.engs[qeng]
        if sb.dtot > 0:
            key = ('d', id(sb.dsem))
            if self.known[qeng].get(key, 0) < sb.dtot:
                self.known[qeng][key] = sb.dtot
                e.wait_ge(sb.dsem, sb.dtot)
        self._deps(qeng, reads, writes)
        sb.dtot += 16
        e.dma_start(out=out_ap, in_=in_ap, **kw).then_inc(sb.dsem, 16)
        dep = ('d', sb.dsem, sb.dtot)
        for b in writes:
            b.w = dep
            b.r = {}
        for b in reads:
            b.r[('d', id(sb.dsem))] = dep

    def barrier(self, release=()):
        for eng, e in self.engs.items():
            for src, sem in self.sem.items():
                c = self.cnt[src]
                key = ('e', src)
                if c > 0 and self.known[eng].get(key, 0) < c:
                    self.known[eng][key] = c
                    e.wait_ge(sem, c)
            for b in self.live:
                key = ('d', id(b.dsem))
                if b.dtot > 0 and self.known[eng].get(key, 0) < b.dtot:
                    self.known[eng][key] = b.dtot
                    e.wait_ge(b.dsem, b.dtot)
        for b in release:
            if b.dsem is not None:
                self.live.remove(b)
                if id(b) not in self.noreuse:
                    self.free_dsems.append((b.dsem, b.dtot))
                b.dsem = None


class Ctx:
    def __init__(self, nc):
        self.nc = nc
        self.k = K(nc)
        self.uid = 0

    def tile(self, es, shape, dt, name=None, bufs=None):
        self.uid += 1
        nm = "%s_%d" % (name or "t", self.uid)
        t = es.enter_context(self.nc.sbuf_tensor(nm, list(shape), dt))
        b = Buf(nm)
        if bufs is not None:
            bufs.append(b)
        return t, b

    def psum(self, es, shape, dt, name=None):
        self.uid += 1
        nm = "%s_%d" % (name or "ps", self.uid)
        t = es.enter_context(self.nc.psum_tensor(nm, list(shape), dt))
        return t, Buf(nm)


def build(dbg=None):
    dbg = dbg or {}
    nc = bass.Bass("TRN2", target_bir_lowering=False)
    cx = Ctx(nc)
    k = cx.k

    def din(name, shape, dt=F32):
        return nc.dram_tensor(name, list(shape), dt, kind="ExternalInput").ap()

    def dscr(name, shape, dt=F32):
        kind = "ExternalOutput" if name in dbg.get('outs', ()) else "Internal"
        return nc.dram_tensor(name, list(shape), dt, kind=kind).ap()

    x_d = din("x", [T, D])
    ctx_d = din("ctx", [TC, D])
    cvec_d = din("cvec", [128, 8, 2])
    modw_d = din("mod_w", [2, D, 6 * D])
    modbc_d = din("modb_col", [128, 2, 48, 2])
    modbb_d = din("modb_bc", [128, 2, 6 * D])
    lng_d = din("lng_bc", [128, 4, D])
    lnb_d = din("lnb_bc", [128, 4, D])
    ident_d = din("ident", [128, 128])
    ffn_win_d = din("ffn_w_in", [2, D, 2 * FH])
    ffn_wout_d = din("ffn_w_out", [2, FH, D])
    out_d = nc.dram_tensor("out", [T, D], F32, kind="ExternalOutput").ap()
    ab_win_d = din("ab_w_in", [D, 1536])
    ab_wout_d = din("ab_w_out", [D, D])
    ab_pw_d = din("ab_pool_w", [4, 128, 128])
    poolm_d = din("poolm", [128, 31, 512])
    poolc_d = din("poolc", [128, 8, 256])
    invc_d = din("invc", [128, 4, T])
    invcc_d = din("invcc", [128, 4, TC])
    abv_d = din("abvec", [128, 4, 36])

    ml_win_d = din("ml_w_in", [D, 2 * E])
    ml_wq_d = din("ml_wq", [NH, DH, DH])
    ml_wk_d = din("ml_wk", [NH, DH, DH])
    ml_wv_d = din("ml_wv", [NH, DH, DH])
    ml_wout_d = din("ml_w_out", [E, D])
    wg_d = din("wg", [128, 3, 16, 16])
    bg_d = din("bg_bc", [128, 16])
    mlv_d = din("mlvec", [128, 16, 6])
    masks_d = din("masks", [128, 2, 128])
    NTP = NT + 4
    XMT_d = dscr("XMT", [128, 16, NTP])
    P1_d = dscr("P1", [128, 16, T])
    P2_d = dscr("P2", [128, 16, T])
    QT_d = dscr("QT", [NT // 128, 128, 16 * 128], BF16)
    KT_d = dscr("KT", [NT // 128, 128, 16 * 128], BF16)
    Ktm_d = dscr("Ktm", [NT, E], BF16)
    Vtm_d = dscr("Vtm", [NT, E])
    G_d = dscr("G", [NT, 16])
    Gb_d = dscr("Gb", [NT, 16])
    HF_d = dscr("HF", [T, E])
    HB_d = dscr("HB", [T, E])
    H3_d = dscr("H3", [T, D])
    gbc_d = dscr("gbc", [6, 128, D])
    H1_d = din("H1", [NT, D]) if dbg.get('h1_in') else dscr("H1", [NT, D])
    H2_d = din("H2", [NT, D]) if dbg.get('h2_in') else dscr("H2", [NT, D])

    dram_bufs = {}

    def DB(name):
        if name not in dram_bufs:
            dram_bufs[name] = Buf(name)
        return dram_bufs[name]

    with ExitStack() as top:
        identF, B_identF = cx.tile(top, [128, 128], F32, "identF")
        identB, B_identB = cx.tile(top, [128, 128], BF16, "identB")
        modT, B_modT = cx.tile(top, [128, 2, 48, 2], F32, "modT")
        epsT, B_eps = cx.tile(top, [128, 1], F32, "eps")
        k.dma('sp', identF[:], ident_d[:, :], B_identF, writes=[B_identF])
        k.op('dve', lambda e: e.tensor_copy(out=identB[:], in_=identF[:]), reads=[B_identF], writes=[B_identB])
        k.op('pool', lambda e: e.memset(epsT[:], EPS), writes=[B_eps])

        def phase_mods():
            with ExitStack() as es:
                loc = []
                cv, B_cv = cx.tile(es, [128, 8, 2], F32, "cv", loc)
                scv, B_scv = cx.tile(es, [128, 8, 2], F32, "scv", loc)
                scb = [cx.tile(es, [128, 8, 128], F32, "scb", loc) for _ in range(2)]
                mbc, B_mbc = cx.tile(es, [128, 2, 48, 2], F32, "mbc", loc)
                wt = [cx.tile(es, [128, 8, 512], F32, "wt", loc) for _ in range(4)]
                mbb = [cx.tile(es, [128, 512], F32, "mbb", loc) for _ in range(2)]
                gts = {}
                pcs = [cx.psum(es, [128, 512], F32, "pc") for _ in range(2)]
                pgs = [cx.psum(es, [128, 512], F32, "pg") for _ in range(2)]
                k.dma('sp', cv[:], cvec_d[:, :, :], B_cv, writes=[B_cv])
                k.dma('sp', mbc[:], modbc_d[:, :, :, :], B_mbc, writes=[B_mbc])
                k.op('act', lambda e: e.activation(out=scv[:], in_=cv[:], func=AF.Silu), reads=[B_cv], writes=[B_scv])
                for r in range(2):
                    for kc in range(8):
                        k.op('dve', lambda e: e.tensor_copy(out=scb[r][0][:, kc, :],
                                                           in_=scv[:, kc, r:r + 1].to_broadcast([128, 128])),
                             reads=[B_scv], writes=[scb[r][1]])
                it = 0
                ig = 0

                def mods_load(q):
                    lq, jq = q // 12, q % 12
                    wvq = modw_d[lq].rearrange("(kc p) n -> p kc n", p=128)
                    k.dma('sp', wt[q % 4][0][:], wvq[:, :, jq * 512:(jq + 1) * 512], wt[q % 4][1], writes=[wt[q % 4][1]])

                for q in range(3):
                    mods_load(q)
                for l in range(2):
                    for jb in range(12):
                        w, B_w = wt[it % 4]
                        pc, B_pc = pcs[it % 2]
                        if it + 3 < 24:
                            mods_load(it + 3)
                        it += 1
                        fns = []
                        for jj in range(4):
                            for kc in range(8):
                                fns.append(lambda e, jj=jj, kc=kc: e.matmul(
                                    pc[:, jj * 2:(jj + 1) * 2], lhsT=w[:, kc, jj * 128:(jj + 1) * 128],
                                    rhs=scv[:, kc, 0:2], start=(kc == 0), stop=(kc == 7)))
                        k.op('pe', fns, reads=[B_w, B_scv], writes=[B_pc])
                        k.op('dve', lambda e: e.tensor_tensor(
                            out=modT[:, l, jb * 4:(jb + 1) * 4, :],
                            in0=pc[:, 0:8].rearrange("p (a b) -> p a b", b=2),
                            in1=mbc[:, l, jb * 4:(jb + 1) * 4, :], op=ALU.add),
                            reads=[B_mbc], writes=[B_pc, B_modT])
                        ch = jb // 2
                        if ch in (2, 5):
                            for r in range(2):
                                if r == 1 and l == 1:
                                    continue
                                key = (l, ch, r)
                                if key not in gts:
                                    gts[key] = cx.tile(es, [128, D], F32, "gt", loc)
                                gt, B_gt = gts[key]
                                pg, B_pg = pgs[ig % 2]
                                mb, B_mb = mbb[ig % 2]
                                ig += 1
                                k.dma('sp', mb[:], modbb_d[:, l, jb * 512:(jb + 1) * 512], B_mb, writes=[B_mb])
                                fns = [lambda e, kc=kc: e.matmul(pg[:], lhsT=scb[r][0][:, kc, :], rhs=w[:, kc, :],
                                                                 start=(kc == 0), stop=(kc == 7)) for kc in range(8)]
                                k.op('pe', fns, reads=[B_w, scb[r][1]], writes=[B_pg])
                                hf = jb % 2
                                k.op('dve', lambda e: e.tensor_tensor(out=gt[:, hf * 512:(hf + 1) * 512], in0=pg[:],
                                                                     in1=mb[:], op=ALU.add),
                                     reads=[B_mb], writes=[B_pg, B_gt])
                                if hf == 1:
                                    idx = {(0, 2, 0): 0, (0, 5, 0): 1, (0, 2, 1): 2, (0, 5, 1): 3,
                                           (1, 2, 0): 4, (1, 5, 0): 5}[key]
                                    k.dma('sp', gbc_d[idx], gt[:], B_gt, reads=[B_gt], writes=[DB("gbc%d" % idx)])
                    for c0 in (8, 32):
                        k.op('dve', lambda e: e.tensor_scalar(out=modT[:, l, c0:c0 + 8, :], in0=modT[:, l, c0:c0 + 8, :],
                                                             scalar1=1.0, scalar2=None, op0=ALU.add),
                             reads=[B_modT], writes=[B_modT])
                k.barrier(release=loc)

        def post_norm_g(psA, B_psA, psB, B_psB, res, B_res, gate, B_gate, lng, B_lng, lnb, B_lnb,
                        t1, B_t1, sm, B_sm, out_ap, out_buf):
            k.op('dve', lambda e: e.tensor_tensor(out=t1[:, 0:512], in0=psA, in1=gate[:, 0:512], op=ALU.mult),
                 reads=[B_gate], writes=[B_psA, B_t1])
            k.op('dve', lambda e: e.tensor_tensor(out=t1[:, 512:1024], in0=psB, in1=gate[:, 512:1024], op=ALU.mult),
                 reads=[B_gate], writes=[B_psB, B_t1])
            yield
            k.op('dve', lambda e: e.scalar_tensor_tensor(out=t1[:], in0=res, scalar=ALPHA, in1=t1[:],
                                                        op0=ALU.mult, op1=ALU.add),
                 reads=[B_res], writes=[B_t1])
            yield
            k.op('dve', [lambda e: e.bn_stats(out=sm[:, 0:6], in_=t1[:, 0:512]),
                         lambda e: e.bn_stats(out=sm[:, 6:12], in_=t1[:, 512:1024])],
                 reads=[B_t1], writes=[B_sm])
            yield
            k.op('dve', lambda e: e.bn_aggr(out=sm[:, 12:14], in_=sm[:, 0:12]), reads=[B_sm], writes=[B_sm])
            yield
            k.op('act', lambda e: e.activation(out=sm[:, 14:15], in_=sm[:, 13:14], func=AF.Sqrt, bias=epsT[:, 0:1],
                                               scale=1.0), reads=[B_sm, B_eps], writes=[B_sm])
            yield
            k.op('dve', lambda e: e.reciprocal(out=sm[:, 14:15], in_=sm[:, 14:15]), reads=[B_sm], writes=[B_sm])
            yield
            k.op('dve', lambda e: e.scalar_tensor_tensor(out=sm[:, 15:16], in0=sm[:, 12:13], scalar=-1.0,
                                                        in1=sm[:, 14:15], op0=ALU.mult, op1=ALU.mult),
                 reads=[B_sm], writes=[B_sm])
            yield
            k.op('act', lambda e: e.activation(out=t1[:], in_=t1[:], func=AF.Identity, scale=sm[:, 14:15],
                                               bias=sm[:, 15:16]), reads=[B_sm, B_t1], writes=[B_t1])
            yield
            k.op('dve', lambda e: e.tensor_tensor(out=t1[:], in0=t1[:], in1=lng, op=ALU.mult),
                 reads=[B_lng], writes=[B_t1])
            yield
            k.op('dve', lambda e: e.tensor_tensor(out=t1[:], in0=t1[:], in1=lnb, op=ALU.add),
                 reads=[B_lnb], writes=[B_t1])
            yield
            k.dma('sp', out_ap, t1[:], B_t1, reads=[B_t1], writes=[out_buf])

        def post_norm(*args):
            for _ in post_norm_g(*args):
                pass

        def phase_ffn(l, Hin, Hin_name, Hout, Hout_name, groups):
            NTK = 256
            with ExitStack() as es:
                loc = []
                fwin = [cx.tile(es, [128, 8, 1408], BF16, "fwin", loc) for _ in range(4)]
                fwout = [cx.tile(es, [128, 11, D], BF16, "fwout", loc) for _ in range(2)]
                wi = ffn_win_d[l].rearrange("(kc p) n -> p kc n", p=128)
                wo = ffn_wout_d[l].rearrange("(f p) n -> p f n", p=128)
                for pi in (0, 2, 1, 3):
                    k.dma('pool', fwin[pi][0][:], wi[:, :, pi * 1408:(pi + 1) * 1408], fwin[pi][1],
                          writes=[fwin[pi][1]])
                for pi in range(2):
                    k.dma('pool', fwout[pi][0][:], wo[:, pi * 11:(pi + 1) * 11, :], fwout[pi][1],
                          writes=[fwout[pi][1]])
                gates = {}
                for r in sorted(set(g[2] for g in groups)):
                    gt, B_gt = cx.tile(es, [128, D], F32, "gf", loc)
                    idx = {(0, 0): 1, (0, 1): 3, (1, 0): 5}[(l, r)]
                    k.dma('sp', gt[:], gbc_d[idx], B_gt, reads=[DB("gbc%d" % idx)], writes=[B_gt])
                    gates[r] = (gt, B_gt)
                lng, B_lng = cx.tile(es, [128, D], F32, "lng", loc)
                lnb, B_lnb = cx.tile(es, [128, D], F32, "lnb", loc)
                k.dma('sp', lng[:], lng_d[:, l * 2 + 1, :], B_lng, writes=[B_lng])
                k.dma('sp', lnb[:], lnb_d[:, l * 2 + 1, :], B_lnb, writes=[B_lnb])
                xts = [cx.tile(es, [128, D], F32, "xt", loc) for _ in range(4)]
                t1s = [cx.tile(es, [128, D], F32, "t1", loc) for _ in range(2)]
                sms = [cx.tile(es, [128, 16], F32, "sm", loc) for _ in range(2)]
                uTs = [cx.tile(es, [128, 8, NTK], BF16, "uT", loc) for _ in range(2)]
                actT, B_actT = cx.tile(es, [128, 22, NTK], BF16, "actT", loc)
                sAs = [cx.tile(es, [128, NTK], F32, "sA", loc) for _ in range(4)]
                ptA = cx.psum(es, [128, 512], F32, "ptA")
                ptB = cx.psum(es, [128, 512], F32, "ptB")
                pabs = [cx.psum(es, [128, 512], F32, "pab") for _ in range(4)]
                pos = [cx.psum(es, [128, 512], F32, "po") for _ in range(2)]
                ix = 0
                ip = 0
                def ffn_loads(gq):
                    rin_ = groups[gq][0]
                    for s in range(2):
                        xt, B_xt = xts[(2 * gq + s) % 4]
                        k.dma('sp', xt[:], Hin[rin_ + s * 128: rin_ + (s + 1) * 128, :], B_xt,
                              reads=[DB(Hin_name)], writes=[B_xt])

                cnf = dict(ip=0)

                def ffn_group(gi):
                    rin, rout, r = groups[gi]
                    uT, B_uT = uTs[gi % 2]
                    ffn_loads(gi)
                    yield
                    xs = []
                    for s in range(2):
                        xt, B_xt = xts[(2 * gi + s) % 4]
                        xs.append((xt, B_xt))
                        for (pt, B_pt), k0 in ((ptA, 0), (ptB, 4)):
                            k.op('pe', [lambda e, kk=kk: e.transpose(out=pt[:, kk * 128:(kk + 1) * 128],
                                                                    in_=xt[:, (k0 + kk) * 128:(k0 + kk + 1) * 128],
                                                                    identity=identF[:]) for kk in range(4)],
                                 reads=[B_xt, B_identF], writes=[B_pt])
                            yield
                            for kk in range(4):
                                kc = k0 + kk
                                k.op('act', lambda e: e.activation(
                                    out=uT[:, kc, s * 128:(s + 1) * 128], in_=pt[:, kk * 128:(kk + 1) * 128],
                                    func=AF.Identity, scale=modT[:, l, 32 + kc, r:r + 1],
                                    bias=modT[:, l, 24 + kc, r:r + 1]),
                                    reads=[B_modT], writes=[B_pt, B_uT])
                                yield
                    while gi > 0 and not cnf.get(('out', gi - 1)):
                        yield
                    for f in range(22):
                        ip = cnf['ip']
                        pab, B_pa = pabs[ip % 4]
                        B_pg = B_pa
                        pa = pab[:, 0:256]
                        pg = pab[:, 256:512]
                        sA, B_sA = sAs[ip % 4]
                        cnf['ip'] += 1
                        pi = f // 11
                        c0 = (f % 11) * 128
                        wa, B_wa = fwin[pi]
                        wg, B_wg = fwin[2 + pi]
                        k.op('pe', [lambda e, kc=kc: e.matmul(pa[:, 0:NTK], lhsT=wa[:, kc, c0:c0 + 128],
                                                              rhs=uT[:, kc, :], start=(kc == 0), stop=(kc == 7))
                                    for kc in range(8)] +
                                   [lambda e, kc=kc: e.matmul(pg[:, 0:NTK], lhsT=wg[:, kc, c0:c0 + 128],
                                                              rhs=uT[:, kc, :], start=(kc == 0), stop=(kc == 7))
                                    for kc in range(8)], reads=[B_wa, B_wg, B_uT], writes=[B_pa])
                        yield
                        k.op('act', lambda e: e.activation(out=sA[:], in_=pa[:, 0:NTK], func=AF.Silu),
                             writes=[B_pa, B_sA])
                        yield
                        k.op('dve', lambda e: e.tensor_tensor(out=actT[:, f, :], in0=pg[:, 0:NTK], in1=sA[:],
                                                             op=ALU.mult), reads=[B_sA], writes=[B_pg, B_actT])
                        yield
                    for s in range(2):
                        for hf in range(2):
                            po, B_po = pos[hf]
                            k.op('pe', [lambda e, f=f: e.matmul(po[:], lhsT=actT[:, f, s * 128:(s + 1) * 128],
                                                                rhs=fwout[f // 11][0][:, f % 11, hf * 512:(hf + 1) * 512],
                                                                start=(f == 0), stop=(f == 21)) for f in range(22)],
                                 reads=[B_actT, fwout[0][1], fwout[1][1]], writes=[B_po])
                            yield
                        if s == 1:
                            cnf[('out', gi)] = True
                        t1, B_t1 = t1s[s]
                        sm, B_sm = sms[s]
                        gt, B_gt = gates[r]
                        yield from post_norm_g(pos[0][0][:], pos[0][1], pos[1][0][:], pos[1][1], xs[s][0][:], xs[s][1],
                                               gt, B_gt, lng[:], B_lng, lnb[:], B_lnb, t1, B_t1, sm, B_sm,
                                               Hout[rout + s * 128: rout + (s + 1) * 128, :], DB(Hout_name))

                run_pipelined(ffn_group, len(groups), depth=2, skew=70)
                k.barrier(release=loc)

        def xrows(n0, nrows):
            if n0 < TC:
                return ctx_d[n0:n0 + nrows, :]
            return x_d[n0 - TC:n0 - TC + nrows, :]

        def make_uT_g(l, r, jsh, jsc, src_ap, B_src_dram, xt, B_xt, ptA, ptB, uT, B_uT, s):
            if src_ap is not None:
                k.dma('sp', xt[:], src_ap, B_xt, reads=B_src_dram, writes=[B_xt])
            for (pt, B_pt), k0 in ((ptA, 0), (ptB, 4)):
                k.op('pe', [lambda e, kk=kk: e.transpose(out=pt[:, kk * 128:(kk + 1) * 128],
                                                        in_=xt[:, (k0 + kk) * 128:(k0 + kk + 1) * 128],
                                                        identity=identF[:]) for kk in range(4)],
                     reads=[B_xt, B_identF], writes=[B_pt])
                yield
                for kk in range(4):
                    kc = k0 + kk
                    k.op('act', lambda e: e.activation(
                        out=uT[:, kc, s * 128:(s + 1) * 128], in_=pt[:, kk * 128:(kk + 1) * 128],
                        func=AF.Identity, scale=modT[:, l, jsc + kc, r:r + 1], bias=modT[:, l, jsh + kc, r:r + 1]),
                        reads=[B_modT], writes=[B_pt, B_uT])
                    yield

        def make_uT(*args):
            for _ in make_uT_g(*args):
                pass

        L0_GROUPS = [(0, 2, 1)] + [(TC + g * 512, 4, 0) for g in range(8)]
        POOL_D = {0: list(range(-1, 4)), 1: list(range(-1, 5)), 2: list(range(-2, 6)), 3: list(range(-4, 8))}
        POOL_IDX = {}
        for gi in range(4):
            for dl in POOL_D[gi]:
                POOL_IDX[(gi, dl)] = len(POOL_IDX)

        def layer0_mixer(groups):
            with ExitStack() as l0:
                keep = []
                A_tm, B_A = cx.tile(l0, [128, NT // 128, 512], BF16, "A_tm", keep)
                YB, B_YB = cx.tile(l0, [128, 4, NT], BF16, "YB", keep)
                abv, B_abv = cx.tile(l0, [128, 4, 36], F32, "abv", keep)
                k.dma('sp', abv[:], abv_d[:, :, :], B_abv, writes=[B_abv])
                lg = ExitStack()
                keepg = []
                with lg:
                    GLU, B_GLU = cx.tile(lg, [128, 4, T + 30], BF16, "GLU", keepg)
                    GLUc, B_GLUc = cx.tile(lg, [128, 4, TC + 30], BF16, "GLUc", keepg)
                    k.op('pool', lambda e: e.memset(GLU[:], 0.0), writes=[B_GLU])
                    k.op('pool', lambda e: e.memset(GLUc[:], 0.0), writes=[B_GLUc])
                    with ExitStack() as es:
                        loc = []
                        win, B_win = cx.tile(es, [128, 8, 1536], BF16, "win", loc)
                        k.dma('pool', win[:], ab_win_d.rearrange("(kc p) n -> p kc n", p=128), B_win, writes=[B_win])
                        xts = [cx.tile(es, [128, D], F32, "xt", loc) for _ in range(8)]
                        uTs = [cx.tile(es, [128, 8, 512], BF16, "uT", loc) for _ in range(2)]
                        sgs = [cx.tile(es, [128, 512], F32, "sg", loc) for _ in range(2)]

                        def a0_loads(gq):
                            nq, nsq, _ = groups[gq]
                            for sq in range(nsq):
                                k.dma('sp', xts[(gq % 2) * 4 + sq][0][:], xrows(nq + sq * 128, 128), xts[(gq % 2) * 4 + sq][1],
                                      writes=[xts[(gq % 2) * 4 + sq][1]])

                        ptA = cx.psum(es, [128, 512], F32, "ptA")
                        ptB = cx.psum(es, [128, 512], F32, "ptB")
                        pas = [cx.psum(es, [128, 512], F32, "pa") for _ in range(2)]
                        pvs = [cx.psum(es, [128, 512], F32, "pv") for _ in range(2)]
                        pgs = [cx.psum(es, [128, 512], F32, "pg") for _ in range(2)]
                        cna = dict(ia=0, iv=0)

                        def a0_group(gidx):
                            n0, nsub, r = groups[gidx]
                            ntok = nsub * 128
                            uT, B_uT = uTs[gidx % 2]
                            a0_loads(gidx)
                            yield
                            yield from acquire(cna, 'st1')
                            for s in range(nsub):
                                xt, B_xt = xts[(gidx % 2) * 4 + s]
                                yield from make_uT_g(0, r, 0, 8, None, [], xt, B_xt, ptA, ptB, uT, B_uT, s)
                            cna['st1'] = False
                            yield from acquire(cna, 'st2')
                            for s in range(nsub):
                                pa, B_pa = pas[cna['ia'] % 2]
                                cna['ia'] += 1
                                k.op('pe', [lambda e, kc=kc: e.matmul(pa[:], lhsT=uT[:, kc, s * 128:(s + 1) * 128],
                                                                      rhs=win[:, kc, 0:512], start=(kc == 0), stop=(kc == 7))
                                            for kc in range(8)], reads=[B_uT, B_win], writes=[B_pa])
                                yield
                                ti = n0 // 128 + s
                                k.op('act', lambda e: e.copy(out=A_tm[:, ti, :], in_=pa[:]), writes=[B_pa, B_A])
                                yield
                            Gt, B_Gt, goff = (GLUc, B_GLUc, 15 + n0) if r == 1 else (GLU, B_GLU, 15 + n0 - TC)
                            for jc in range(4):
                                iv = cna['iv']
                                pv, B_pv = pvs[iv % 2]
                                pg, B_pg = pgs[iv % 2]
                                sg, B_sg = sgs[iv % 2]
                                cna['iv'] += 1
                                k.op('pe', [lambda e, kc=kc: e.matmul(pv[:, 0:ntok], lhsT=win[:, kc, 512 + jc * 128:512 + (jc + 1) * 128],
                                                                      rhs=uT[:, kc, 0:ntok], start=(kc == 0), stop=(kc == 7))
                                            for kc in range(8)], reads=[B_uT, B_win], writes=[B_pv])
                                k.op('pe', [lambda e, kc=kc: e.matmul(pg[:, 0:ntok], lhsT=win[:, kc, 1024 + jc * 128:1024 + (jc + 1) * 128],
                                                                      rhs=uT[:, kc, 0:ntok], start=(kc == 0), stop=(kc == 7))
                                            for kc in range(8)], reads=[B_uT, B_win], writes=[B_pg])
                                yield
                                k.op('act', lambda e: e.activation(out=sg[:, 0:ntok], in_=pg[:, 0:ntok], func=AF.Sigmoid),
                                     writes=[B_pg, B_sg])
                                yield
                                k.op('dve', lambda e: e.tensor_tensor(out=Gt[:, jc, goff:goff + ntok], in0=pv[:, 0:ntok],
                                                                     in1=sg[:, 0:ntok], op=ALU.mult),
                                     reads=[B_sg], writes=[B_pv, B_Gt])
                                yield
                            cna['st2'] = False

                        run_pipelined(a0_group, len(groups), depth=2, skew=1)
                        k.barrier(release=loc)
                    with ExitStack() as es:
                        loc = []
                        DG, B_DG = cx.tile(es, [128, 4 * 31, 128], BF16, "DG", loc)
                        for jc in range(4):
                            k.op('dve', lambda e: e.tensor_tensor(
                                out=DG[:, jc * 31:(jc + 1) * 31, :], in0=identF[:].unsqueeze(1).to_broadcast([128, 31, 128]),
                                in1=abv[:, jc, 0:31].unsqueeze(2).to_broadcast([128, 31, 128]),
                                op=ALU.mult), reads=[B_identF, B_abv], writes=[B_DG])
                        onesF, B_ones = cx.tile(es, [128, 128], F32, "ones", loc)
                        k.op('pool', lambda e: e.memset(onesF[:], 1.0), writes=[B_ones])
                        CVs = [cx.tile(es, [128, 4, 512], F32, "CV", loc) for _ in range(2)]
                        SQs = [cx.tile(es, [128, 4, 512], F32, "SQ", loc) for _ in range(2)]
                        MEs = [cx.tile(es, [128, 512], F32, "ME", loc) for _ in range(2)]
                        M2s = [cx.tile(es, [128, 512], F32, "M2", loc) for _ in range(2)]
                        VARs = [cx.tile(es, [128, 512], F32, "VAR", loc) for _ in range(2)]
                        xcs = [cx.tile(es, [128, 512], F32, "xc", loc) for _ in range(2)]
                        pcvs = [cx.psum(es, [128, 512], F32, "pcv") for _ in range(2)]
                        pS = cx.psum(es, [128, 512], F32, "pS")
                        pSS = cx.psum(es, [128, 512], F32, "pSS")
                        cnb = dict(ic=0)

                        def b0a_group(gidx):
                            n0, nsub, r = groups[gidx]
                            ntok = nsub * 128
                            CV, B_CV = CVs[gidx % 2]
                            SQ, B_SQ = SQs[gidx % 2]
                            ME, B_ME = MEs[gidx % 2]
                            M2, B_M2 = M2s[gidx % 2]
                            VAR, B_VAR = VARs[gidx % 2]
                            Gt, B_Gt, goff = (GLUc, B_GLUc, n0) if r == 1 else (GLU, B_GLU, n0 - TC)
                            yield from acquire(cnb, 'st1')
                            for jc in range(4):
                                pcv, B_pcv = pcvs[cnb['ic'] % 2]
                                cnb['ic'] += 1
                                k.op('pe', [lambda e, kk=kk: e.matmul(pcv[:, 0:ntok], lhsT=DG[:, jc * 31 + kk, :],
                                                                      rhs=Gt[:, jc, goff + kk:goff + kk + ntok],
                                                                      start=(kk == 0), stop=(kk == 30)) for kk in range(31)],
                                     reads=[B_DG, B_Gt], writes=[B_pcv])
                                yield
                                k.op('act', lambda e: e.activation(out=CV[:, jc, 0:ntok], in_=pcv[:, 0:ntok], func=AF.Identity,
                                                                   bias=abv[:, jc, 31:32], scale=1.0),
                                     reads=[B_abv], writes=[B_pcv, B_CV])
                                yield
                                k.op('dve', lambda e: e.tensor_tensor(out=SQ[:, jc, 0:ntok], in0=CV[:, jc, 0:ntok],
                                                                      in1=CV[:, jc, 0:ntok], op=ALU.mult),
                                     reads=[B_CV], writes=[B_SQ])
                                yield
                            cnb['st1'] = False
                            yield from acquire(cnb, 'st2')
                            k.op('pe', [lambda e, jc=jc: e.matmul(pS[0][:, 0:ntok], lhsT=onesF[:], rhs=CV[:, jc, 0:ntok],
                                                                  start=(jc == 0), stop=(jc == 3)) for jc in range(4)],
                                 reads=[B_ones, B_CV], writes=[pS[1]])
                            k.op('pe', [lambda e, jc=jc: e.matmul(pSS[0][:, 0:ntok], lhsT=onesF[:], rhs=SQ[:, jc, 0:ntok],
                                                                  start=(jc == 0), stop=(jc == 3)) for jc in range(4)],
                                 reads=[B_ones, B_SQ], writes=[pSS[1]])
                            yield
                            k.op('dve', lambda e: e.tensor_scalar(out=ME[:, 0:ntok], in0=pS[0][:, 0:ntok], scalar1=1.0 / 512,
                                                                 scalar2=None, op0=ALU.mult), writes=[pS[1], B_ME])
                            yield
                            k.op('dve', lambda e: e.tensor_tensor(out=M2[:, 0:ntok], in0=ME[:, 0:ntok], in1=ME[:, 0:ntok],
                                                                  op=ALU.mult), reads=[B_ME], writes=[B_M2])
                            yield
                            k.op('dve', lambda e: e.scalar_tensor_tensor(out=VAR[:, 0:ntok], in0=pSS[0][:, 0:ntok],
                                                                        scalar=1.0 / 512, in1=M2[:, 0:ntok],
                                                                        op0=ALU.mult, op1=ALU.subtract),
                                 reads=[B_M2], writes=[pSS[1], B_VAR])
                            yield
                            k.op('act', lambda e: e.activation(out=VAR[:, 0:ntok], in_=VAR[:, 0:ntok], func=AF.Sqrt,
                                                               bias=epsT[:, 0:1], scale=1.0), reads=[B_eps], writes=[B_VAR])
                            yield
                            k.op('dve', lambda e: e.reciprocal(out=VAR[:, 0:ntok], in_=VAR[:, 0:ntok]), writes=[B_VAR])
                            yield
                            for jc in range(4):
                                xc, B_xc = xcs[jc % 2]
                                k.op('dve', lambda e: e.tensor_tensor(out=xc[:, 0:ntok], in0=CV[:, jc, 0:ntok], in1=ME[:, 0:ntok],
                                                                     op=ALU.subtract), reads=[B_CV, B_ME], writes=[B_xc])
                                yield
                                k.op('dve', lambda e: e.tensor_tensor(out=xc[:, 0:ntok], in0=xc[:, 0:ntok], in1=VAR[:, 0:ntok],
                                                                      op=ALU.mult), reads=[B_VAR], writes=[B_xc])
                                yield
                                k.op('act', lambda e: e.activation(out=YB[:, jc, n0:n0 + ntok], in_=xc[:, 0:ntok], func=AF.Silu,
                                                                   scale=abv[:, jc, 32:33], bias=abv[:, jc, 33:34]),
                                     reads=[B_xc, B_abv], writes=[B_YB])
                                yield
                            cnb['st2'] = False

                        run_pipelined(b0a_group, len(groups), depth=2, skew=1)
                        k.barrier(release=loc)
                    k.barrier(release=keepg)
                with ExitStack() as es:
                    loc = []
                    poolm, B_poolm = cx.tile(es, [128, 31, 512], BF16, "poolm", loc)
                    poolc, B_poolc = cx.tile(es, [128, 8, 256], BF16, "poolc", loc)
                    pw, B_pw = cx.tile(es, [128, 4, 128], BF16, "pw", loc)
                    wout, B_wout = cx.tile(es, [128, 8, D], BF16, "wout", loc)
                    k.dma('pool', poolm[:], poolm_d[:, :, :], B_poolm, writes=[B_poolm])
                    k.dma('pool', poolc[:], poolc_d[:, :, :], B_poolc, writes=[B_poolc])
                    k.dma('pool', pw[:], ab_pw_d.rearrange("g c d -> c g d"), B_pw, writes=[B_pw])
                    k.dma('pool', wout[:], ab_wout_d.rearrange("(kc p) n -> p kc n", p=128), B_wout, writes=[B_wout])
                    gates = {}
                    for r in sorted(set(g[2] for g in groups)):
                        gt, B_gt = cx.tile(es, [128, D], F32, "gm", loc)
                        idx = {0: 0, 1: 2}[r]
                        k.dma('sp', gt[:], gbc_d[idx], B_gt, reads=[DB("gbc%d" % idx)], writes=[B_gt])
                        gates[r] = (gt, B_gt)
                    lng, B_lng = cx.tile(es, [128, D], F32, "lng", loc)
                    lnb, B_lnb = cx.tile(es, [128, D], F32, "lnb", loc)
                    k.dma('sp', lng[:], lng_d[:, 0, :], B_lng, writes=[B_lng])
                    k.dma('sp', lnb[:], lnb_d[:, 0, :], B_lnb, writes=[B_lnb])
                    invs = [cx.tile(es, [128, 4, 512], F32, "inv", loc) for _ in range(2)]
                    YAs = [cx.tile(es, [128, 4, 512], BF16, "YA", loc) for _ in range(2)]
                    dTs = [cx.tile(es, [128, 4, 512], BF16, "dT", loc) for _ in range(2)]
                    tqs = [cx.tile(es, [128, 512], F32, "tq", loc) for _ in range(2)]
                    sgs = [cx.tile(es, [128, 512], F32, "sg", loc) for _ in range(2)]
                    xts = [cx.tile(es, [128, D], F32, "xt", loc) for _ in range(2)]
                    t1s = [cx.tile(es, [128, D], F32, "t1", loc) for _ in range(2)]
                    sms = [cx.tile(es, [128, 16], F32, "sm", loc) for _ in range(2)]
                    pbs = [cx.psum(es, [128, 512], F32, "pb") for _ in range(2)]
                    psgs = [cx.psum(es, [128, 512], F32, "psg") for _ in range(2)]
                    pws = [cx.psum(es, [128, 512], F32, "pw") for _ in range(2)]
                    pos = [cx.psum(es, [128, 512], F32, "po") for _ in range(2)]
                    ib = 0
                    ix = 0
                    def b0b_loads(gq):
                        nq, nsq, rq = groups[gq]
                        inv_, B_inv_ = invs[gq % 2]
                        if rq == 1:
                            k.dma('sp', inv_[:, :, 0:nsq * 128], invcc_d[:, :, :], B_inv_, writes=[B_inv_])
                        else:
                            k.dma('sp', inv_[:, :, 0:nsq * 128], invc_d[:, :, nq - TC:nq - TC + nsq * 128], B_inv_, writes=[B_inv_])

                    cnp = dict(ib=0, ix=0)

                    def b0b_group(gidx):
                        n0, nsub, r = groups[gidx]
                        ntok = nsub * 128
                        inv, B_inv = invs[gidx % 2]
                        YA, B_YA = YAs[gidx % 2]
                        dT, B_dT = dTs[gidx % 2]
                        b0b_loads(gidx)
                        yield
                        yield from acquire(cnp, 'st1')
                        for gi in range(4):
                            ib = cnp['ib']
                            pb, B_pb = pbs[ib % 2]
                            psg, B_psg = psgs[ib % 2]
                            pwp, B_pwp = pws[ib % 2]
                            tq, B_tq = tqs[ib % 2]
                            sg, B_sg = sgs[ib % 2]
                            cnp['ib'] += 1
                            if r == 1:
                                srcs = [(st, poolc[:, gi * 2 + st, 0:ntok]) for st in range(2)]
                                Bm = B_poolc
                            else:
                                g = (n0 - TC) // 512
                                srcs = []
                                for dl in POOL_D[gi]:
                                    st = 4 * g + dl
                                    if 0 <= st < 32:
                                        srcs.append((2 + st, poolm[:, POOL_IDX[(gi, dl)], 0:ntok]))
                                Bm = B_poolm
                            k.op('pe', [lambda e, i=i: e.matmul(pb[:, 0:ntok], lhsT=A_tm[:, srcs[i][0], gi * 128:(gi + 1) * 128],
                                                                rhs=srcs[i][1], start=(i == 0), stop=(i == len(srcs) - 1))
                                        for i in range(len(srcs))], reads=[B_A, Bm], writes=[B_pb])
                            k.op('pe', [lambda e, s=s: e.matmul(psg[:, s * 128:(s + 1) * 128],
                                                                lhsT=A_tm[:, n0 // 128 + s, gi * 128:(gi + 1) * 128],
                                                                rhs=identB[:], start=True, stop=True) for s in range(nsub)],
                                 reads=[B_A, B_identB], writes=[B_psg])
                            yield
                            k.op('act', lambda e: e.activation(out=sg[:, 0:ntok], in_=psg[:, 0:ntok], func=AF.Copy, scale=-1.0),
                                 writes=[B_psg, B_sg])
                            k.op('dve', lambda e: e.tensor_tensor(out=tq[:, 0:ntok], in0=pb[:, 0:ntok], in1=inv[:, gi, 0:ntok],
                                                                 op=ALU.mult), reads=[B_inv], writes=[B_pb, B_tq])
                            yield
                            k.op('dve', lambda e: e.tensor_tensor(out=dT[:, gi, 0:ntok], in0=tq[:, 0:ntok], in1=sg[:, 0:ntok],
                                                                  op=ALU.add), reads=[B_tq, B_sg], writes=[B_dT])
                            yield
                            k.op('pe', lambda e: e.matmul(pwp[:, 0:ntok], lhsT=pw[:, gi, :], rhs=dT[:, gi, 0:ntok],
                                                          start=True, stop=True), reads=[B_pw, B_dT], writes=[B_pwp])
                            yield
                            k.op('dve', lambda e: e.tensor_scalar(out=YA[:, gi, 0:ntok], in0=pwp[:, 0:ntok],
                                                                 scalar1=abv[:, gi, 34:35], scalar2=abv[:, gi, 35:36],
                                                                 op0=ALU.add, op1=ALU.mult),
                                 reads=[B_abv], writes=[B_pwp, B_YA])
                            yield
                        cnp['st1'] = False
                        yield from acquire(cnp, 'st2')
                        for s in range(nsub):
                            ix = cnp['ix']
                            xt, B_xt = xts[ix % 2]
                            t1, B_t1 = t1s[ix % 2]
                            sm, B_sm = sms[ix % 2]
                            cnp['ix'] += 1
                            k.dma('sp', xt[:], xrows(n0 + s * 128, 128), B_xt, writes=[B_xt])
                            for hf in range(2):
                                po, B_po = pos[hf]
                                fns = []
                                for kc in range(8):
                                    lh = YA[:, kc, s * 128:(s + 1) * 128] if kc < 4 else \
                                        YB[:, kc - 4, n0 + s * 128:n0 + (s + 1) * 128]
                                    fns.append(lambda e, kc=kc, lh=lh: e.matmul(po[:], lhsT=lh,
                                                                               rhs=wout[:, kc, hf * 512:(hf + 1) * 512],
                                                                               start=(kc == 0), stop=(kc == 7)))
                                k.op('pe', fns, reads=[B_YA, B_YB, B_wout], writes=[B_po])
                                yield
                            gt, B_gt = gates[r]
                            yield from post_norm_g(pos[0][0][:], pos[0][1], pos[1][0][:], pos[1][1], xt[:], B_xt,
                                                   gt, B_gt, lng[:], B_lng, lnb[:], B_lnb, t1, B_t1, sm, B_sm,
                                                   H1_d[n0 + s * 128:n0 + (s + 1) * 128, :], DB("H1"))
                        cnp['st2'] = False

                    run_pipelined(b0b_group, len(groups), depth=2, skew=1)
                    k.barrier(release=loc)
                k.barrier(release=keep)

        def pcol(n):
            return 1 + n if n < TC else n + 3

        def jb_of(c):
            return 3 - c if c < 4 else 71 - c

        SCL = float(DH) ** -0.5

        def phase_a1a(groups):
            with ExitStack() as es:
                loc = []
                wins = [cx.tile(es, [128, 8, 1024], BF16, "mwin", loc) for _ in range(4)]
                wv_ = ml_win_d.rearrange("(kc p) n -> p kc n", p=128)
                for pi in range(4):
                    k.dma('pool', wins[pi][0][:], wv_[:, :, pi * 1024:(pi + 1) * 1024], wins[pi][1], writes=[wins[pi][1]])
                zt, B_zt = cx.tile(es, [128, 16, 2], F32, "zt", loc)
                k.op('pool', lambda e: e.memset(zt[:], 0.0), writes=[B_zt])
                k.dma('sp', XMT_d[:, :, 0:1], zt[:, :, 0:1], B_zt, reads=[B_zt], writes=[DB("XMT")], allow_slow_non_contiguous=True)
                k.dma('sp', XMT_d[:, :, 257:259], zt[:, :, 0:2], B_zt, reads=[B_zt], writes=[DB("XMT")], allow_slow_non_contiguous=True)
                k.dma('sp', XMT_d[:, :, NTP - 1:NTP], zt[:, :, 0:1], B_zt, reads=[B_zt], writes=[DB("XMT")], allow_slow_non_contiguous=True)
                xts = [cx.tile(es, [128, D], F32, "xt", loc) for _ in range(8)]
                uTs = [cx.tile(es, [128, 8, 512], BF16, "uT", loc) for _ in range(2)]
                sts = [cx.tile(es, [128, 512], F32, "st", loc) for _ in range(6)]

                def a1a_loads(gq):
                    nq, nsq, _ = groups[gq]
                    for sq in range(nsq):
                        k.dma('sp', xts[(gq % 2) * 4 + sq][0][:], H2_d[nq + sq * 128:nq + (sq + 1) * 128, :],
                              xts[(gq % 2) * 4 + sq][1], reads=[DB("H2")], writes=[xts[(gq % 2) * 4 + sq][1]])

                ptA = cx.psum(es, [128, 512], F32, "ptA")
                ptB = cx.psum(es, [128, 512], F32, "ptB")
                pms = [cx.psum(es, [128, 512], F32, "pm") for _ in range(6)]
                cnm = dict(im=0)

                def a1a_group(gidx):
                    n0, nsub, r = groups[gidx]
                    ntok = nsub * 128
                    uT, B_uT = uTs[gidx % 2]
                    a1a_loads(gidx)
                    yield
                    yield from acquire(cnm, 'st1')
                    for s_ in range(nsub):
                        xt, B_xt = xts[(gidx % 2) * 4 + s_]
                        yield from make_uT_g(1, r, 0, 8, None, [DB("H2")], xt, B_xt, ptA, ptB, uT, B_uT, s_)
                    cnm['st1'] = False
                    yield from acquire(cnm, 'st2')
                    for ec in range(32 if r == 0 else 16):
                        im = cnm['im']
                        pm, B_pm = pms[im % 6]
                        st, B_st = sts[im % 6]
                        cnm['im'] += 1
                        w, B_w = wins[ec // 8]
                        c0 = (ec % 8) * 128
                        k.op('pe', [lambda e, kc=kc: e.matmul(pm[:, 0:ntok], lhsT=w[:, kc, c0:c0 + 128], rhs=uT[:, kc, 0:ntok],
                                                              start=(kc == 0), stop=(kc == 7)) for kc in range(8)],
                             reads=[B_w, B_uT], writes=[B_pm])
                        yield
                        if ec < 16:
                            k.op('act', lambda e: e.copy(out=st[:, 0:ntok], in_=pm[:, 0:ntok]), writes=[B_pm, B_st])
                            k.dma('sp', XMT_d[:, ec, pcol(n0):pcol(n0) + ntok], st[:, 0:ntok], B_st, reads=[B_st],
                                  writes=[DB("XMT")])
                        else:
                            k.op('act', lambda e: e.activation(out=st[:, 0:ntok], in_=pm[:, 0:ntok], func=AF.Silu),
                                 writes=[B_pm, B_st])
                            k.dma('sp', P2_d[:, ec - 16, n0 - TC:n0 - TC + ntok], st[:, 0:ntok], B_st, reads=[B_st],
                                  writes=[DB("P2")])
                        yield
                    cnm['st2'] = False

                run_pipelined(a1a_group, len(groups), depth=2, skew=1)
                k.barrier(release=loc)

        def phase_a1b(groups):
            with ExitStack() as es:
                loc = []
                wq, B_wq = cx.tile(es, [128, 4, 4, 512], BF16, "wq", loc)
                wk, B_wk = cx.tile(es, [128, 4, 4, 512], BF16, "wk", loc)
                wv, B_wv = cx.tile(es, [128, 4, 4, 512], BF16, "wv", loc)
                for (w_, B_w, src) in ((wq, B_wq, ml_wq_d), (wk, B_wk, ml_wk_d), (wv, B_wv, ml_wv_d)):
                    k.dma('pool', w_[:], src.rearrange("h (dc p) n -> p h dc n", p=128), B_w, writes=[B_w])
                wgf, B_wgf = cx.tile(es, [128, 3, 16, 16], F32, "wgf", loc)
                wg, B_wg = cx.tile(es, [128, 3, 16, 16], BF16, "wg", loc)
                bg, B_bg = cx.tile(es, [128, 16], F32, "bg", loc)
                mlv, B_mlv = cx.tile(es, [128, 16, 6], F32, "mlv", loc)
                k.dma('sp', wgf[:], wg_d[:, :, :, :], B_wgf, writes=[B_wgf])
                k.dma('sp', bg[:], bg_d[:, :], B_bg, writes=[B_bg])
                k.dma('sp', mlv[:], mlv_d[:, :, :], B_mlv, writes=[B_mlv])
                k.op('dve', lambda e: e.tensor_scalar(out=wgf[:, 1, :, :], in0=wgf[:, 1, :, :], scalar1=1.0 / SCL,
                                                     scalar2=None, op0=ALU.mult), reads=[B_wgf], writes=[B_wgf])
                k.op('dve', lambda e: e.tensor_copy(out=wg[:], in_=wgf[:]), reads=[B_wgf], writes=[B_wg])
                XMs = [cx.tile(es, [128, 16, 258], F32, "XM", loc) for _ in range(2)]
                XMbs = [cx.tile(es, [128, 16, 256], BF16, "XMb", loc) for _ in range(2)]
                XCbs = [cx.tile(es, [128, 16, 256], BF16, "XCb", loc) for _ in range(2)]
                qT2 = [cx.tile(es, [128, 16, 256], BF16, "qT", loc) for _ in range(2)]
                kT2 = [cx.tile(es, [128, 16, 256], BF16, "kT", loc) for _ in range(2)]
                vT2 = [cx.tile(es, [128, 16, 256], BF16, "vT", loc) for _ in range(2)]
                tcs = [cx.tile(es, [128, 256], F32, "tc", loc) for _ in range(4)]
                xcfs = [cx.tile(es, [128, 256], F32, "xcf", loc) for _ in range(4)]
                p1s = [cx.tile(es, [128, 256], F32, "p1s", loc) for _ in range(3)]
                tms = [cx.tile(es, [128, 512], F32, "tm", loc) for _ in range(4)]
                tmbs = [cx.tile(es, [128, 512], BF16, "tmb", loc) for _ in range(4)]
                gss = [cx.tile(es, [128, 16], F32, "gs", loc) for _ in range(2)]
                pfs = [cx.psum(es, [128, 512], F32, "pf") for _ in range(4)]
                pts = [cx.psum(es, [128, 512], F32, "ptm") for _ in range(3)]
                pgt = cx.psum(es, [128, 512], F32, "pgt")
                i1 = 0
                ipf = 0
                ipt = 0
                itm = 0
                igs = 0
                def a1b_loads(gq):
                    pcq = pcol(groups[gq][0])
                    k.dma('sp', XMs[gq % 2][0][:], XMT_d[:, :, pcq - 1:pcq + 257], XMs[gq % 2][1], reads=[DB("XMT")],
                          writes=[XMs[gq % 2][1]])

                cn = dict(i1=0, ipf=0, ipt=0, itm=0, igs=0)

                def a1b_group(gidx):
                    n0, r = groups[gidx]
                    XM, B_XM = XMs[gidx % 2]
                    XMb, B_XMb = XMbs[gidx % 2]
                    XCb, B_XCb = XCbs[gidx % 2]
                    qT, B_qT = qT2[gidx % 2]
                    kT, B_kT = kT2[gidx % 2]
                    vT, B_vT = vT2[gidx % 2]
                    a1b_loads(gidx)
                    yield
                    yield from acquire(cn, 'st1')
                    k.op('act', lambda e: e.copy(out=XMb[:], in_=XM[:, :, 1:257]), reads=[B_XM], writes=[B_XMb])

                    def conv_tail(ec):
                        xcf, B_xcf = xcfs[ec % 4]
                        k.op('dve', lambda e: e.tensor_copy(out=XCb[:, ec, :], in_=xcf[:]), reads=[B_xcf], writes=[B_XCb])
                        if r == 0:
                            p1, B_p1 = p1s[cn['i1'] % 3]
                            cn['i1'] += 1
                            k.op('dve', lambda e: e.tensor_scalar(out=p1[:], in0=xcf[:], scalar1=mlv[:, ec, 4:5], scalar2=None,
                                                                  op0=ALU.mult), reads=[B_xcf, B_mlv], writes=[B_p1])
                            k.dma('sp', P1_d[:, ec, n0 - TC:n0 - TC + 256], p1[:], B_p1, reads=[B_p1], writes=[DB("P1")])

                    for ec in range(16):
                        tc_, B_tc = tcs[ec % 4]
                        xcf, B_xcf = xcfs[ec % 4]
                        k.op('dve', lambda e: e.tensor_scalar(out=tc_[:], in0=XM[:, ec, 0:256], scalar1=mlv[:, ec, 0:1],
                                                             scalar2=None, op0=ALU.mult),
                             reads=[B_XM, B_mlv], writes=[B_tc])
                        yield
                        k.op('dve', lambda e: e.scalar_tensor_tensor(out=tc_[:], in0=XM[:, ec, 1:257], scalar=mlv[:, ec, 1:2],
                                                                    in1=tc_[:], op0=ALU.mult, op1=ALU.add),
                             reads=[B_XM, B_mlv], writes=[B_tc])
                        yield
                        k.op('dve', lambda e: e.scalar_tensor_tensor(out=tc_[:], in0=XM[:, ec, 2:258], scalar=mlv[:, ec, 2:3],
                                                                    in1=tc_[:], op0=ALU.mult, op1=ALU.add),
                             reads=[B_XM, B_mlv], writes=[B_tc])
                        yield
                        k.op('act', lambda e: e.activation(out=xcf[:], in_=tc_[:], func=AF.Silu, bias=mlv[:, ec, 3:4], scale=1.0),
                             reads=[B_tc, B_mlv], writes=[B_xcf])
                        if ec >= 2:
                            conv_tail(ec - 2)
                        yield
                    conv_tail(14)
                    yield
                    conv_tail(15)
                    yield
                    cn['st1'] = False
                    yield from acquire(cn, 'st2')
                    for (dst, B_dst, w_, B_w, src, B_src, scl) in ((qT, B_qT, wq, B_wq, XCb, B_XCb, 1.0),
                                                                  (kT, B_kT, wk, B_wk, XCb, B_XCb, SCL),
                                                                  (vT, B_vT, wv, B_wv, XMb, B_XMb, 1.0)):
                        for h in range(4):
                            for op_ in range(2):
                                pf, B_pf = pfs[cn['ipf'] % 4]
                                cn['ipf'] += 1
                                fns = []
                                for o2 in range(2):
                                    oc = op_ * 2 + o2
                                    for dc in range(4):
                                        fns.append(lambda e, dc=dc, oc=oc, o2=o2: e.matmul(
                                            pf[:, o2 * 256:(o2 + 1) * 256], lhsT=w_[:, h, dc, oc * 128:(oc + 1) * 128],
                                            rhs=src[:, h * 4 + dc, :], start=(dc == 0), stop=(dc == 3)))
                                k.op('pe', fns, reads=[B_w, B_src], writes=[B_pf])
                                yield
                                dv = dst[:, h * 4 + op_ * 2:h * 4 + op_ * 2 + 2, :]
                                pv_ = pf[:, 0:512].rearrange("p (a t) -> p a t", a=2)
                                if cn['ipf'] % 2:
                                    k.op('act', lambda e: e.activation(out=dv, in_=pv_, func=AF.Copy, scale=scl),
                                         writes=[B_pf, B_dst])
                                else:
                                    k.op('dve', lambda e: e.tensor_scalar(out=dv, in0=pv_, scalar1=scl,
                                                                         scalar2=None, op0=ALU.mult), writes=[B_pf, B_dst])
                                yield
                    for s_ in range(2):
                        for (w_, B_w, src, B_src, scl, dd, dn) in ((wk, B_wk, XCb, B_XCb, SCL, Ktm_d, "Ktm"),
                                                                  (wv, B_wv, XMb, B_XMb, 1.0, Vtm_d, "Vtm")):
                            for h in range(4):
                                pt, B_pt = pts[cn['ipt'] % 3]
                                cn['ipt'] += 1
                                tm, B_tm = (tmbs if dn == "Ktm" else tms)[cn['itm'] % 4]
                                cn['itm'] += 1
                                k.op('pe', [lambda e, dc=dc: e.matmul(pt[:], lhsT=src[:, h * 4 + dc, s_ * 128:(s_ + 1) * 128],
                                                                      rhs=w_[:, h, dc, :], start=(dc == 0), stop=(dc == 3))
                                            for dc in range(4)], reads=[B_w, B_src], writes=[B_pt])
                                yield
                                if cn['itm'] % 2:
                                    k.op('act', lambda e: e.activation(out=tm[:], in_=pt[:], func=AF.Copy, scale=scl),
                                         writes=[B_pt, B_tm])
                                else:
                                    k.op('dve', lambda e: e.tensor_scalar(out=tm[:], in0=pt[:], scalar1=scl, scalar2=None,
                                                                         op0=ALU.mult), writes=[B_pt, B_tm])
                                yield
                                k.dma('sp', dd[n0 + s_ * 128:n0 + (s_ + 1) * 128, h * 512:(h + 1) * 512], tm[:], B_tm,
                                      reads=[B_tm], writes=[DB(dn)])
                        gs, B_gs = gss[cn['igs'] % 2]
                        cn['igs'] += 1
                        fns = []
                        for j, (src, B_src) in enumerate(((qT, B_qT), (kT, B_kT), (vT, B_vT))):
                            for ec in range(16):
                                fns.append(lambda e, j=j, ec=ec, src=src: e.matmul(
                                    pgt[0][:, 0:16], lhsT=src[:, ec, s_ * 128:(s_ + 1) * 128], rhs=wg[:, j, ec, :],
                                    start=(j == 0 and ec == 0), stop=(j == 2 and ec == 15)))
                        k.op('pe', fns, reads=[B_qT, B_kT, B_vT, B_wg], writes=[pgt[1]])
                        yield
                        k.op('dve', lambda e: e.tensor_tensor(out=gs[:], in0=pgt[0][:, 0:16], in1=bg[:], op=ALU.add),
                             reads=[B_bg], writes=[pgt[1], B_gs])
                        yield
                        nn = n0 + s_ * 128
                        k.dma('sp', G_d[nn:nn + 128, :], gs[:], B_gs, reads=[B_gs], writes=[DB("G")])
                        cblk = n0 // 256
                        jb = 0 if cblk == 0 else NT // 256 - cblk
                        k.dma('sp', Gb_d[jb * 256 + s_ * 128:jb * 256 + (s_ + 1) * 128, :], gs[:], B_gs, reads=[B_gs],
                              writes=[DB("Gb")])
                    if r == 0:
                        for s_ in range(2):
                            u = n0 // 128 + s_
                            k.dma('sp', QT_d[u].rearrange("p (a t) -> p a t", t=128), qT[:, :, s_ * 128:(s_ + 1) * 128], B_qT,
                                  reads=[B_qT], writes=[DB("QT")])
                            k.dma('sp', KT_d[u].rearrange("p (a t) -> p a t", t=128), kT[:, :, s_ * 128:(s_ + 1) * 128], B_kT,
                                  reads=[B_kT], writes=[DB("KT")])
                    cn['st2'] = False

                run_pipelined(a1b_group, len(groups), depth=2, skew=1)
                k.barrier(release=loc)

        CB = 256
        NBK = NT // CB

        def cmapB(d, j):
            if d == 0 or j == 0:
                return j
            return NBK - j

        def phase_g1(d, COL, B_COL, DECB, B_DECB, ABF, B_ABF, EKB, B_EKB):
            NP = NBK
            with ExitStack() as es:
                loc = []
                KEEP, B_KEEP = cx.tile(es, [NP, 4, CB], F32, "KEEP", loc)
                RST, B_RST = cx.tile(es, [NP, 4, CB], F32, "RST", loc)
                onesF, B_ones = cx.tile(es, [NP, 128], F32, "ones", loc)
                oneT, B_one = cx.tile(es, [NP, 1], F32, "one", loc)
                k.op('pool', lambda e: e.memset(KEEP[:], 1.0), writes=[B_KEEP])
                k.op('pool', lambda e: e.memset(KEEP[:, :, 0:1], 0.0), writes=[B_KEEP])
                k.op('pool', lambda e: e.memset(RST[:], 0.0), writes=[B_RST])
                k.op('pool', lambda e: e.memset(RST[:, :, 0:1], -1e30), writes=[B_RST])
                k.op('pool', lambda e: e.memset(onesF[:], 1.0), writes=[B_ones])
                k.op('pool', lambda e: e.memset(oneT[:], 1.0), writes=[B_one])
                GC, B_GC = cx.tile(es, [NP, CB, 16], F32, "GC", loc)
                IG, B_IG = cx.tile(es, [NP, 4, CB], F32, "IG", loc)
                FG, B_FG = cx.tile(es, [NP, 4, CB], F32, "FG", loc)
                NB, B_NB = cx.tile(es, [NP, 4, CB], F32, "NB", loc)
                Gg, B_Gg = cx.tile(es, [NP, 4, CB], F32, "Gg", loc)
                WE, B_WE = cx.tile(es, [NP, 4, CB], F32, "WE", loc)
                Mx, B_Mx = cx.tile(es, [NP, 4, CB], F32, "Mx", loc)
                Rr, B_Rr = cx.tile(es, [NP, 4, CB], F32, "Rr", loc)
                X5, B_X5 = cx.tile(es, [NP, 20, CB], F32, "X5", loc)
                Y5, B_Y5 = cx.tile(es, [NP, 20, CB], F32, "Y5", loc)
                sm, B_sm = cx.tile(es, [NP, 8, 4], F32, "smg", loc)
                rows, B_rows = cx.tile(es, [4, 4, NP], F32, "rows", loc)
                tmpd, B_tmpd = cx.tile(es, [NP, NP], F32, "tmpd", loc)
                pA = cx.psum(es, [128, 512], F32, "pA")
                pB = cx.psum(es, [128, 512], F32, "pB")
                pC = cx.psum(es, [128, 512], F32, "pC")
                idn = identF[0:NP, 0:NP]
                src = (G_d if d == 0 else Gb_d).rearrange("(c t) j -> c t j", t=CB)
                k.dma('sp', GC[:], src, B_GC, reads=[DB("G"), DB("Gb")], writes=[B_GC])
                if d == 0:
                    vi = GC[:, :, 0:4].rearrange("p t h -> p h t")
                    vf = GC[:, :, 4:8].rearrange("p t h -> p h t")
                else:
                    vi = GC[:, ::-1, 8:12].rearrange("p t h -> p h t")
                    vf = GC[:, ::-1, 12:16].rearrange("p t h -> p h t")
                k.op('dve', lambda e: e.tensor_copy(out=IG[:], in_=vi), reads=[B_GC], writes=[B_IG])
                k.op('dve', lambda e: e.tensor_copy(out=FG[:], in_=vf), reads=[B_GC], writes=[B_FG])
                k.op('act', lambda e: e.activation(out=FG[:], in_=FG[:], func=AF.Exp, scale=-1.0), writes=[B_FG])
                k.op('act', lambda e: e.activation(out=FG[:], in_=FG[:], func=AF.Ln, bias=oneT[:, 0:1], scale=1.0),
                     reads=[B_one], writes=[B_FG])
                fl = lambda t_: t_[:].rearrange("p h t -> p (h t)")
                k.op('dve', lambda e: e.tensor_tensor_scan(out=fl(NB), data0=fl(KEEP), data1=fl(FG), initial=0.0,
                                                          op0=ALU.mult, op1=ALU.add),
                     reads=[B_KEEP, B_FG], writes=[B_NB])
                k.op('dve', lambda e: e.tensor_tensor(out=Gg[:], in0=IG[:], in1=NB[:], op=ALU.add),
                     reads=[B_IG, B_NB], writes=[B_Gg])
                k.op('dve', lambda e: e.tensor_tensor(out=WE[:], in0=Gg[:], in1=NB[:, :, CB - 1:CB].to_broadcast([NP, 4, CB]),
                                                     op=ALU.subtract), reads=[B_Gg, B_NB], writes=[B_WE])
                k.op('dve', lambda e: e.tensor_reduce(out=sm[:, 1, :], in_=WE[:], axis=AX.X, op=ALU.max),
                     reads=[B_WE], writes=[B_sm])
                k.op('dve', lambda e: e.tensor_scalar(out=sm[:, 0, :], in0=NB[:, :, CB - 1], scalar1=-1.0, scalar2=None,
                                                     op0=ALU.mult), reads=[B_NB], writes=[B_sm])
                k.op('dve', lambda e: e.tensor_tensor_scan(out=fl(Mx), data0=fl(RST), data1=fl(Gg), initial=0.0,
                                                          op0=ALU.add, op1=ALU.max),
                     reads=[B_RST, B_Gg], writes=[B_Mx])
                k.op('pe', [lambda e: e.transpose(out=pA[0][0:4, 0:NP], in_=sm[:, 0, :], identity=idn),
                            lambda e: e.transpose(out=pA[0][0:4, NP:2 * NP], in_=sm[:, 1, :], identity=idn)],
                     reads=[B_sm, B_identF], writes=[pA[1]])
                k.op('dve', lambda e: e.tensor_copy(out=rows[:, 0:2, :],
                                                   in_=pA[0][0:4, 0:2 * NP].rearrange("p (a c) -> p a c", a=2)),
                     writes=[pA[1], B_rows])
                k.op('dve', lambda e: e.tensor_tensor_scan(out=rows[:, 2, :], data0=rows[:, 0, :], data1=rows[:, 1, :],
                                                          initial=0.0, op0=ALU.add, op1=ALU.max), writes=[B_rows])
                k.op('dve', [lambda e: e.memset(rows[:, 3, 0:1], 0.0),
                             lambda e: e.tensor_copy(out=rows[:, 3, 1:NP], in_=rows[:, 2, 0:NP - 1])], writes=[B_rows])
                k.op('pe', [lambda e: e.transpose(out=pB[0][0:NP, 0:4], in_=rows[:, 2, :], identity=identF[0:4, 0:4]),
                            lambda e: e.transpose(out=pB[0][0:NP, 4:8], in_=rows[:, 3, :], identity=identF[0:4, 0:4])],
                     reads=[B_rows, B_identF], writes=[pB[1]])
                k.op('dve', lambda e: e.tensor_copy(out=sm[:, 2:4, :], in_=pB[0][0:NP, 0:8].rearrange("p (a c) -> p a c", a=2)),
                     writes=[pB[1], B_sm])
                k.op('dve', lambda e: e.tensor_tensor(out=sm[:, 4, :], in0=sm[:, 0, :], in1=sm[:, 3, :], op=ALU.add),
                     writes=[B_sm])
                k.op('dve', lambda e: e.tensor_tensor(out=sm[:, 4, :], in0=sm[:, 4, :], in1=sm[:, 2, :], op=ALU.subtract),
                     writes=[B_sm])
                k.op('act', lambda e: e.activation(out=sm[:, 4, :], in_=sm[:, 4, :], func=AF.Exp), writes=[B_sm])
                bc = lambda a_: a_.unsqueeze(2).to_broadcast([NP, 4, CB])
                x5 = lambda i: X5[:, i * 4:(i + 1) * 4, :]
                lastb = lambda t_: t_[:, :, CB - 1:CB].to_broadcast([NP, 4, CB])
                k.op('dve', lambda e: e.tensor_tensor(out=x5(0), in0=WE[:], in1=bc(sm[:, 2, :]), op=ALU.subtract),
                     reads=[B_WE, B_sm], writes=[B_X5])
                k.op('dve', lambda e: e.tensor_tensor(out=Rr[:], in0=Mx[:], in1=bc(sm[:, 3, :]), op=ALU.max),
                     reads=[B_Mx, B_sm], writes=[B_Rr])
                k.op('dve', lambda e: e.tensor_tensor(out=x5(1), in0=Gg[:], in1=lastb(Rr), op=ALU.subtract),
                     reads=[B_Gg, B_Rr], writes=[B_X5])
                k.op('dve', lambda e: e.tensor_tensor(out=x5(2), in0=bc(sm[:, 3, :]), in1=Rr[:], op=ALU.subtract),
                     reads=[B_Rr, B_sm], writes=[B_X5])
                k.op('dve', lambda e: e.tensor_tensor(out=x5(3), in0=lastb(Rr), in1=Rr[:], op=ALU.subtract),
                     reads=[B_Rr], writes=[B_X5])
                k.op('dve', lambda e: e.tensor_tensor(out=x5(4), in0=NB[:], in1=Rr[:], op=ALU.subtract),
                     reads=[B_NB, B_Rr], writes=[B_X5])
                k.op('act', lambda e: e.activation(out=X5[:], in_=X5[:], func=AF.Exp), writes=[B_X5])
                if d == 0:
                    Ysrc, B_Ysrc = X5, B_X5
                else:
                    k.op('dve', lambda e: e.tensor_copy(out=Y5[:], in_=X5[:, :, ::-1]), reads=[B_X5], writes=[B_Y5])
                    Ysrc, B_Ysrc = Y5, B_Y5
                for bk, (pb_, B_pb) in enumerate((pA, pB)):
                    qs = list(range(bk * 10, bk * 10 + 10))
                    fns = []
                    for q in qs:
                        for ti in range(2):
                            o = ((q - bk * 10) * 2 + ti) * NP
                            fns.append(lambda e, q=q, ti=ti, o=o: e.transpose(out=pb_[:, o:o + NP],
                                                                             in_=Ysrc[:, q, ti * 128:(ti + 1) * 128], identity=idn))
                    k.op('pe', fns, reads=[B_Ysrc, B_identF], writes=[B_pb])
                    k.op('dve', lambda e: e.tensor_copy(
                        out=COL[:, qs[0]:qs[-1] + 1, :, :],
                        in_=pb_[:, 0:20 * NP].rearrange("p (a b c) -> p a b c", b=2, c=NP)),
                        writes=[B_pb, B_COL])
                k.op('dve', lambda e: e.tensor_copy(out=EKB[:], in_=COL[:, 0:4, :, :]), reads=[B_COL], writes=[B_EKB])
                k.op('dve', lambda e: e.tensor_copy(out=ABF[:], in_=COL[:, 4:8, :, :]), reads=[B_COL], writes=[B_ABF])
                for h in range(4):
                    k.op('dve', lambda e: e.tensor_scalar(out=tmpd[:], in0=idn, scalar1=sm[:, 4, h:h + 1], scalar2=None,
                                                         op0=ALU.mult), reads=[B_sm, B_identF], writes=[B_tmpd])
                    k.op('pe', lambda e: e.matmul(pC[0][:, h * NP:(h + 1) * NP], lhsT=onesF[:], rhs=tmpd[:], start=True,
                                                  stop=True), reads=[B_ones, B_tmpd], writes=[pC[1]])
                k.op('dve', lambda e: e.tensor_copy(out=DECB[:], in_=pC[0][:, 0:4 * NP].rearrange("p (a c) -> p a c", c=NP)),
                     writes=[pC[1], B_DECB])
                k.barrier(release=loc)

        def acquire(lk, name):
            while lk.get(name):
                yield
            lk[name] = True

        def run_pipelined(make_gen, n, depth, skew):
            active = []
            nxt = 0
            tick = 0
            last_start = -10 ** 9
            while nxt < n or active:
                if nxt < n and len(active) < depth and tick - last_start >= skew:
                    active.append(make_gen(nxt))
                    nxt += 1
                    last_start = tick
                for g in list(active):
                    try:
                        next(g)
                    except StopIteration:
                        active.remove(g)
                tick += 1

        def run_interleaved(gens):
            active = list(gens)
            while active:
                for g in list(active):
                    try:
                        next(g)
                    except StopIteration:
                        active.remove(g)

        def phase_scan_all():
            with ExitStack() as es:
                loc = []
                G1 = []
                for d in range(2):
                    COL, B_COL = cx.tile(es, [128, 20, 2, NBK], F32, "COL", loc)
                    DECB, B_DECB = cx.tile(es, [128, 4, NBK], F32, "DECB", loc)
                    ABF, B_ABF = cx.tile(es, [128, 4, 2, NBK], BF16, "ABF", loc)
                    EKB, B_EKB = cx.tile(es, [128, 4, 2, NBK], BF16, "EKB", loc)
                    msk, B_msk = cx.tile(es, [128, 128], F32, "msk", loc)
                    G1.append((COL, B_COL, DECB, B_DECB, ABF, B_ABF, EKB, B_EKB, msk, B_msk))
                for d in range(2):
                    COL, B_COL, DECB, B_DECB, ABF, B_ABF, EKB, B_EKB, msk, B_msk = G1[d]
                    phase_g1(d, COL, B_COL, DECB, B_DECB, ABF, B_ABF, EKB, B_EKB)
                    k.dma('sp', msk[:], masks_d[:, d, :], B_msk, writes=[B_msk])
                for hs0 in (0, 2):
                    with ExitStack() as es2:
                        loc2 = []
                        gens = [scan_dir(es2, loc2, d, hs0, G1[d]) for d in range(2)]
                        run_interleaved(gens)
                        k.barrier(release=loc2)
                k.barrier(release=loc)

        def scan_dir(es, loc, d, hs0, g1, nblocks=NBK):
            COL, B_COL, DECB, B_DECB, ABF, B_ABF, EKB, B_EKB, msk, B_msk = g1
            NHP = 2
            HW = NHP * 512
            C_, _ = cx.tile(es, [128, NHP * 4, 512], F32, "C", loc)
            Cb_, _ = cx.tile(es, [128, NHP * 4, 512], BF16, "Cb", loc)
            n_, B_n = cx.tile(es, [128, NHP * 4], F32, "n", loc)
            nb_, B_nb = cx.tile(es, [128, NHP * 4], BF16, "nb", loc)
            BC = [Buf("C_%d" % i) for i in range(NHP * 4)]
            BCb = [Buf("Cb_%d" % i) for i in range(NHP * 4)]
            k.op('pool', lambda e: e.memset(C_[:], 0.0), writes=BC)
            k.op('pool', lambda e: e.memset(Cb_[:], 0.0), writes=BCb)
            k.op('pool', lambda e: e.memset(n_[:], 0.0), writes=[B_n])
            k.op('pool', lambda e: e.memset(nb_[:], 0.0), writes=[B_nb])
            kbs = [[cx.tile(es, [128, HW], BF16, "kb", loc) for _ in range(2)] for _ in range(2)]
            kTs = [[cx.tile(es, [128, NHP * 4, 128], BF16, "kTt", loc) for _ in range(2)] for _ in range(2)]
            vws = [[cx.tile(es, [128, HW], BF16, "vw", loc) for _ in range(2)] for _ in range(2)]
            vas = [[cx.tile(es, [128, HW], BF16, "va", loc) for _ in range(2)] for _ in range(2)]
            qTs = [[cx.tile(es, [128, NHP * 4, 128], BF16, "qTt", loc) for _ in range(2)] for _ in range(2)]
            vts = [cx.tile(es, [128, HW], F32, "vt", loc) for _ in range(2)]
            PTs = [cx.tile(es, [128, NHP, 128], BF16, "PT", loc) for _ in range(3)]
            HOs = [cx.tile(es, [128, 512], F32, "HO", loc) for _ in range(2)]
            Tqs = [cx.tile(es, [128, 512], F32, "Tq", loc) for _ in range(2)]
            smS = [cx.tile(es, [128, 40], F32, "smS", loc) for _ in range(2)]
            pks = cx.psum(es, [128, 512], F32, "pks")
            bank3 = [cx.psum(es, [128, 512], F32, "pb3") for _ in range(3)]
            SM0 = 256
            cnt = dict(vt=0, pt=0, ho=0, tq=0, ss=0, bk=0)
            yield

            def loads(j):
                c = cmapB(d, j)
                lat = c >= 1
                tl = [2 * c, 2 * c + 1] if d == 0 else [2 * c + 1, 2 * c]
                for i, u in enumerate(tl):
                    kb, B_kb = kbs[j % 2][i]
                    k.dma('sp', kb[:], Ktm_d[u * 128:(u + 1) * 128, hs0 * 512:hs0 * 512 + HW], B_kb, reads=[DB("Ktm")], writes=[B_kb])
                    if lat:
                        kT, B_kT = kTs[j % 2][i]
                        qT, B_qT = qTs[j % 2][i]
                        k.dma('sp', kT[:], KT_d[u][:, hs0 * 512:hs0 * 512 + NHP * 512].rearrange("p (a t) -> p a t", t=128),
                              B_kT, reads=[DB("KT")], writes=[B_kT])
                        k.dma('sp', qT[:], QT_d[u][:, hs0 * 512:hs0 * 512 + NHP * 512].rearrange("p (a t) -> p a t", t=128),
                              B_qT, reads=[DB("QT")], writes=[B_qT])

            loads(0)
            yield
            for j in range(nblocks):
                c = cmapB(d, j)
                lat = c >= 1
                tl = [2 * c, 2 * c + 1] if d == 0 else [2 * c + 1, 2 * c]
                col = lambda q, h, ti: COL[:, q * 4 + h, ti, j:j + 1]
                for i, u in enumerate(tl):
                    ti = u - 2 * c
                    vt, B_vt = vts[cnt['vt'] % 2]
                    cnt['vt'] += 1
                    vw, B_vw = vws[j % 2][i]
                    va, B_va = vas[j % 2][i]
                    k.dma('sp', vt[:], Vtm_d[u * 128:(u + 1) * 128, hs0 * 512:hs0 * 512 + HW], B_vt, reads=[DB("Vtm")], writes=[B_vt])
                    for hl in range(NHP):
                        h = hs0 + hl
                        k.op('act', lambda e: e.activation(out=vw[:, hl * 512:(hl + 1) * 512], in_=vt[:, hl * 512:(hl + 1) * 512],
                                                           func=AF.Copy, scale=col(0, h, ti)),
                             reads=[B_vt, B_COL], writes=[B_vw])
                        if lat:
                            k.op('act', lambda e: e.activation(out=va[:, hl * 512:(hl + 1) * 512], in_=vt[:, hl * 512:(hl + 1) * 512],
                                                               func=AF.Copy, scale=col(1, h, ti)),
                                 reads=[B_vt, B_COL], writes=[B_va])
                    yield
                if j + 1 < nblocks:
                    loads(j + 1)
                if lat:
                    for i, u in enumerate(tl):
                        ti = u - 2 * c
                        qT, B_qT = qTs[j % 2][i]
                        sS, B_sS = smS[cnt['ss'] % 2]
                        cnt['ss'] += 1
                        PTl = []
                        for w in range(i + 1):
                            kT, B_kT = kTs[j % 2][w]
                            PT, B_PT = PTs[cnt['pt'] % 3]
                            cnt['pt'] += 1
                            PTl.append((PT, B_PT, w))
                            fns = []
                            for hl in range(NHP):
                                for kc in range(4):
                                    fns.append(lambda e, hl=hl, kc=kc: e.matmul(pks[0][:, hl * 128:(hl + 1) * 128],
                                                                               lhsT=kT[:, hl * 4 + kc, :], rhs=qT[:, hl * 4 + kc, :],
                                                                               start=(kc == 0), stop=(kc == 3)))
                            k.op('pe', fns, reads=[B_qT, B_kT], writes=[pks[1]])
                            yield
                            kqv = pks[0][:, 0:NHP * 128].rearrange("p (h t) -> p h t", h=NHP)
                            if w == i:
                                k.op('dve', lambda e: e.tensor_tensor(out=PT[:], in0=kqv,
                                                                     in1=msk[:].unsqueeze(1).to_broadcast([128, NHP, 128]), op=ALU.mult),
                                     reads=[B_msk], writes=[pks[1], B_PT])
                            else:
                                k.op('dve', lambda e: e.tensor_copy(out=PT[:], in_=kqv), writes=[pks[1], B_PT])
                            yield
                        fns = []
                        rds = [B_qT, B_nb, B_ABF]
                        for hl in range(NHP):
                            h = hs0 + hl
                            for wi, (PT, B_PT, w) in enumerate(PTl):
                                tw = tl[w] - 2 * c
                                fns.append(lambda e, hl=hl, h=h, PT=PT, tw=tw, wi=wi: e.matmul(
                                    pks[0][:, SM0 + 2 * hl:SM0 + 2 * hl + 1], lhsT=PT[:, hl, :], rhs=ABF[:, h, tw, j:j + 1],
                                    start=(wi == 0), stop=(wi == len(PTl) - 1)))
                                rds.append(B_PT)
                            for kc in range(4):
                                fns.append(lambda e, hl=hl, kc=kc: e.matmul(pks[0][:, SM0 + 2 * hl + 1:SM0 + 2 * hl + 2],
                                                                           lhsT=qT[:, hl * 4 + kc, :],
                                                                           rhs=nb_[:, hl * 4 + kc:hl * 4 + kc + 1],
                                                                           start=(kc == 0), stop=(kc == 3)))
                        k.op('pe', fns, reads=rds, writes=[pks[1]])
                        yield
                        NS = 2 * NHP
                        k.op('act', lambda e: e.copy(out=sS[:, 0:NS], in_=pks[0][:, SM0:SM0 + NS]), writes=[pks[1], B_sS])
                        yield
                        sv = lambda a_: sS[:, 8 * a_:8 * a_ + NHP]
                        ev = sS[:, 0:NS].rearrange("p (h two) -> p two h", two=2)
                        cq = lambda q: COL[:, q * 4 + hs0:q * 4 + hs0 + NHP, ti, j]
                        k.op('dve', lambda e: e.tensor_tensor(out=sv(1), in0=ev[:, 0, :], in1=cq(3), op=ALU.mult),
                             reads=[B_COL], writes=[B_sS])
                        k.op('dve', lambda e: e.tensor_tensor(out=sv(2), in0=ev[:, 1, :], in1=cq(2), op=ALU.mult),
                             reads=[B_COL], writes=[B_sS])
                        yield
                        k.op('dve', lambda e: e.tensor_tensor(out=sv(1), in0=sv(1), in1=sv(2), op=ALU.add), writes=[B_sS])
                        yield
                        k.op('dve', lambda e: e.tensor_scalar(out=sv(2), in0=sv(1), scalar1=-1.0, scalar2=None, op0=ALU.mult),
                             writes=[B_sS])
                        yield
                        k.op('dve', lambda e: e.tensor_tensor(out=sv(1), in0=sv(1), in1=sv(2), op=ALU.max), writes=[B_sS])
                        yield
                        k.op('dve', lambda e: e.tensor_tensor(out=sv(1), in0=sv(1), in1=cq(4), op=ALU.max),
                             reads=[B_COL], writes=[B_sS])
                        yield
                        k.op('dve', lambda e: e.reciprocal(out=sv(1), in_=sv(1)), writes=[B_sS])
                        yield
                        k.op('dve', lambda e: e.tensor_tensor(out=sv(2), in0=sv(1), in1=cq(3), op=ALU.mult),
                             reads=[B_COL], writes=[B_sS])
                        k.op('dve', lambda e: e.tensor_tensor(out=sv(3), in0=sv(1), in1=cq(2), op=ALU.mult),
                             reads=[B_COL], writes=[B_sS])
                        yield
                        Hd, Hn = (HF_d, "HF") if d == 0 else (HB_d, "HB")
                        for hl in range(NHP):
                            h = hs0 + hl
                            Tq, B_Tq = Tqs[cnt['tq'] % 2]
                            cnt['tq'] += 1
                            HO, B_HO = HOs[cnt['ho'] % 2]
                            cnt['ho'] += 1
                            pnum_, B_pnum = bank3[cnt['bk'] % 3]
                            pqc_, B_pqc = bank3[(cnt['bk'] + 1) % 3]
                            cnt['bk'] += 2
                            fns = []
                            rds = []
                            for wi, (PT, B_PT, w) in enumerate(PTl):
                                va, B_va = vas[j % 2][w]
                                fns.append(lambda e, PT=PT, va=va, wi=wi: e.matmul(pnum_[:], lhsT=PT[:, hl, :],
                                                                                  rhs=va[:, hl * 512:(hl + 1) * 512],
                                                                                  start=(wi == 0), stop=(wi == len(PTl) - 1)))
                                rds += [B_PT, B_va]
                            k.op('pe', fns, reads=rds, writes=[B_pnum])
                            k.op('pe', [lambda e, kc=kc: e.matmul(pqc_[:], lhsT=qT[:, hl * 4 + kc, :], rhs=Cb_[:, hl * 4 + kc, :],
                                                                  start=(kc == 0), stop=(kc == 3)) for kc in range(4)],
                                 reads=[B_qT] + BCb[hl * 4:hl * 4 + 4], writes=[B_pqc])
                            yield
                            k.op('act', lambda e: e.activation(out=Tq[:], in_=pqc_[:], func=AF.Copy, scale=sS[:, 24 + hl:25 + hl]),
                                 reads=[B_sS], writes=[B_pqc, B_Tq])
                            yield
                            k.op('dve', lambda e: e.scalar_tensor_tensor(out=HO[:], in0=pnum_[:], scalar=sS[:, 16 + hl:17 + hl],
                                                                        in1=Tq[:], op0=ALU.mult, op1=ALU.add),
                                 reads=[B_sS, B_Tq], writes=[B_pnum, B_HO])
                            yield
                            r0 = u * 128 - TC
                            k.dma('sp', Hd[r0:r0 + 128, h * 512:(h + 1) * 512], HO[:], B_HO, reads=[B_HO], writes=[DB(Hn)])
                if j == nblocks - 1:
                    continue
                fns = []
                for hl in range(NHP):
                    h = hs0 + hl
                    for kc in range(4):
                        for i, u in enumerate(tl):
                            ti = u - 2 * c
                            kb, B_kb = kbs[j % 2][i]
                            fns.append(lambda e, hl=hl, h=h, kc=kc, kb=kb, ti=ti, i=i: e.matmul(
                                pks[0][:, SM0 + 32 + hl * 4 + kc:SM0 + 33 + hl * 4 + kc],
                                lhsT=kb[:, hl * 512 + kc * 128:hl * 512 + (kc + 1) * 128], rhs=EKB[:, h, ti, j:j + 1],
                                start=(i == 0), stop=(i == 1)))
                k.op('pe', fns, reads=[B_EKB, kbs[j % 2][0][1], kbs[j % 2][1][1]], writes=[pks[1]])
                yield
                k.op('dve', lambda e: e.tensor_tensor(out=n_[:].rearrange("p (h c) -> p h c", h=NHP),
                                                     in0=n_[:].rearrange("p (h c) -> p h c", h=NHP),
                                                     in1=DECB[:, hs0:hs0 + NHP, j:j + 1].to_broadcast([128, NHP, 4]), op=ALU.mult),
                     reads=[B_DECB, B_nb], writes=[B_n])
                yield
                k.op('dve', lambda e: e.tensor_tensor(out=n_[:], in0=n_[:], in1=pks[0][:, SM0 + 32:SM0 + 32 + NHP * 4], op=ALU.add),
                     writes=[pks[1], B_n])
                yield
                k.op('dve', lambda e: e.tensor_copy(out=nb_[:], in_=n_[:]), reads=[B_n], writes=[B_nb])
                for hl in range(NHP):
                    h = hs0 + hl
                    for kc in range(4):
                        ii = hl * 4 + kc
                        pdc_, B_pdc = bank3[cnt['bk'] % 3]
                        cnt['bk'] += 1
                        fns = []
                        rds = []
                        for i in range(2):
                            kb, B_kb = kbs[j % 2][i]
                            vw, B_vw = vws[j % 2][i]
                            fns.append(lambda e, kb=kb, vw=vw, i=i: e.matmul(pdc_[:], lhsT=kb[:, hl * 512 + kc * 128:hl * 512 + (kc + 1) * 128],
                                                                           rhs=vw[:, hl * 512:(hl + 1) * 512], start=(i == 0), stop=(i == 1)))
                            rds += [B_kb, B_vw]
                        k.op('pe', fns, reads=rds, writes=[B_pdc])
                        k.op('act', lambda e: e.activation(out=C_[:, ii, :], in_=C_[:, ii, :], func=AF.Copy,
                                                           scale=DECB[:, h, j:j + 1]),
                             reads=[B_DECB, BCb[ii]], writes=[BC[ii]])
                        yield
                        k.op('dve', lambda e: e.tensor_tensor(out=C_[:, ii, :], in0=C_[:, ii, :], in1=pdc_[:], op=ALU.add),
                             writes=[B_pdc, BC[ii]])
                        yield
                        k.op('dve', lambda e: e.tensor_copy(out=Cb_[:, ii, :], in_=C_[:, ii, :]), reads=[BC[ii]], writes=[BCb[ii]])
                        yield

        def phase_c1(ntiles=T // 128):
            with ExitStack() as es:
                loc = []
                mwout, B_mwout = cx.tile(es, [128, 16, D], BF16, "mwout", loc)
                k.dma('pool', mwout[:], ml_wout_d.rearrange("(ec p) n -> p ec n", p=128), B_mwout, writes=[B_mwout])
                mlv, B_mlv = cx.tile(es, [128, 16, 6], F32, "mlv", loc)
                k.dma('sp', mlv[:], mlv_d[:, :, :], B_mlv, writes=[B_mlv])
                gt, B_gt = cx.tile(es, [128, D], F32, "gm1", loc)
                k.dma('sp', gt[:], gbc_d[4], B_gt, reads=[DB("gbc4")], writes=[B_gt])
                lng, B_lng = cx.tile(es, [128, D], F32, "lng", loc)
                lnb, B_lnb = cx.tile(es, [128, D], F32, "lnb", loc)
                k.dma('sp', lng[:], lng_d[:, 2, :], B_lng, writes=[B_lng])
                k.dma('sp', lnb[:], lnb_d[:, 2, :], B_lnb, writes=[B_lnb])
                ND = 3
                hfs = [cx.tile(es, [128, E], F32, "hf", loc) for _ in range(ND)]
                hbs = [cx.tile(es, [128, E], F32, "hb", loc) for _ in range(ND)]
                hns = [cx.tile(es, [128, E], BF16, "hn", loc) for _ in range(ND)]
                p1s = [cx.tile(es, [128, 16, 128], F32, "p1t", loc) for _ in range(ND)]
                p2s = [cx.tile(es, [128, 16, 128], F32, "p2t", loc) for _ in range(ND)]
                yTs = [cx.tile(es, [128, 16, 128], BF16, "yT", loc) for _ in range(ND)]
                sts = [cx.tile(es, [128, 48], F32, "stc", loc) for _ in range(ND)]
                xts = [cx.tile(es, [128, D], F32, "xt", loc) for _ in range(ND)]
                t1s = [cx.tile(es, [128, D], F32, "t1", loc) for _ in range(ND)]
                sms = [cx.tile(es, [128, 16], F32, "sm", loc) for _ in range(ND)]
                pths = [cx.psum(es, [128, 1024], BF16, "pth") for _ in range(4)]
                pos = [cx.psum(es, [128, 512], F32, "po") for _ in range(4)]
                def c1_loads(iq):
                    tq0 = iq * 128
                    k.dma('sp', hfs[iq % ND][0][:], HF_d[tq0:tq0 + 128, :], hfs[iq % ND][1], reads=[DB("HF")], writes=[hfs[iq % ND][1]])
                    k.dma('sp', hbs[iq % ND][0][:], HB_d[tq0:tq0 + 128, :], hbs[iq % ND][1], reads=[DB("HB")], writes=[hbs[iq % ND][1]])
                    k.dma('sp', p1s[iq % ND][0][:], P1_d[:, :, tq0:tq0 + 128], p1s[iq % ND][1], reads=[DB("P1")], writes=[p1s[iq % ND][1]])
                    k.dma('sp', p2s[iq % ND][0][:], P2_d[:, :, tq0:tq0 + 128], p2s[iq % ND][1], reads=[DB("P2")], writes=[p2s[iq % ND][1]])
                    k.dma('sp', xts[iq % ND][0][:], H2_d[TC + tq0:TC + tq0 + 128, :], xts[iq % ND][1], reads=[DB("H2")], writes=[xts[iq % ND][1]])

                def c1_tile(i):
                    t0 = i * 128
                    hf, B_hf = hfs[i % ND]
                    hb, B_hb = hbs[i % ND]
                    hn, B_hn = hns[i % ND]
                    p1, B_p1 = p1s[i % ND]
                    p2, B_p2 = p2s[i % ND]
                    yT, B_yT = yTs[i % ND]
                    st, B_st = sts[i % ND]
                    xt, B_xt = xts[i % ND]
                    t1, B_t1 = t1s[i % ND]
                    sm, B_sm = sms[i % ND]
                    c1_loads(i)
                    yield
                    k.op('dve', lambda e: e.tensor_tensor(out=hf[:], in0=hf[:], in1=hb[:], op=ALU.add), reads=[B_hb], writes=[B_hf])
                    yield
                    k.op('dve', [lambda e, h=h: e.bn_stats(out=st[:, h * 6:(h + 1) * 6], in_=hf[:, h * 512:(h + 1) * 512])
                                 for h in range(4)], reads=[B_hf], writes=[B_st])
                    yield
                    for h in range(4):
                        k.op('dve', lambda e: e.bn_aggr(out=st[:, 24 + 2 * h:26 + 2 * h], in_=st[:, h * 6:(h + 1) * 6]), writes=[B_st])
                        yield
                    mvv = st[:, 24:32].rearrange("p (h two) -> p two h", two=2)
                    k.op('act', lambda e: e.activation(out=st[:, 32:36], in_=mvv[:, 1, :], func=AF.Sqrt, bias=epsT[:, 0:1], scale=1.0),
                         reads=[B_eps], writes=[B_st])
                    yield
                    k.op('dve', lambda e: e.reciprocal(out=st[:, 32:36], in_=st[:, 32:36]), writes=[B_st])
                    yield
                    k.op('dve', lambda e: e.scalar_tensor_tensor(out=st[:, 36:40], in0=mvv[:, 0, :], scalar=-1.0, in1=st[:, 32:36],
                                                                op0=ALU.mult, op1=ALU.mult), writes=[B_st])
                    yield
                    for h in range(4):
                        k.op('act', lambda e: e.activation(out=hn[:, h * 512:(h + 1) * 512], in_=hf[:, h * 512:(h + 1) * 512],
                                                           func=AF.Identity, scale=st[:, 32 + h:33 + h], bias=st[:, 36 + h:37 + h]),
                             reads=[B_st, B_hf], writes=[B_hn])
                        yield
                    for half in range(2):
                        pth, B_pth = pths[(2 * i + half) % 4]
                        k.op('pe', [lambda e, q=q: e.transpose(out=pth[:, q * 128:(q + 1) * 128],
                                                              in_=hn[:, (half * 8 + q) * 128:(half * 8 + q + 1) * 128],
                                                              identity=identB[:]) for q in range(8)],
                             reads=[B_hn, B_identB], writes=[B_pth])
                        yield
                        for q in range(8):
                            ec = half * 8 + q
                            k.op('dve', lambda e: e.scalar_tensor_tensor(out=p1[:, ec, :], in0=pth[:, q * 128:(q + 1) * 128],
                                                                        scalar=mlv[:, ec, 5:6], in1=p1[:, ec, :],
                                                                        op0=ALU.mult, op1=ALU.add),
                                 reads=[B_mlv], writes=[B_pth, B_p1])
                            yield
                    k.op('dve', lambda e: e.tensor_tensor(out=yT[:], in0=p1[:], in1=p2[:], op=ALU.mult),
                         reads=[B_p1, B_p2], writes=[B_yT])
                    yield
                    po0, po1 = pos[(2 * i) % 4], pos[(2 * i + 1) % 4]
                    for hfx, (po, B_po) in enumerate((po0, po1)):
                        k.op('pe', [lambda e, ec=ec: e.matmul(po[:], lhsT=yT[:, ec, :], rhs=mwout[:, ec, hfx * 512:(hfx + 1) * 512],
                                                              start=(ec == 0), stop=(ec == 15)) for ec in range(16)],
                             reads=[B_yT, B_mwout], writes=[B_po])
                        yield
                    yield from post_norm_g(po0[0][:], po0[1], po1[0][:], po1[1], xt[:], B_xt,
                                           gt, B_gt, lng[:], B_lng, lnb[:], B_lnb, t1, B_t1, sm, B_sm,
                                           H3_d[t0:t0 + 128, :], DB("H3"))

                run_pipelined(c1_tile, ntiles, depth=ND, skew=18)
                k.barrier(release=loc)

        phase_mods()
        if dbg.get('full'):
            layer0_mixer(L0_GROUPS)
            phase_ffn(0, H1_d, "H1", H2_d, "H2", [(i * 256, i * 256, 1 if i == 0 else 0) for i in range(NT // 256)])
            phase_a1a(L0_GROUPS)
            phase_a1b([(i * 256, 1 if i == 0 else 0) for i in range(NT // 256)])
            phase_scan_all()
            phase_c1()
            phase_ffn(1, H3_d, "H3", out_d, "out", [(i * 256, i * 256, 0) for i in range(T // 256)])
        if dbg.get('a1'):
            ng = dbg.get('ngroups', 9)
            phase_a1a(L0_GROUPS[:ng])
            nt256 = (256 + (ng - 1) * 512) // 256
            phase_a1b([(i * 256, 1 if i == 0 else 0) for i in range(nt256)])
        if dbg.get('scan'):
            phase_scan_all()
        if dbg.get('c1'):
            phase_c1(dbg.get('c1tiles', T // 128))
        if dbg.get('l0mix'):
            layer0_mixer(L0_GROUPS[:dbg.get('ngroups', 9)])
        if dbg.get('ffn0'):
            groups = [(i * 256, i * 256, 1 if i == 0 else 0) for i in range(dbg.get('ngroups', 17))]
            phase_ffn(0, H1_d, "H1", H2_d, "H2", groups)
        k.barrier()
    return nc


POOL_W = (2, 4, 8, 16)
POOL_D = {0: list(range(-1, 4)), 1: list(range(-1, 5)), 2: list(range(-2, 6)), 3: list(range(-4, 8))}

def pool_consts():
    mats = []
    for gi, w in enumerate(POOL_W):
        h = w // 2
        for dl in POOL_D[gi]:
            m = np.zeros((2, 64, 8, 64), np.float32)
            for rr in range(2):
                srow = 2 * dl + rr
                for rho in range(8):
                    if rho - h <= srow <= rho + h - 1:
                        for c in range(64):
                            lo = max(c - h, 0); hi = min(c + h - 1, 63)
                            m[rr, lo:hi + 1, rho, c] = 1.0
            mats.append(m.reshape(128, 512))
    poolm = np.ascontiguousarray(np.stack(mats, 1))
    pc = np.zeros((128, 8, 256), np.float32)
    for gi, w in enumerate(POOL_W):
        h = w // 2
        for st in range(2):
            for p in range(128):
                n_src = st * 128 + p
                for n in range(256):
                    if n - h <= n_src <= n + h - 1:
                        pc[p, gi * 2 + st, n] = 1.0
    def cnt(L, w):
        t = np.arange(L)
        lo = np.clip(t - w // 2, 0, L); hi = np.clip(t + w - w // 2, 0, L)
        return (hi - lo).astype(np.float32)
    invc = np.zeros((4, 64, 64), np.float32); invcc = np.zeros((4, 256), np.float32)
    for gi, w in enumerate(POOL_W):
        c64 = cnt(64, w)
        invc[gi] = (np.float32(1.0) / c64)[:, None] * (np.float32(1.0) / c64)[None, :]
        invcc[gi] = np.float32(1.0) / cnt(256, w)
    invc = np.ascontiguousarray(np.broadcast_to(invc.reshape(1, 4, 4096), (128, 4, 4096)))
    invcc = np.ascontiguousarray(np.broadcast_to(invcc.reshape(1, 4, 256), (128, 4, 256)))
    return poolm, pc, invc, invcc

def abvec(inp):
    v = np.zeros((128, 4, 36), np.float32)
    v[:, :, 0:31] = inp['ab_conv_w'][0].reshape(31, 4, 128).transpose(2, 1, 0)
    v[:, :, 31] = inp['ab_conv_b'][0].reshape(4, 128).T
    v[:, :, 32] = inp['ab_norm_g'][0].reshape(4, 128).T
    v[:, :, 33] = inp['ab_norm_b'][0].reshape(4, 128).T
    v[:, :, 34] = inp['ab_pool_b'][0].reshape(4, 128).T
    v[:, :, 35] = inp['ab_pool_scale'][0].reshape(4, 128).T
    return v

def ml_consts(inp):
    wgt = inp['ml_w_gate'][0]
    wg = np.ascontiguousarray(wgt.reshape(2, 3, 16, 128, 8).transpose(3, 1, 2, 0, 4).reshape(128, 3, 16, 16)).astype(np.float32)
    bg = np.ascontiguousarray(np.broadcast_to(inp['ml_b_gate'][0].reshape(1, 16), (128, 16))).astype(np.float32)
    mlv = np.zeros((128, 16, 6), np.float32)
    for i in range(3):
        mlv[:, :, i] = inp['ml_conv_w'][0][i].reshape(16, 128).T
    mlv[:, :, 3] = inp['ml_conv_b'][0].reshape(16, 128).T
    mlv[:, :, 4] = inp['ml_skip'][0].reshape(16, 128).T
    mlv[:, :, 5] = inp['ml_norm_g'][0].reshape(16, 128).T
    masks = np.zeros((128, 2, 128), np.float32)
    s = np.arange(128)[:, None]; t = np.arange(128)[None, :]
    masks[:, 0, :] = (s <= t)
    masks[:, 1, :] = (s >= t)
    return dict(wg=wg, bg_bc=bg, mlvec=mlv, masks=masks, ml_w_in=inp['ml_w_in'][0], ml_wq=inp['ml_wq'][0], ml_wk=inp['ml_wk'][0],
                ml_wv=inp['ml_wv'][0], ml_w_out=inp['ml_w_out'][0])


def host_inputs(inp, b):
    c = inp['c'][b]
    cc = inp['c_ctx']
    cvec = np.stack([c.reshape(8, 128).T, cc.reshape(8, 128).T], axis=-1).astype(np.float32)
    mb = inp['mod_b']
    modb_col = np.ascontiguousarray(np.broadcast_to(mb.reshape(2, 48, 128).transpose(2, 0, 1)[..., None], (128, 2, 48, 2))).astype(np.float32)
    modb_bc = np.ascontiguousarray(np.broadcast_to(mb[None], (128, 2, 6144))).astype(np.float32)
    lng = np.ascontiguousarray(np.broadcast_to(inp['ln_g'].reshape(1, 4, 1024), (128, 4, 1024))).astype(np.float32)
    lnb = np.ascontiguousarray(np.broadcast_to(inp['ln_b'].reshape(1, 4, 1024), (128, 4, 1024))).astype(np.float32)
    return dict(x=np.ascontiguousarray(inp['x'][b]), ctx=np.ascontiguousarray(inp['ctx'][b]), cvec=np.ascontiguousarray(cvec),
                modb_col=modb_col, modb_bc=modb_bc, lng_bc=lng, lnb_bc=lnb)


_NC_CACHE = {}


def kernel(**inputs):
    inp = {k_: np.asarray(v, dtype=np.float32) for k_, v in inputs.items()}
    if 'nc' not in _NC_CACHE:
        _NC_CACHE['nc'] = build(dict(full=True))
    nc = _NC_CACHE['nc']
    poolm, pc, invc, invcc = pool_consts()
    shared = dict(mod_w=inp['mod_w'], ident=np.eye(128, dtype=np.float32), ffn_w_in=inp['ffn_w_in'], ffn_w_out=inp['ffn_w_out'],
                  ab_w_in=inp['ab_w_in'][0], ab_w_out=inp['ab_w_out'][0], ab_pool_w=inp['ab_pool_w'][0],
                  poolm=poolm, poolc=pc, invc=invc, invcc=invcc, abvec=abvec(inp))
    shared.update(ml_consts(inp))
    shared = {k_: np.ascontiguousarray(v, dtype=np.float32) for k_, v in shared.items()}
    in_maps = []
    for b in range(8):
        m = dict(shared)
        m.update(host_inputs(inp, b))
        in_maps.append(m)
    res = run_bass_kernel_spmd(nc, in_maps, core_ids=list(range(8)))
    return np.stack([np.asarray(r["out"], dtype=np.float32) for r in res.results], axis=0)
```

```python
import numpy as np
import concourse.bass as bass
import concourse.mybir as mybir
from concourse.bass_utils import run_bass_kernel_spmd
from contextlib import ExitStack

F32 = mybir.dt.float32
BF16 = mybir.dt.bfloat16
ALU = mybir.AluOpType
AF = mybir.ActivationFunctionType
AX = mybir.AxisListType

D = 1024
T = 4096
TC = 256
NT = T + TC
FH = 2816
E = 2048
NH = 4
DH = 512
CH = 64
NCH = NT // CH
ALPHA = (2.0 * 2) ** 0.25
EPS = 1e-5
POOL_W = (2, 4, 8, 16)


class Buf:
    __slots__ = ('name', 'w', 'r', 'dsem', 'dtot')

    def __init__(self, name):
        self.name = name
        self.w = None
        self.r = {}
        self.dsem = None
        self.dtot = 0


class K:
    def __init__(self, nc):
        self.nc = nc
        self.engs = {'pe': nc.tensor, 'act': nc.scalar, 'dve': nc.vector, 'pool': nc.gpsimd, 'sp': nc.sync}
        self.sem = {e: nc.alloc_semaphore("s_" + e) for e in ('pe', 'act', 'dve', 'pool')}
        self.cnt = {e: 0 for e in self.sem}
        self.known = {e: {} for e in self.engs}
        self.free_dsems = []
        self.live = []
        self.nsem = 0
        self.noreuse = set()

    def _need(self, eng, dep, waits):
        if dep is None:
            return
        if dep[0] == 'e':
            src, c = dep[1], dep[2]
            if src == eng and eng == 'pe':
                return
            key = ('e', src)
            sem = self.sem[src]
        else:
            sem, c = dep[1], dep[2]
            key = ('d', id(sem))
        if self.known[eng].get(key, 0) >= c:
            return
        if key in waits and waits[key][1] >= c:
            return
        waits[key] = (sem, c)

    def _deps(self, eng, reads, writes):
        waits = {}
        for b in reads:
            self._need(eng, b.w, waits)
        for b in writes:
            self._need(eng, b.w, waits)
            for v in b.r.values():
                self._need(eng, v, waits)
        e = self.engs[eng]
        for key, (sem, c) in waits.items():
            self.known[eng][key] = c
            e.wait_ge(sem, c)

    def op(self, eng, fns, reads=(), writes=()):
        if not isinstance(fns, (list, tuple)):
            fns = [fns]
        self._deps(eng, reads, writes)
        e = self.engs[eng]
        for f in fns[:-1]:
            f(e)
        self.cnt[eng] += 1
        c = self.cnt[eng]
        fns[-1](e).then_inc(self.sem[eng], 1)
        dep = ('e', eng, c)
        for b in writes:
            b.w = dep
            b.r = {}
        for b in reads:
            b.r[('e', eng)] = dep

    def _dsem(self, sb):
        if sb.dsem is None:
            if self.free_dsems:
                sb.dsem, sb.dtot = self.free_dsems.pop()
            else:
                sb.dsem = self.nc.alloc_semaphore("d%d" % self.nsem)
                self.nsem += 1
                sb.dtot = 0
            self.live.append(sb)

    def dma(self, qeng, out_ap, in_ap, sb, reads=(), writes=(), **kw):
        if qeng == 'pool':
            assert sb.dsem is None
            sb.dsem = self.nc.alloc_semaphore("w%d" % self.nsem)
            self.nsem += 1
            sb.dtot = 0
            self.live.append(sb)
            self.noreuse.add(id(sb))
        self._dsem(sb)
        e = self.engs[qeng]
        if sb.dtot > 0:
            key = ('d', id(sb.dsem))
            if self.known[qeng].get(key, 0) < sb.dtot:
                self.known[qeng][key] = sb.dtot
                e.wait_ge(sb.dsem, sb.dtot)
        self._deps(qeng, reads, writes)
        sb.dtot += 16
        e.dma_start(out=out_ap, in_=in_ap, **kw).then_inc(sb.dsem, 16)
        dep = ('d', sb.dsem, sb.dtot)
        for b in writes:
            b.w = dep
            b.r = {}
        for b in reads:
            b.r[('d', id(sb.dsem))] = dep

    def barrier(self, release=()):
        for eng, e in self.engs.items():
            for src, sem in self.sem.items():
                c = self.cnt[src]
                key = ('e', src)
                if c > 0 and self.known[eng].get(key, 0) < c:
                    self.known[eng][key] = c
                    e.wait_ge(sem, c)
            for b in self.live:
                key = ('d', id(b.dsem))
                if b.dtot > 0 and self.known[eng].get(key, 0) < b.dtot:
                    self.known[eng][key] = b.dtot
                    e.wait_ge(b.dsem, b.dtot)
        for b in release:
            if b.dsem is not None:
                self.live.remove(b)
                if id(b) not in self.noreuse:
                    self.free_dsems.append((b.dsem, b.dtot))
                b.dsem = None


class Ctx:
    def __init__(self, nc):
        self.nc = nc
        self.k = K(nc)
        self.uid = 0

    def tile(self, es, shape, dt, name=None, bufs=None):
        self.uid += 1
        nm = "%s_%d" % (name or "t", self.uid)
        t = es.enter_context(self.nc.sbuf_tensor(nm, list(shape), dt))
        b = Buf(nm)
        if bufs is not None:
            bufs.append(b)
        return t, b

    def psum(self, es, shape, dt, name=None):
        self.uid += 1
        nm = "%s_%d" % (name or "ps", self.uid)
        t = es.enter_context(self.nc.psum_tensor(nm, list(shape), dt))
        return t, Buf(nm)


def build(dbg=None):
    dbg = dbg or {}
    nc = bass.Bass("TRN2", target_bir_lowering=False)
    cx = Ctx(nc)
    k = cx.k

    def din(name, shape, dt=F32):
        return nc.dram_tensor(name, list(shape), dt, kind="ExternalInput").ap()

    def dscr(name, shape, dt=F32):
        kind = "ExternalOutput" if name in dbg.get('outs', ()) else "Internal"
        return nc.dram_tensor(name, list(shape), dt, kind=kind).ap()

    x_d = din("x", [T, D])
    ctx_d = din("ctx", [TC, D])
    cvec_d = din("cvec", [128, 8, 2])
    modw_d = din("mod_w", [2, D, 6 * D])
    modbc_d = din("modb_col", [128, 2, 48, 2])
    modbb_d = din("modb_bc", [128, 2, 6 * D])
    lng_d = din("lng_bc", [128, 4, D])
    lnb_d = din("lnb_bc", [128, 4, D])
    ident_d = din("ident", [128, 128])
    ffn_win_d = din("ffn_w_in", [2, D, 2 * FH])
    ffn_wout_d = din("ffn_w_out", [2, FH, D])
    out_d = nc.dram_tensor("out", [T, D], F32, kind="ExternalOutput").ap()
    ab_win_d = din("ab_w_in", [D, 1536])
    ab_wout_d = din("ab_w_out", [D, D])
    ab_pw_d = din("ab_pool_w", [4, 128, 128])
    poolm_d = din("poolm", [128, 31, 512])
    poolc_d = din("poolc", [128, 8, 256])
    invc_d = din("invc", [128, 4, T])
    invcc_d = din("invcc", [128, 4, TC])
    abv_d = din("abvec", [128, 4, 36])

    ml_win_d = din("ml_w_in", [D, 2 * E])
    ml_wq_d = din("ml_wq", [NH, DH, DH])
    ml_wk_d = din("ml_wk", [NH, DH, DH])
    ml_wv_d = din("ml_wv", [NH, DH, DH])
    ml_wout_d = din("ml_w_out", [E, D])
    wg_d = din("wg", [128, 3, 16, 16])
    bg_d = din("bg_bc", [128, 16])
    mlv_d = din("mlvec", [128, 16, 6])
    masks_d = din("masks", [128, 2, 128])
    NTP = NT + 4
    XMT_d = dscr("XMT", [128, 16, NTP])
    P1_d = dscr("P1", [128, 16, T])
    P2_d = dscr("P2", [128, 16, T])
    QT_d = dscr("QT", [NT // 128, 128, 16 * 128], BF16)
    KT_d = dscr("KT", [NT // 128, 128, 16 * 128], BF16)
    Ktm_d = dscr("Ktm", [NT, E], BF16)
    Vtm_d = dscr("Vtm", [NT, E])
    G_d = dscr("G", [NT, 16])
    Gb_d = dscr("Gb", [NT, 16])
    HF_d = dscr("HF", [T, E])
    HB_d = dscr("HB", [T, E])
    H3_d = dscr("H3", [T, D])
    gbc_d = dscr("gbc", [6, 128, D])
    H1_d = din("H1", [NT, D]) if dbg.get('h1_in') else dscr("H1", [NT, D])
    H2_d = din("H2", [NT, D]) if dbg.get('h2_in') else dscr("H2", [NT, D])

    dram_bufs = {}

    def DB(name):
        if name not in dram_bufs:
            dram_bufs[name] = Buf(name)
        return dram_bufs[name]

    with ExitStack() as top:
        identF, B_identF = cx.tile(top, [128, 128], F32, "identF")
        identB, B_identB = cx.tile(top, [128, 128], BF16, "identB")
        modT, B_modT = cx.tile(top, [128, 2, 48, 2], F32, "modT")
        epsT, B_eps = cx.tile(top, [128, 1], F32, "eps")
        k.dma('sp', identF[:], ident_d[:, :], B_identF, writes=[B_identF])
        k.op('dve', lambda e: e.tensor_copy(out=identB[:], in_=identF[:]), reads=[B_identF], writes=[B_identB])
        k.op('pool', lambda e: e.memset(epsT[:], EPS), writes=[B_eps])

        def phase_mods():
            with ExitStack() as es:
                loc = []
                cv, B_cv = cx.tile(es, [128, 8, 2], F32, "cv", loc)
                scv, B_scv = cx.tile(es, [128, 8, 2], F32, "scv", loc)
                scb = [cx.tile(es, [128, 8, 128], F32, "scb", loc) for _ in range(2)]
                mbc, B_mbc = cx.tile(es, [128, 2, 48, 2], F32, "mbc", loc)
                wt = [cx.tile(es, [128, 8, 512], F32, "wt", loc) for _ in range(4)]
                mbb = [cx.tile(es, [128, 512], F32, "mbb", loc) for _ in range(2)]
                gts = {}
                pcs = [cx.psum(es, [128, 512], F32, "pc") for _ in range(2)]
                pgs = [cx.psum(es, [128, 512], F32, "pg") for _ in range(2)]
                k.dma('sp', cv[:], cvec_d[:, :, :], B_cv, writes=[B_cv])
                k.dma('sp', mbc[:], modbc_d[:, :, :, :], B_mbc, writes=[B_mbc])
                k.op('act', lambda e: e.activation(out=scv[:], in_=cv[:], func=AF.Silu), reads=[B_cv], writes=[B_scv])
                for r in range(2):
                    for kc in range(8):
                        k.op('dve', lambda e: e.tensor_copy(out=scb[r][0][:, kc, :],
                                                           in_=scv[:, kc, r:r + 1].to_broadcast([128, 128])),
                             reads=[B_scv], writes=[scb[r][1]])
                it = 0
                ig = 0

                def mods_load(q):
                    lq, jq = q // 12, q % 12
                    wvq = modw_d[lq].rearrange("(kc p) n -> p kc n", p=128)
                    k.dma('sp', wt[q % 4][0][:], wvq[:, :, jq * 512:(jq + 1) * 512], wt[q % 4][1], writes=[wt[q % 4][1]])

                for q in range(3):
                    mods_load(q)
                for l in range(2):
                    for jb in range(12):
                        w, B_w = wt[it % 4]
                        pc, B_pc = pcs[it % 2]
                        if it + 3 < 24:
                            mods_load(it + 3)
                        it += 1
                        fns = []
                        for jj in range(4):
                            for kc in range(8):
                                fns.append(lambda e, jj=jj, kc=kc: e.matmul(
                                    pc[:, jj * 2:(jj + 1) * 2], lhsT=w[:, kc, jj * 128:(jj + 1) * 128],
                                    rhs=scv[:, kc, 0:2], start=(kc == 0), stop=(kc == 7)))
                        k.op('pe', fns, reads=[B_w, B_scv], writes=[B_pc])
                        k.op('dve', lambda e: e.tensor_tensor(
                            out=modT[:, l, jb * 4:(jb + 1) * 4, :],
                            in0=pc[:, 0:8].rearrange("p (a b) -> p a b", b=2),
                            in1=mbc[:, l, jb * 4:(jb + 1) * 4, :], op=ALU.add),
                            reads=[B_mbc], writes=[B_pc, B_modT])
                        ch = jb // 2
                        if ch in (2, 5):
                            for r in range(2):
                                if r == 1 and l == 1:
                                    continue
                                key = (l, ch, r)
                                if key not in gts:
                                    gts[key] = cx.tile(es, [128, D], F32, "gt", loc)
                                gt, B_gt = gts[key]
                                pg, B_pg = pgs[ig % 2]
                                mb, B_mb = mbb[ig % 2]
                                ig += 1
                                k.dma('sp', mb[:], modbb_d[:, l, jb * 512:(jb + 1) * 512], B_mb, writes=[B_mb])
                                fns = [lambda e, kc=kc: e.matmul(pg[:], lhsT=scb[r][0][:, kc, :], rhs=w[:, kc, :],
                                                                 start=(kc == 0), stop=(kc == 7)) for kc in range(8)]
                                k.op('pe', fns, reads=[B_w, scb[r][1]], writes=[B_pg])
                                hf = jb % 2
                                k.op('dve', lambda e: e.tensor_tensor(out=gt[:, hf * 512:(hf + 1) * 512], in0=pg[:],
                                                                     in1=mb[:], op=ALU.add),
                                     reads=[B_mb], writes=[B_pg, B_gt])
                                if hf == 1:
                                    idx = {(0, 2, 0): 0, (0, 5, 0): 1, (0, 2, 1): 2, (0, 5, 1): 3,
                                           (1, 2, 0): 4, (1, 5, 0): 5}[key]
                                    k.dma('sp', gbc_d[idx], gt[:], B_gt, reads=[B_gt], writes=[DB("gbc%d" % idx)])
                    for c0 in (8, 32):
                        k.op('dve', lambda e: e.tensor_scalar(out=modT[:, l, c0:c0 + 8, :], in0=modT[:, l, c0:c0 + 8, :],
                                                             scalar1=1.0, scalar2=None, op0=ALU.add),
                             reads=[B_modT], writes=[B_modT])
                k.barrier(release=loc)

        def post_norm_g(psA, B_psA, psB, B_psB, res, B_res, gate, B_gate, lng, B_lng, lnb, B_lnb,
                        t1, B_t1, sm, B_sm, out_ap, out_buf):
            k.op('dve', lambda e: e.tensor_tensor(out=t1[:, 0:512], in0=psA, in1=gate[:, 0:512], op=ALU.mult),
                 reads=[B_gate], writes=[B_psA, B_t1])
            k.op('dve', lambda e: e.tensor_tensor(out=t1[:, 512:1024], in0=psB, in1=gate[:, 512:1024], op=ALU.mult),
                 reads=[B_gate], writes=[B_psB, B_t1])
            yield
            k.op('dve', lambda e: e.scalar_tensor_tensor(out=t1[:], in0=res, scalar=ALPHA, in1=t1[:],
                                                        op0=ALU.mult, op1=ALU.add),
                 reads=[B_res], writes=[B_t1])
            yield
            k.op('dve', [lambda e: e.bn_stats(out=sm[:, 0:6], in_=t1[:, 0:512]),
                         lambda e: e.bn_stats(out=sm[:, 6:12], in_=t1[:, 512:1024])],
                 reads=[B_t1], writes=[B_sm])
            yield
            k.op('dve', lambda e: e.bn_aggr(out=sm[:, 12:14], in_=sm[:, 0:12]), reads=[B_sm], writes=[B_sm])
            yield
            k.op('act', lambda e: e.activation(out=sm[:, 14:15], in_=sm[:, 13:14], func=AF.Sqrt, bias=epsT[:, 0:1],
                                               scale=1.0), reads=[B_sm, B_eps], writes=[B_sm])
            yield
            k.op('dve', lambda e: e.reciprocal(out=sm[:, 14:15], in_=sm[:, 14:15]), reads=[B_sm], writes=[B_sm])
            yield
            k.op('dve', lambda e: e.scalar_tensor_tensor(out=sm[:, 15:16], in0=sm[:, 12:13], scalar=-1.0,
                                                        in1=sm[:, 14:15], op0=ALU.mult, op1=ALU.mult),
                 reads=[B_sm], writes=[B_sm])
            yield
            k.op('act', lambda e: e.activation(out=t1[:], in_=t1[:], func=AF.Identity, scale=sm[:, 14:15],
                                               bias=sm[:, 15:16]), reads=[B_sm, B_t1], writes=[B_t1])
            yield
            k.op('dve', lambda e: e.tensor_tensor(out=t1[:], in0=t1[:], in1=lng, op=ALU.mult),
                 reads=[B_lng], writes=[B_t1])
            yield
            k.op('dve', lambda e: e.tensor_tensor(out=t1[:], in0=t1[:], in1=lnb, op=ALU.add),
                 reads=[B_lnb], writes=[B_t1])
            yield
            k.dma('sp', out_ap, t1[:], B_t1, reads=[B_t1], writes=[out_buf])

        def post_norm(*args):
            for _ in post_norm_g(*args):
                pass

        def phase_ffn(l, Hin, Hin_name, Hout, Hout_name, groups):
            NTK = 256
            with ExitStack() as es:
                loc = []
                fwin = [cx.tile(es, [128, 8, 1408], BF16, "fwin", loc) for _ in range(4)]
                fwout = [cx.tile(es, [128, 11, D], BF16, "fwout", loc) for _ in range(2)]
                wi = ffn_win_d[l].rearrange("(kc p) n -> p kc n", p=128)
                wo = ffn_wout_d[l].rearrange("(f p) n -> p f n", p=128)
                for pi in (0, 2, 1, 3):
                    k.dma('pool', fwin[pi][0][:], wi[:, :, pi * 1408:(pi + 1) * 1408], fwin[pi][1],
                          writes=[fwin[pi][1]])
                for pi in range(2):
                    k.dma('pool', fwout[pi][0][:], wo[:, pi * 11:(pi + 1) * 11, :], fwout[pi][1],
                          writes=[fwout[pi][1]])
                gates = {}
                for r in sorted(set(g[2] for g in groups)):
                    gt, B_gt = cx.tile(es, [128, D], F32, "gf", loc)
                    idx = {(0, 0): 1, (0, 1): 3, (1, 0): 5}[(l, r)]
                    k.dma('sp', gt[:], gbc_d[idx], B_gt, reads=[DB("gbc%d" % idx)], writes=[B_gt])
                    gates[r] = (gt, B_gt)
                lng, B_lng = cx.tile(es, [128, D], F32, "lng", loc)
                lnb, B_lnb = cx.tile(es, [128, D], F32, "lnb", loc)
                k.dma('sp', lng[:], lng_d[:, l * 2 + 1, :], B_lng, writes=[B_lng])
                k.dma('sp', lnb[:], lnb_d[:, l * 2 + 1, :], B_lnb, writes=[B_lnb])
                xts = [cx.tile(es, [128, D], F32, "xt", loc) for _ in range(4)]
                t1s = [cx.tile(es, [128, D], F32, "t1", loc) for _ in range(2)]
                sms = [cx.tile(es, [128, 16], F32, "sm", loc) for _ in range(2)]
                uTs = [cx.tile(es, [128, 8, NTK], BF16, "uT", loc) for _ in range(2)]
                actT, B_actT = cx.tile(es, [128, 22, NTK], BF16, "actT", loc)
                sAs = [cx.tile(es, [128, NTK], F32, "sA", loc) for _ in range(4)]
                ptA = cx.psum(es, [128, 512], F32, "ptA")
                ptB = cx.psum(es, [128, 512], F32, "ptB")
                pabs = [cx.psum(es, [128, 512], F32, "pab") for _ in range(4)]
                pos = [cx.psum(es, [128, 512], F32, "po") for _ in range(2)]
                ix = 0
                ip = 0
                def ffn_loads(gq):
                    rin_ = groups[gq][0]
                    for s in range(2):
                        xt, B_xt = xts[(2 * gq + s) % 4]
                        k.dma('sp', xt[:], Hin[rin_ + s * 128: rin_ + (s + 1) * 128, :], B_xt,
                              reads=[DB(Hin_name)], writes=[B_xt])

                cnf = dict(ip=0)

                def ffn_group(gi):
                    rin, rout, r = groups[gi]
                    uT, B_uT = uTs[gi % 2]
                    ffn_loads(gi)
                    yield
                    xs = []
                    for s in range(2):
                        xt, B_xt = xts[(2 * gi + s) % 4]
                        xs.append((xt, B_xt))
                        for (pt, B_pt), k0 in ((ptA, 0), (ptB, 4)):
                            k.op('pe', [lambda e, kk=kk: e.transpose(out=pt[:, kk * 128:(kk + 1) * 128],
                                                                    in_=xt[:, (k0 + kk) * 128:(k0 + kk + 1) * 128],
                                                                    identity=identF[:]) for kk in range(4)],
                                 reads=[B_xt, B_identF], writes=[B_pt])
                            yield
                            for kk in range(4):
                                kc = k0 + kk
                                k.op('act', lambda e: e.activation(
                                    out=uT[:, kc, s * 128:(s + 1) * 128], in_=pt[:, kk * 128:(kk + 1) * 128],
                                    func=AF.Identity, scale=modT[:, l, 32 + kc, r:r + 1],
                                    bias=modT[:, l, 24 + kc, r:r + 1]),
                                    reads=[B_modT], writes=[B_pt, B_uT])
                                yield
                    while gi > 0 and not cnf.get(('out', gi - 1)):
                        yield
                    for f in range(22):
                        ip = cnf['ip']
                        pab, B_pa = pabs[ip % 4]
                        B_pg = B_pa
                        pa = pab[:, 0:256]
                        pg = pab[:, 256:512]
                        sA, B_sA = sAs[ip % 4]
                        cnf['ip'] += 1
                        pi = f // 11
                        c0 = (f % 11) * 128
                        wa, B_wa = fwin[pi]
                        wg, B_wg = fwin[2 + pi]
                        k.op('pe', [lambda e, kc=kc: e.matmul(pa[:, 0:NTK], lhsT=wa[:, kc, c0:c0 + 128],
                                                              rhs=uT[:, kc, :], start=(kc == 0), stop=(kc == 7))
                                    for kc in range(8)] +
                                   [lambda e, kc=kc: e.matmul(pg[:, 0:NTK], lhsT=wg[:, kc, c0:c0 + 128],
                                                              rhs=uT[:, kc, :], start=(kc == 0), stop=(kc == 7))
                                    for kc in range(8)], reads=[B_wa, B_wg, B_uT], writes=[B_pa])
                        yield
                        k.op('act', lambda e: e.activation(out=sA[:], in_=pa[:, 0:NTK], func=AF.Silu),
                             writes=[B_pa, B_sA])
                        yield
                        k.op('dve', lambda e: e.tensor_tensor(out=actT[:, f, :], in0=pg[:, 0:NTK], in1=sA[:],
                                                             op=ALU.mult), reads=[B_sA], writes=[B_pg, B_actT])
                        yield
                    for s in range(2):
                        for hf in range(2):
                            po, B_po = pos[hf]
                            k.op('pe', [lambda e, f=f: e.matmul(po[:], lhsT=actT[:, f, s * 128:(s + 1) * 128],
                                                                rhs=fwout[f // 11][0][:, f % 11, hf * 512:(hf + 1) * 512],
                                                                start=(f == 0), stop=(f == 21)) for f in range(22)],
                                 reads=[B_actT, fwout[0][1], fwout[1][1]], writes=[B_po])
                            yield
                        if s == 1:
                            cnf[('out', gi)] = True
                        t1, B_t1 = t1s[s]
                        sm, B_sm = sms[s]
                        gt, B_gt = gates[r]
                        yield from post_norm_g(pos[0][0][:], pos[0][1], pos[1][0][:], pos[1][1], xs[s][0][:], xs[s][1],
                                               gt, B_gt, lng[:], B_lng, lnb[:], B_lnb, t1, B_t1, sm, B_sm,
                                               Hout[rout + s * 128: rout + (s + 1) * 128, :], DB(Hout_name))

                run_pipelined(ffn_group, len(groups), depth=2, skew=70)
                k.barrier(release=loc)

        def xrows(n0, nrows):
            if n0 < TC:
                return ctx_d[n0:n0 + nrows, :]
            return x_d[n0 - TC:n0 - TC + nrows, :]

        def make_uT_g(l, r, jsh, jsc, src_ap, B_src_dram, xt, B_xt, ptA, ptB, uT, B_uT, s):
            if src_ap is not None:
                k.dma('sp', xt[:], src_ap, B_xt, reads=B_src_dram, writes=[B_xt])
            for (pt, B_pt), k0 in ((ptA, 0), (ptB, 4)):
                k.op('pe', [lambda e, kk=kk: e.transpose(out=pt[:, kk * 128:(kk + 1) * 128],
                                                        in_=xt[:, (k0 + kk) * 128:(k0 + kk + 1) * 128],
                                                        identity=identF[:]) for kk in range(4)],
                     reads=[B_xt, B_identF], writes=[B_pt])
                yield
                for kk in range(4):
                    kc = k0 + kk
                    k.op('act', lambda e: e.activation(
                        out=uT[:, kc, s * 128:(s + 1) * 128], in_=pt[:, kk * 128:(kk + 1) * 128],
                        func=AF.Identity, scale=modT[:, l, jsc + kc, r:r + 1], bias=modT[:, l, jsh + kc, r:r + 1]),
                        reads=[B_modT], writes=[B_pt, B_uT])
                    yield

        def make_uT(*args):
            for _ in make_uT_g(*args):
                pass

        L0_GROUPS = [(0, 2, 1)] + [(TC + g * 512, 4, 0) for g in range(8)]
        POOL_D = {0: list(range(-1, 4)), 1: list(range(-1, 5)), 2: list(range(-2, 6)), 3: list(range(-4, 8))}
        POOL_IDX = {}
        for gi in range(4):
            for dl in POOL_D[gi]:
                POOL_IDX[(gi, dl)] = len(POOL_IDX)

        def layer0_mixer(groups):
            with ExitStack() as l0:
                keep = []
                A_tm, B_A = cx.tile(l0, [128, NT // 128, 512], BF16, "A_tm", keep)
                YB, B_YB = cx.tile(l0, [128, 4, NT], BF16, "YB", keep)
                abv, B_abv = cx.tile(l0, [128, 4, 36], F32, "abv", keep)
                k.dma('sp', abv[:], abv_d[:, :, :], B_abv, writes=[B_abv])
                lg = ExitStack()
                keepg = []
                with lg:
                    GLU, B_GLU = cx.tile(lg, [128, 4, T + 30], BF16, "GLU", keepg)
                    GLUc, B_GLUc = cx.tile(lg, [128, 4, TC + 30], BF16, "GLUc", keepg)
                    k.op('pool', lambda e: e.memset(GLU[:], 0.0), writes=[B_GLU])
                    k.op('pool', lambda e: e.memset(GLUc[:], 0.0), writes=[B_GLUc])
                    with ExitStack() as es:
                        loc = []
                        win, B_win = cx.tile(es, [128, 8, 1536], BF16, "win", loc)
                        k.dma('pool', win[:], ab_win_d.rearrange("(kc p) n -> p kc n", p=128), B_win, writes=[B_win])
                        xts = [cx.tile(es, [128, D], F32, "xt", loc) for _ in range(8)]
                        uTs = [cx.tile(es, [128, 8, 512], BF16, "uT", loc) for _ in range(2)]
                        sgs = [cx.tile(es, [128, 512], F32, "sg", loc) for _ in range(2)]

                        def a0_loads(gq):
                            nq, nsq, _ = groups[gq]
                            for sq in range(nsq):
                                k.dma('sp', xts[(gq % 2) * 4 + sq][0][:], xrows(nq + sq * 128, 128), xts[(gq % 2) * 4 + sq][1],
                                      writes=[xts[(gq % 2) * 4 + sq][1]])

                        ptA = cx.psum(es, [128, 512], F32, "ptA")
                        ptB = cx.psum(es, [128, 512], F32, "ptB")
                        pas = [cx.psum(es, [128, 512], F32, "pa") for _ in range(2)]
                        pvs = [cx.psum(es, [128, 512], F32, "pv") for _ in range(2)]
                        pgs = [cx.psum(es, [128, 512], F32, "pg") for _ in range(2)]
                        cna = dict(ia=0, iv=0)

                        def a0_group(gidx):
                            n0, nsub, r = groups[gidx]
                            ntok = nsub * 128
                            uT, B_uT = uTs[gidx % 2]
                            a0_loads(gidx)
                            yield
                            yield from acquire(cna, 'st1')
                            for s in range(nsub):
                                xt, B_xt = xts[(gidx % 2) * 4 + s]
                                yield from make_uT_g(0, r, 0, 8, None, [], xt, B_xt, ptA, ptB, uT, B_uT, s)
                            cna['st1'] = False
                            yield from acquire(cna, 'st2')
                            for s in range(nsub):
                                pa, B_pa = pas[cna['ia'] % 2]
                                cna['ia'] += 1
                                k.op('pe', [lambda e, kc=kc: e.matmul(pa[:], lhsT=uT[:, kc, s * 128:(s + 1) * 128],
                                                                      rhs=win[:, kc, 0:512], start=(kc == 0), stop=(kc == 7))
                                            for kc in range(8)], reads=[B_uT, B_win], writes=[B_pa])
                                yield
                                ti = n0 // 128 + s
                                k.op('act', lambda e: e.copy(out=A_tm[:, ti, :], in_=pa[:]), writes=[B_pa, B_A])
                                yield
                            Gt, B_Gt, goff = (GLUc, B_GLUc, 15 + n0) if r == 1 else (GLU, B_GLU, 15 + n0 - TC)
                            for jc in range(4):
                                iv = cna['iv']
                                pv, B_pv = pvs[iv % 2]
                                pg, B_pg = pgs[iv % 2]
                                sg, B_sg = sgs[iv % 2]
                                cna['iv'] += 1
                                k.op('pe', [lambda e, kc=kc: e.matmul(pv[:, 0:ntok], lhsT=win[:, kc, 512 + jc * 128:512 + (jc + 1) * 128],
                                                                      rhs=uT[:, kc, 0:ntok], start=(kc == 0), stop=(kc == 7))
                                            for kc in range(8)], reads=[B_uT, B_win], writes=[B_pv])
                                k.op('pe', [lambda e, kc=kc: e.matmul(pg[:, 0:ntok], lhsT=win[:, kc, 1024 + jc * 128:1024 + (jc + 1) * 128],
                                                                      rhs=uT[:, kc, 0:ntok], start=(kc == 0), stop=(kc == 7))
                                            for kc in range(8)], reads=[B_uT, B_win], writes=[B_pg])
                                yield
                                k.op('act', lambda e: e.activation(out=sg[:, 0:ntok], in_=pg[:, 0:ntok], func=AF.Sigmoid),
                                     writes=[B_pg, B_sg])
                                yield
                                k.op('dve', lambda e: e.tensor_tensor(out=Gt[:, jc, goff:goff + ntok], in0=pv[:, 0:ntok],
                                                                     in1=sg[:, 0:ntok], op=ALU.mult),
                                     reads=[B_sg], writes=[B_pv, B_Gt])
                                yield
                            cna['st2'] = False

                        run_pipelined(a0_group, len(groups), depth=2, skew=1)
                        k.barrier(release=loc)
                    with ExitStack() as es:
                        loc = []
                        DG, B_DG = cx.tile(es, [128, 4 * 31, 128], BF16, "DG", loc)
                        for jc in range(4):
                            k.op('dve', lambda e: e.tensor_tensor(
                                out=DG[:, jc * 31:(jc + 1) * 31, :], in0=identF[:].unsqueeze(1).to_broadcast([128, 31, 128]),
                                in1=abv[:, jc, 0:31].unsqueeze(2).to_broadcast([128, 31, 128]),
                                op=ALU.mult), reads=[B_identF, B_abv], writes=[B_DG])
                        onesF, B_ones = cx.tile(es, [128, 128], F32, "ones", loc)
                        k.op('pool', lambda e: e.memset(onesF[:], 1.0), writes=[B_ones])
                        CVs = [cx.tile(es, [128, 4, 512], F32, "CV", loc) for _ in range(2)]
                        SQs = [cx.tile(es, [128, 4, 512], F32, "SQ", loc) for _ in range(2)]
                        MEs = [cx.tile(es, [128, 512], F32, "ME", loc) for _ in range(2)]
                        M2s = [cx.tile(es, [128, 512], F32, "M2", loc) for _ in range(2)]
                        VARs = [cx.tile(es, [128, 512], F32, "VAR", loc) for _ in range(2)]
                        xcs = [cx.tile(es, [128, 512], F32, "xc", loc) for _ in range(2)]
                        pcvs = [cx.psum(es, [128, 512], F32, "pcv") for _ in range(2)]
                        pS = cx.psum(es, [128, 512], F32, "pS")
                        pSS = cx.psum(es, [128, 512], F32, "pSS")
                        cnb = dict(ic=0)

                        def b0a_group(gidx):
                            n0, nsub, r = groups[gidx]
                            ntok = nsub * 128
                            CV, B_CV = CVs[gidx % 2]
                            SQ, B_SQ = SQs[gidx % 2]
                            ME, B_ME = MEs[gidx % 2]
                            M2, B_M2 = M2s[gidx % 2]
                            VAR, B_VAR = VARs[gidx % 2]
                            Gt, B_Gt, goff = (GLUc, B_GLUc, n0) if r == 1 else (GLU, B_GLU, n0 - TC)
                            yield from acquire(cnb, 'st1')
                            for jc in range(4):
                                pcv, B_pcv = pcvs[cnb['ic'] % 2]
                                cnb['ic'] += 1
                                k.op('pe', [lambda e, kk=kk: e.matmul(pcv[:, 0:ntok], lhsT=DG[:, jc * 31 + kk, :],
                                                                      rhs=Gt[:, jc, goff + kk:goff + kk + ntok],
                                                                      start=(kk == 0), stop=(kk == 30)) for kk in range(31)],
                                     reads=[B_DG, B_Gt], writes=[B_pcv])
                                yield
                                k.op('act', lambda e: e.activation(out=CV[:, jc, 0:ntok], in_=pcv[:, 0:ntok], func=AF.Identity,
                                                                   bias=abv[:, jc, 31:32], scale=1.0),
                                     reads=[B_abv], writes=[B_pcv, B_CV])
                                yield
                                k.op('dve', lambda e: e.tensor_tensor(out=SQ[:, jc, 0:ntok], in0=CV[:, jc, 0:ntok],
                                                                      in1=CV[:, jc, 0:ntok], op=ALU.mult),
                                     reads=[B_CV], writes=[B_SQ])
                                yield
                            cnb['st1'] = False
                            yield from acquire(cnb, 'st2')
                            k.op('pe', [lambda e, jc=jc: e.matmul(pS[0][:, 0:ntok], lhsT=onesF[:], rhs=CV[:, jc, 0:ntok],
                                                                  start=(jc == 0), stop=(jc == 3)) for jc in range(4)],
                                 reads=[B_ones, B_CV], writes=[pS[1]])
                            k.op('pe', [lambda e, jc=jc: e.matmul(pSS[0][:, 0:ntok], lhsT=onesF[:], rhs=SQ[:, jc, 0:ntok],
                                                                  start=(jc == 0), stop=(jc == 3)) for jc in range(4)],
                                 reads=[B_ones, B_SQ], writes=[pSS[1]])
                            yield
                            k.op('dve', lambda e: e.tensor_scalar(out=ME[:, 0:ntok], in0=pS[0][:, 0:ntok], scalar1=1.0 / 512,
                                                                 scalar2=None, op0=ALU.mult), writes=[pS[1], B_ME])
                            yield
                            k.op('dve', lambda e: e.tensor_tensor(out=M2[:, 0:ntok], in0=ME[:, 0:ntok], in1=ME[:, 0:ntok],
                                                                  op=ALU.mult), reads=[B_ME], writes=[B_M2])
                            yield
                            k.op('dve', lambda e: e.scalar_tensor_tensor(out=VAR[:, 0:ntok], in0=pSS[0][:, 0:ntok],
                                                                        scalar=1.0 / 512, in1=M2[:, 0:ntok],
                                                                        op0=ALU.mult, op1=ALU.subtract),
                                 reads=[B_M2], writes=[pSS[1], B_VAR])
                            yield
                            k.op('act', lambda e: e.activation(out=VAR[:, 0:ntok], in_=VAR[:, 0:ntok], func=AF.Sqrt,
                                                               bias=epsT[:, 0:1], scale=1.0), reads=[B_eps], writes=[B_VAR])
                            yield
                            k.op('dve', lambda e: e.reciprocal(out=VAR[:, 0:ntok], in_=VAR[:, 0:ntok]), writes=[B_VAR])
                            yield
                            for jc in range(4):
                                xc, B_xc = xcs[jc % 2]
                                k.op('dve', lambda e: e.tensor_tensor(out=xc[:, 0:ntok], in0=CV[:, jc, 0:ntok], in1=ME[:, 0:ntok],
                                                                     op=ALU.subtract), reads=[B_CV, B_ME], writes=[B_xc])
                                yield
                                k.op('dve', lambda e: e.tensor_tensor(out=xc[:, 0:ntok], in0=xc[:, 0:ntok], in1=VAR[:, 0:ntok],
                                                                      op=ALU.mult), reads=[B_VAR], writes=[B_xc])
                                yield
                                k.op('act', lambda e: e.activation(out=YB[:, jc, n0:n0 + ntok], in_=xc[:, 0:ntok], func=AF.Silu,
                                                                   scale=abv[:, jc, 32:33], bias=abv[:, jc, 33:34]),
                                     reads=[B_xc, B_abv], writes=[B_YB])
                                yield
                            cnb['st2'] = False

                        run_pipelined(b0a_group, len(groups), depth=2, skew=1)
                        k.barrier(release=loc)
                    k.barrier(release=keepg)
                with ExitStack() as es:
                    loc = []
                    poolm, B_poolm = cx.tile(es, [128, 31, 512], BF16, "poolm", loc)
                    poolc, B_poolc = cx.tile(es, [128, 8, 256], BF16, "poolc", loc)
                    pw, B_pw = cx.tile(es, [128, 4, 128], BF16, "pw", loc)
                    wout, B_wout = cx.tile(es, [128, 8, D], BF16, "wout", loc)
                    k.dma('pool', poolm[:], poolm_d[:, :, :], B_poolm, writes=[B_poolm])
                    k.dma('pool', poolc[:], poolc_d[:, :, :], B_poolc, writes=[B_poolc])
                    k.dma('pool', pw[:], ab_pw_d.rearrange("g c d -> c g d"), B_pw, writes=[B_pw])
                    k.dma('pool', wout[:], ab_wout_d.rearrange("(kc p) n -> p kc n", p=128), B_wout, writes=[B_wout])
                    gates = {}
                    for r in sorted(set(g[2] for g in groups)):
                        gt, B_gt = cx.tile(es, [128, D], F32, "gm", loc)
                        idx = {0: 0, 1: 2}[r]
                        k.dma('sp', gt[:], gbc_d[idx], B_gt, reads=[DB("gbc%d" % idx)], writes=[B_gt])
                        gates[r] = (gt, B_gt)
                    lng, B_lng = cx.tile(es, [128, D], F32, "lng", loc)
                    lnb, B_lnb = cx.tile(es, [128, D], F32, "lnb", loc)
                    k.dma('sp', lng[:], lng_d[:, 0, :], B_lng, writes=[B_lng])
                    k.dma('sp', lnb[:], lnb_d[:, 0, :], B_lnb, writes=[B_lnb])
                    invs = [cx.tile(es, [128, 4, 512], F32, "inv", loc) for _ in range(2)]
                    YAs = [cx.tile(es, [128, 4, 512], BF16, "YA", loc) for _ in range(2)]
                    dTs = [cx.tile(es, [128, 4, 512], BF16, "dT", loc) for _ in range(2)]
                    tqs = [cx.tile(es, [128, 512], F32, "tq", loc) for _ in range(2)]
                    sgs = [cx.tile(es, [128, 512], F32, "sg", loc) for _ in range(2)]
                    xts = [cx.tile(es, [128, D], F32, "xt", loc) for _ in range(2)]
                    t1s = [cx.tile(es, [128, D], F32, "t1", loc) for _ in range(2)]
                    sms = [cx.tile(es, [128, 16], F32, "sm", loc) for _ in range(2)]
                    pbs = [cx.psum(es, [128, 512], F32, "pb") for _ in range(2)]
                    psgs = [cx.psum(es, [128, 512], F32, "psg") for _ in range(2)]
                    pws = [cx.psum(es, [128, 512], F32, "pw") for _ in range(2)]
                    pos = [cx.psum(es, [128, 512], F32, "po") for _ in range(2)]
                    ib = 0
                    ix = 0
                    def b0b_loads(gq):
                        nq, nsq, rq = groups[gq]
                        inv_, B_inv_ = invs[gq % 2]
                        if rq == 1:
                            k.dma('sp', inv_[:, :, 0:nsq * 128], invcc_d[:, :, :], B_inv_, writes=[B_inv_])
                        else:
                            k.dma('sp', inv_[:, :, 0:nsq * 128], invc_d[:, :, nq - TC:nq - TC + nsq * 128], B_inv_, writes=[B_inv_])

                    cnp = dict(ib=0, ix=0)

                    def b0b_group(gidx):
                        n0, nsub, r = groups[gidx]
                        ntok = nsub * 128
                        inv, B_inv = invs[gidx % 2]
                        YA, B_YA = YAs[gidx % 2]
                        dT, B_dT = dTs[gidx % 2]
                        b0b_loads(gidx)
                        yield
                        yield from acquire(cnp, 'st1')
                        for gi in range(4):
                            ib = cnp['ib']
                            pb, B_pb = pbs[ib % 2]
                            psg, B_psg = psgs[ib % 2]
                            pwp, B_pwp = pws[ib % 2]
                            tq, B_tq = tqs[ib % 2]
                            sg, B_sg = sgs[ib % 2]
                            cnp['ib'] += 1
                            if r == 1:
                                srcs = [(st, poolc[:, gi * 2 + st, 0:ntok]) for st in range(2)]
                                Bm = B_poolc
                            else:
                                g = (n0 - TC) // 512
                                srcs = []
                                for dl in POOL_D[gi]:
                                    st = 4 * g + dl
                                    if 0 <= st < 32:
                                        srcs.append((2 + st, poolm[:, POOL_IDX[(gi, dl)], 0:ntok]))
                                Bm = B_poolm
                            k.op('pe', [lambda e, i=i: e.matmul(pb[:, 0:ntok], lhsT=A_tm[:, srcs[i][0], gi * 128:(gi + 1) * 128],
                                                                rhs=srcs[i][1], start=(i == 0), stop=(i == len(srcs) - 1))
                                        for i in range(len(srcs))], reads=[B_A, Bm], writes=[B_pb])
                            k.op('pe', [lambda e, s=s: e.matmul(psg[:, s * 128:(s + 1) * 128],
                                                                lhsT=A_tm[:, n0 // 128 + s, gi * 128:(gi + 1) * 128],
                                                                rhs=identB[:], start=True, stop=True) for s in range(nsub)],
                                 reads=[B_A, B_identB], writes=[B_psg])
                            yield
                            k.op('act', lambda e: e.activation(out=sg[:, 0:ntok], in_=psg[:, 0:ntok], func=AF.Copy, scale=-1.0),
                                 writes=[B_psg, B_sg])
                            k.op('dve', lambda e: e.tensor_tensor(out=tq[:, 0:ntok], in0=pb[:, 0:ntok], in1=inv[:, gi, 0:ntok],
                                                                 op=ALU.mult), reads=[B_inv], writes=[B_pb, B_tq])
                            yield
                            k.op('dve', lambda e: e.tensor_tensor(out=dT[:, gi, 0:ntok], in0=tq[:, 0:ntok], in1=sg[:, 0:ntok],
                                                                  op=ALU.add), reads=[B_tq, B_sg], writes=[B_dT])
                            yield
                            k.op('pe', lambda e: e.matmul(pwp[:, 0:ntok], lhsT=pw[:, gi, :], rhs=dT[:, gi, 0:ntok],
                                                          start=True, stop=True), reads=[B_pw, B_dT], writes=[B_pwp])
                            yield
                            k.op('dve', lambda e: e.tensor_scalar(out=YA[:, gi, 0:ntok], in0=pwp[:, 0:ntok],
                                                                 scalar1=abv[:, gi, 34:35], scalar2=abv[:, gi, 35:36],
                                                                 op0=ALU.add, op1=ALU.mult),
                                 reads=[B_abv], writes=[B_pwp, B_YA])
                            yield
                        cnp['st1'] = False
                        yield from acquire(cnp, 'st2')
                        for s in range(nsub):
                            ix = cnp['ix']
                            xt, B_xt = xts[ix % 2]
                            t1, B_t1 = t1s[ix % 2]
                            sm, B_sm = sms[ix % 2]
                            cnp['ix'] += 1
                            k.dma('sp', xt[:], xrows(n0 + s * 128, 128), B_xt, writes=[B_xt])
                            for hf in range(2):
                                po, B_po = pos[hf]
                                fns = []
                                for kc in range(8):
                                    lh = YA[:, kc, s * 128:(s + 1) * 128] if kc < 4 else \
                                        YB[:, kc - 4, n0 + s * 128:n0 + (s + 1) * 128]
                                    fns.append(lambda e, kc=kc, lh=lh: e.matmul(po[:], lhsT=lh,
                                                                               rhs=wout[:, kc, hf * 512:(hf + 1) * 512],
                                                                               start=(kc == 0), stop=(kc == 7)))
                                k.op('pe', fns, reads=[B_YA, B_YB, B_wout], writes=[B_po])
                                yield
                            gt, B_gt = gates[r]
                            yield from post_norm_g(pos[0][0][:], pos[0][1], pos[1][0][:], pos[1][1], xt[:], B_xt,
                                                   gt, B_gt, lng[:], B_lng, lnb[:], B_lnb, t1, B_t1, sm, B_sm,
                                                   H1_d[n0 + s * 128:n0 + (s + 1) * 128, :], DB("H1"))
                        cnp['st2'] = False

                    run_pipelined(b0b_group, len(groups), depth=2, skew=1)
                    k.barrier(release=loc)
                k.barrier(release=keep)

        def pcol(n):
            return 1 + n if n < TC else n + 3

        def jb_of(c):
            return 3 - c if c < 4 else 71 - c

        SCL = float(DH) ** -0.5

        def phase_a1a(groups):
            with ExitStack() as es:
                loc = []
                wins = [cx.tile(es, [128, 8, 1024], BF16, "mwin", loc) for _ in range(4)]
                wv_ = ml_win_d.rearrange("(kc p) n -> p kc n", p=128)
                for pi in range(4):
                    k.dma('pool', wins[pi][0][:], wv_[:, :, pi * 1024:(pi + 1) * 1024], wins[pi][1], writes=[wins[pi][1]])
                zt, B_zt = cx.tile(es, [128, 16, 2], F32, "zt", loc)
                k.op('pool', lambda e: e.memset(zt[:], 0.0), writes=[B_zt])
                k.dma('sp', XMT_d[:, :, 0:1], zt[:, :, 0:1], B_zt, reads=[B_zt], writes=[DB("XMT")], allow_slow_non_contiguous=True)
                k.dma('sp', XMT_d[:, :, 257:259], zt[:, :, 0:2], B_zt, reads=[B_zt], writes=[DB("XMT")], allow_slow_non_contiguous=True)
                k.dma('sp', XMT_d[:, :, NTP - 1:NTP], zt[:, :, 0:1], B_zt, reads=[B_zt], writes=[DB("XMT")], allow_slow_non_contiguous=True)
                xts = [cx.tile(es, [128, D], F32, "xt", loc) for _ in range(8)]
                uTs = [cx.tile(es, [128, 8, 512], BF16, "uT", loc) for _ in range(2)]
                sts = [cx.tile(es, [128, 512], F32, "st", loc) for _ in range(6)]

                def a1a_loads(gq):
                    nq, nsq, _ = groups[gq]
                    for sq in range(nsq):
                        k.dma('sp', xts[(gq % 2) * 4 + sq][0][:], H2_d[nq + sq * 128:nq + (sq + 1) * 128, :],
                              xts[(gq % 2) * 4 + sq][1], reads=[DB("H2")], writes=[xts[(gq % 2) * 4 + sq][1]])

                ptA = cx.psum(es, [128, 512], F32, "ptA")
                ptB = cx.psum(es, [128, 512], F32, "ptB")
                pms = [cx.psum(es, [128, 512], F32, "pm") for _ in range(6)]
                cnm = dict(im=0)

                def a1a_group(gidx):
                    n0, nsub, r = groups[gidx]
                    ntok = nsub * 128
                    uT, B_uT = uTs[gidx % 2]
                    a1a_loads(gidx)
                    yield
                    yield from acquire(cnm, 'st1')
                    for s_ in range(nsub):
                        xt, B_xt = xts[(gidx % 2) * 4 + s_]
                        yield from make_uT_g(1, r, 0, 8, None, [DB("H2")], xt, B_xt, ptA, ptB, uT, B_uT, s_)
                    cnm['st1'] = False
                    yield from acquire(cnm, 'st2')
                    for ec in range(32 if r == 0 else 16):
                        im = cnm['im']
                        pm, B_pm = pms[im % 6]
                        st, B_st = sts[im % 6]
                        cnm['im'] += 1
                        w, B_w = wins[ec // 8]
                        c0 = (ec % 8) * 128
                        k.op('pe', [lambda e, kc=kc: e.matmul(pm[:, 0:ntok], lhsT=w[:, kc, c0:c0 + 128], rhs=uT[:, kc, 0:ntok],
                                                              start=(kc == 0), stop=(kc == 7)) for kc in range(8)],
                             reads=[B_w, B_uT], writes=[B_pm])
                        yield
                        if ec < 16:
                            k.op('act', lambda e: e.copy(out=st[:, 0:ntok], in_=pm[:, 0:ntok]), writes=[B_pm, B_st])
                            k.dma('sp', XMT_d[:, ec, pcol(n0):pcol(n0) + ntok], st[:, 0:ntok], B_st, reads=[B_st],
                                  writes=[DB("XMT")])
                        else:
                            k.op('act', lambda e: e.activation(out=st[:, 0:ntok], in_=pm[:, 0:ntok], func=AF.Silu),
                                 writes=[B_pm, B_st])
                            k.dma('sp', P2_d[:, ec - 16, n0 - TC:n0 - TC + ntok], st[:, 0:ntok], B_st, reads=[B_st],
                                  writes=[DB("P2")])
                        yield
                    cnm['st2'] = False

                run_pipelined(a1a_group, len(groups), depth=2, skew=1)
                k.barrier(release=loc)

        def phase_a1b(groups):
            with ExitStack() as es:
                loc = []
                wq, B_wq = cx.tile(es, [128, 4, 4, 512], BF16, "wq", loc)
                wk, B_wk = cx.tile(es, [128, 4, 4, 512], BF16, "wk", loc)
                wv, B_wv = cx.tile(es, [128, 4, 4, 512], BF16, "wv", loc)
                for (w_, B_w, src) in ((wq, B_wq, ml_wq_d), (wk, B_wk, ml_wk_d), (wv, B_wv, ml_wv_d)):
                    k.dma('pool', w_[:], src.rearrange("h (dc p) n -> p h dc n", p=128), B_w, writes=[B_w])
                wgf, B_wgf = cx.tile(es, [128, 3, 16, 16], F32, "wgf", loc)
                wg, B_wg = cx.tile(es, [128, 3, 16, 16], BF16, "wg", loc)
                bg, B_bg = cx.tile(es, [128, 16], F32, "bg", loc)
                mlv, B_mlv = cx.tile(es, [128, 16, 6], F32, "mlv", loc)
                k.dma('sp', wgf[:], wg_d[:, :, :, :], B_wgf, writes=[B_wgf])
                k.dma('sp', bg[:], bg_d[:, :], B_bg, writes=[B_bg])
                k.dma('sp', mlv[:], mlv_d[:, :, :], B_mlv, writes=[B_mlv])
                k.op('dve', lambda e: e.tensor_scalar(out=wgf[:, 1, :, :], in0=wgf[:, 1, :, :], scalar1=1.0 / SCL,
                                                     scalar2=None, op0=ALU.mult), reads=[B_wgf], writes=[B_wgf])
                k.op('dve', lambda e: e.tensor_copy(out=wg[:], in_=wgf[:]), reads=[B_wgf], writes=[B_wg])
                XMs = [cx.tile(es, [128, 16, 258], F32, "XM", loc) for _ in range(2)]
                XMbs = [cx.tile(es, [128, 16, 256], BF16, "XMb", loc) for _ in range(2)]
                XCbs = [cx.tile(es, [128, 16, 256], BF16, "XCb", loc) for _ in range(2)]
                qT2 = [cx.tile(es, [128, 16, 256], BF16, "qT", loc) for _ in range(2)]
                kT2 = [cx.tile(es, [128, 16, 256], BF16, "kT", loc) for _ in range(2)]
                vT2 = [cx.tile(es, [128, 16, 256], BF16, "vT", loc) for _ in range(2)]
                tcs = [cx.tile(es, [128, 256], F32, "tc", loc) for _ in range(4)]
                xcfs = [cx.tile(es, [128, 256], F32, "xcf", loc) for _ in range(4)]
                p1s = [cx.tile(es, [128, 256], F32, "p1s", loc) for _ in range(3)]
                tms = [cx.tile(es, [128, 512], F32, "tm", loc) for _ in range(4)]
                tmbs = [cx.tile(es, [128, 512], BF16, "tmb", loc) for _ in range(4)]
                gss = [cx.tile(es, [128, 16], F32, "gs", loc) for _ in range(2)]
                pfs = [cx.psum(es, [128, 512], F32, "pf") for _ in range(4)]
                pts = [cx.psum(es, [128, 512], F32, "ptm") for _ in range(3)]
                pgt = cx.psum(es, [128, 512], F32, "pgt")
                i1 = 0
                ipf = 0
                ipt = 0
                itm = 0
                igs = 0
                def a1b_loads(gq):
                    pcq = pcol(groups[gq][0])
                    k.dma('sp', XMs[gq % 2][0][:], XMT_d[:, :, pcq - 1:pcq + 257], XMs[gq % 2][1], reads=[DB("XMT")],
                          writes=[XMs[gq % 2][1]])

                cn = dict(i1=0, ipf=0, ipt=0, itm=0, igs=0)

                def a1b_group(gidx):
                    n0, r = groups[gidx]
                    XM, B_XM = XMs[gidx % 2]
                    XMb, B_XMb = XMbs[gidx % 2]
                    XCb, B_XCb = XCbs[gidx % 2]
                    qT, B_qT = qT2[gidx % 2]
                    kT, B_kT = kT2[gidx % 2]
                    vT, B_vT = vT2[gidx % 2]
                    a1b_loads(gidx)
                    yield
                    yield from acquire(cn, 'st1')
                    k.op('act', lambda e: e.copy(out=XMb[:], in_=XM[:, :, 1:257]), reads=[B_XM], writes=[B_XMb])

                    def conv_tail(ec):
                        xcf, B_xcf = xcfs[ec % 4]
                        k.op('dve', lambda e: e.tensor_copy(out=XCb[:, ec, :], in_=xcf[:]), reads=[B_xcf], writes=[B_XCb])
                        if r == 0:
                            p1, B_p1 = p1s[cn['i1'] % 3]
                            cn['i1'] += 1
                            k.op('dve', lambda e: e.tensor_scalar(out=p1[:], in0=xcf[:], scalar1=mlv[:, ec, 4:5], scalar2=None,
                                                                  op0=ALU.mult), reads=[B_xcf, B_mlv], writes=[B_p1])
                            k.dma('sp', P1_d[:, ec, n0 - TC:n0 - TC + 256], p1[:], B_p1, reads=[B_p1], writes=[DB("P1")])

                    for ec in range(16):
                        tc_, B_tc = tcs[ec % 4]
                        xcf, B_xcf = xcfs[ec % 4]
                        k.op('dve', lambda e: e.tensor_scalar(out=tc_[:], in0=XM[:, ec, 0:256], scalar1=mlv[:, ec, 0:1],
                                                             scalar2=None, op0=ALU.mult),
                             reads=[B_XM, B_mlv], writes=[B_tc])
                        yield
                        k.op('dve', lambda e: e.scalar_tensor_tensor(out=tc_[:], in0=XM[:, ec, 1:257], scalar=mlv[:, ec, 1:2],
                                                                    in1=tc_[:], op0=ALU.mult, op1=ALU.add),
                             reads=[B_XM, B_mlv], writes=[B_tc])
                        yield
                        k.op('dve', lambda e: e.scalar_tensor_tensor(out=tc_[:], in0=XM[:, ec, 2:258], scalar=mlv[:, ec, 2:3],
                                                                    in1=tc_[:], op0=ALU.mult, op1=ALU.add),
                             reads=[B_XM, B_mlv], writes=[B_tc])
                        yield
                        k.op('act', lambda e: e.activation(out=xcf[:], in_=tc_[:], func=AF.Silu, bias=mlv[:, ec, 3:4], scale=1.0),
                             reads=[B_tc, B_mlv], writes=[B_xcf])
                        if ec >= 2:
                            conv_tail(ec - 2)
                        yield
                    conv_tail(14)
                    yield
                    conv_tail(15)
                    yield
                    cn['st1'] = False
                    yield from acquire(cn, 'st2')
                    for (dst, B_dst, w_, B_w, src, B_src, scl) in ((qT, B_qT, wq, B_wq, XCb, B_XCb, 1.0),
                                                                  (kT, B_kT, wk, B_wk, XCb, B_XCb, SCL),
                                                                  (vT, B_vT, wv, B_wv, XMb, B_XMb, 1.0)):
                        for h in range(4):
                            for op_ in range(2):
                                pf, B_pf = pfs[cn['ipf'] % 4]
                                cn['ipf'] += 1
                                fns = []
                                for o2 in range(2):
                                    oc = op_ * 2 + o2
                                    for dc in range(4):
                                        fns.append(lambda e, dc=dc, oc=oc, o2=o2: e.matmul(
                                            pf[:, o2 * 256:(o2 + 1) * 256], lhsT=w_[:, h, dc, oc * 128:(oc + 1) * 128],
                                            rhs=src[:, h * 4 + dc, :], start=(dc == 0), stop=(dc == 3)))
                                k.op('pe', fns, reads=[B_w, B_src], writes=[B_pf])
                                yield
                                dv = dst[:, h * 4 + op_ * 2:h * 4 + op_ * 2 + 2, :]
                                pv_ = pf[:, 0:512].rearrange("p (a t) -> p a t", a=2)
                                if cn['ipf'] % 2:
                                    k.op('act', lambda e: e.activation(out=dv, in_=pv_, func=AF.Copy, scale=scl),
                                         writes=[B_pf, B_dst])
                                else:
                                    k.op('dve', lambda e: e.tensor_scalar(out=dv, in0=pv_, scalar1=scl,
                                                                         scalar2=None, op0=ALU.mult), writes=[B_pf, B_dst])
                                yield
                    for s_ in range(2):
                        for (w_, B_w, src, B_src, scl, dd, dn) in ((wk, B_wk, XCb, B_XCb, SCL, Ktm_d, "Ktm"),
                                                                  (wv, B_wv, XMb, B_XMb, 1.0, Vtm_d, "Vtm")):
                            for h in range(4):
                                pt, B_pt = pts[cn['ipt'] % 3]
                                cn['ipt'] += 1
                                tm, B_tm = (tmbs if dn == "Ktm" else tms)[cn['itm'] % 4]
                                cn['itm'] += 1
                                k.op('pe', [lambda e, dc=dc: e.matmul(pt[:], lhsT=src[:, h * 4 + dc, s_ * 128:(s_ + 1) * 128],
                                                                      rhs=w_[:, h, dc, :], start=(dc == 0), stop=(dc == 3))
                                            for dc in range(4)], reads=[B_w, B_src], writes=[B_pt])
                                yield
                                if cn['itm'] % 2:
                                    k.op('act', lambda e: e.activation(out=tm[:], in_=pt[:], func=AF.Copy, scale=scl),
                                         writes=[B_pt, B_tm])
                                else:
                                    k.op('dve', lambda e: e.tensor_scalar(out=tm[:], in0=pt[:], scalar1=scl, scalar2=None,
                                                                         op0=ALU.mult), writes=[B_pt, B_tm])
                                yield
                                k.dma('sp', dd[n0 + s_ * 128:n0 + (s_ + 1) * 128, h * 512:(h + 1) * 512], tm[:], B_tm,
                                      reads=[B_tm], writes=[DB(dn)])
                        gs, B_gs = gss[cn['igs'] % 2]
                        cn['igs'] += 1
                        fns = []
                        for j, (src, B_src) in enumerate(((qT, B_qT), (kT, B_kT), (vT, B_vT))):
                            for ec in range(16):
                                fns.append(lambda e, j=j, ec=ec, src=src: e.matmul(
                                    pgt[0][:, 0:16], lhsT=src[:, ec, s_ * 128:(s_ + 1) * 128], rhs=wg[:, j, ec, :],
                                    start=(j == 0 and ec == 0), stop=(j == 2 and ec == 15)))
                        k.op('pe', fns, reads=[B_qT, B_kT, B_vT, B_wg], writes=[pgt[1]])
                        yield
                        k.op('dve', lambda e: e.tensor_tensor(out=gs[:], in0=pgt[0][:, 0:16], in1=bg[:], op=ALU.add),
                             reads=[B_bg], writes=[pgt[1], B_gs])
                        yield
                        nn = n0 + s_ * 128
                        k.dma('sp', G_d[nn:nn + 128, :], gs[:], B_gs, reads=[B_gs], writes=[DB("G")])
                        cblk = n0 // 256
                        jb = 0 if cblk == 0 else NT // 256 - cblk
                        k.dma('sp', Gb_d[jb * 256 + s_ * 128:jb * 256 + (s_ + 1) * 128, :], gs[:], B_gs, reads=[B_gs],
                              writes=[DB("Gb")])
                    if r == 0:
                        for s_ in range(2):
                            u = n0 // 128 + s_
                            k.dma('sp', QT_d[u].rearrange("p (a t) -> p a t", t=128), qT[:, :, s_ * 128:(s_ + 1) * 128], B_qT,
                                  reads=[B_qT], writes=[DB("QT")])
                            k.dma('sp', KT_d[u].rearrange("p (a t) -> p a t", t=128), kT[:, :, s_ * 128:(s_ + 1) * 128], B_kT,
                                  reads=[B_kT], writes=[DB("KT")])
                    cn['st2'] = False

                run_pipelined(a1b_group, len(groups), depth=2, skew=1)
                k.barrier(release=loc)

        CB = 256
        NBK = NT // CB

        def cmapB(d, j):
            if d == 0 or j == 0:
                return j
            return NBK - j

        def phase_g1(d, COL, B_COL, DECB, B_DECB, ABF, B_ABF, EKB, B_EKB):
            NP = NBK
            with ExitStack() as es:
                loc = []
                KEEP, B_KEEP = cx.tile(es, [NP, 4, CB], F32, "KEEP", loc)
                RST, B_RST = cx.tile(es, [NP, 4, CB], F32, "RST", loc)
                onesF, B_ones = cx.tile(es, [NP, 128], F32, "ones", loc)
                oneT, B_one = cx.tile(es, [NP, 1], F32, "one", loc)
                k.op('pool', lambda e: e.memset(KEEP[:], 1.0), writes=[B_KEEP])
                k.op('pool', lambda e: e.memset(KEEP[:, :, 0:1], 0.0), writes=[B_KEEP])
                k.op('pool', lambda e: e.memset(RST[:], 0.0), writes=[B_RST])
                k.op('pool', lambda e: e.memset(RST[:, :, 0:1], -1e30), writes=[B_RST])
                k.op('pool', lambda e: e.memset(onesF[:], 1.0), writes=[B_ones])
                k.op('pool', lambda e: e.memset(oneT[:], 1.0), writes=[B_one])
                GC, B_GC = cx.tile(es, [NP, CB, 16], F32, "GC", loc)
                IG, B_IG = cx.tile(es, [NP, 4, CB], F32, "IG", loc)
                FG, B_FG = cx.tile(es, [NP, 4, CB], F32, "FG", loc)
                NB, B_NB = cx.tile(es, [NP, 4, CB], F32, "NB", loc)
                Gg, B_Gg = cx.tile(es, [NP, 4, CB], F32, "Gg", loc)
                WE, B_WE = cx.tile(es, [NP, 4, CB], F32, "WE", loc)
                Mx, B_Mx = cx.tile(es, [NP, 4, CB], F32, "Mx", loc)
                Rr, B_Rr = cx.tile(es, [NP, 4, CB], F32, "Rr", loc)
                X5, B_X5 = cx.tile(es, [NP, 20, CB], F32, "X5", loc)
                Y5, B_Y5 = cx.tile(es, [NP, 20, CB], F32, "Y5", loc)
                sm, B_sm = cx.tile(es, [NP, 8, 4], F32, "smg", loc)
                rows, B_rows = cx.tile(es, [4, 4, NP], F32, "rows", loc)
                tmpd, B_tmpd = cx.tile(es, [NP, NP], F32, "tmpd", loc)
                pA = cx.psum(es, [128, 512], F32, "pA")
                pB = cx.psum(es, [128, 512], F32, "pB")
                pC = cx.psum(es, [128, 512], F32, "pC")
                idn = identF[0:NP, 0:NP]
                src = (G_d if d == 0 else Gb_d).rearrange("(c t) j -> c t j", t=CB)
                k.dma('sp', GC[:], src, B_GC, reads=[DB("G"), DB("Gb")], writes=[B_GC])
                if d == 0:
                    vi = GC[:, :, 0:4].rearrange("p t h -> p h t")
                    vf = GC[:, :, 4:8].rearrange("p t h -> p h t")
                else:
                    vi = GC[:, ::-1, 8:12].rearrange("p t h -> p h t")
                    vf = GC[:, ::-1, 12:16].rearrange("p t h -> p h t")
                k.op('dve', lambda e: e.tensor_copy(out=IG[:], in_=vi), reads=[B_GC], writes=[B_IG])
                k.op('dve', lambda e: e.tensor_copy(out=FG[:], in_=vf), reads=[B_GC], writes=[B_FG])
                k.op('act', lambda e: e.activation(out=FG[:], in_=FG[:], func=AF.Exp, scale=-1.0), writes=[B_FG])
                k.op('act', lambda e: e.activation(out=FG[:], in_=FG[:], func=AF.Ln, bias=oneT[:, 0:1], scale=1.0),
                     reads=[B_one], writes=[B_FG])
                fl = lambda t_: t_[:].rearrange("p h t -> p (h t)")
                k.op('dve', lambda e: e.tensor_tensor_scan(out=fl(NB), data0=fl(KEEP), data1=fl(FG), initial=0.0,
                                                          op0=ALU.mult, op1=ALU.add),
                     reads=[B_KEEP, B_FG], writes=[B_NB])
                k.op('dve', lambda e: e.tensor_tensor(out=Gg[:], in0=IG[:], in1=NB[:], op=ALU.add),
                     reads=[B_IG, B_NB], writes=[B_Gg])
                k.op('dve', lambda e: e.tensor_tensor(out=WE[:], in0=Gg[:], in1=NB[:, :, CB - 1:CB].to_broadcast([NP, 4, CB]),
                                                     op=ALU.subtract), reads=[B_Gg, B_NB], writes=[B_WE])
                k.op('dve', lambda e: e.tensor_reduce(out=sm[:, 1, :], in_=WE[:], axis=AX.X, op=ALU.max),
                     reads=[B_WE], writes=[B_sm])
                k.op('dve', lambda e: e.tensor_scalar(out=sm[:, 0, :], in0=NB[:, :, CB - 1], scalar1=-1.0, scalar2=None,
                                                     op0=ALU.mult), reads=[B_NB], writes=[B_sm])
                k.op('dve', lambda e: e.tensor_tensor_scan(out=fl(Mx), data0=fl(RST), data1=fl(Gg), initial=0.0,
                                                          op0=ALU.add, op1=ALU.max),
                     reads=[B_RST, B_Gg], writes=[B_Mx])
                k.op('pe', [lambda e: e.transpose(out=pA[0][0:4, 0:NP], in_=sm[:, 0, :], identity=idn),
                            lambda e: e.transpose(out=pA[0][0:4, NP:2 * NP], in_=sm[:, 1, :], identity=idn)],
                     reads=[B_sm, B_identF], writes=[pA[1]])
                k.op('dve', lambda e: e.tensor_copy(out=rows[:, 0:2, :],
                                                   in_=pA[0][0:4, 0:2 * NP].rearrange("p (a c) -> p a c", a=2)),
                     writes=[pA[1], B_rows])
                k.op('dve', lambda e: e.tensor_tensor_scan(out=rows[:, 2, :], data0=rows[:, 0, :], data1=rows[:, 1, :],
                                                          initial=0.0, op0=ALU.add, op1=ALU.max), writes=[B_rows])
                k.op('dve', [lambda e: e.memset(rows[:, 3, 0:1], 0.0),
                             lambda e: e.tensor_copy(out=rows[:, 3, 1:NP], in_=rows[:, 2, 0:NP - 1])], writes=[B_rows])
                k.op('pe', [lambda e: e.transpose(out=pB[0][0:NP, 0:4], in_=rows[:, 2, :], identity=identF[0:4, 0:4]),
                            lambda e: e.transpose(out=pB[0][0:NP, 4:8], in_=rows[:, 3, :], identity=identF[0:4, 0:4])],
                     reads=[B_rows, B_identF], writes=[pB[1]])
                k.op('dve', lambda e: e.tensor_copy(out=sm[:, 2:4, :], in_=pB[0][0:NP, 0:8].rearrange("p (a c) -> p a c", a=2)),
                     writes=[pB[1], B_sm])
                k.op('dve', lambda e: e.tensor_tensor(out=sm[:, 4, :], in0=sm[:, 0, :], in1=sm[:, 3, :], op=ALU.add),
                     writes=[B_sm])
                k.op('dve', lambda e: e.tensor_tensor(out=sm[:, 4, :], in0=sm[:, 4, :], in1=sm[:, 2, :], op=ALU.subtract),
                     writes=[B_sm])
                k.op('act', lambda e: e.activation(out=sm[:, 4, :], in_=sm[:, 4, :], func=AF.Exp), writes=[B_sm])
                bc = lambda a_: a_.unsqueeze(2).to_broadcast([NP, 4, CB])
                x5 = lambda i: X5[:, i * 4:(i + 1) * 4, :]
                lastb = lambda t_: t_[:, :, CB - 1:CB].to_broadcast([NP, 4, CB])
                k.op('dve', lambda e: e.tensor_tensor(out=x5(0), in0=WE[:], in1=bc(sm[:, 2, :]), op=ALU.subtract),
                     reads=[B_WE, B_sm], writes=[B_X5])
                k.op('dve', lambda e: e.tensor_tensor(out=Rr[:], in0=Mx[:], in1=bc(sm[:, 3, :]), op=ALU.max),
                     reads=[B_Mx, B_sm], writes=[B_Rr])
                k.op('dve', lambda e: e.tensor_tensor(out=x5(1), in0=Gg[:], in1=lastb(Rr), op=ALU.subtract),
                     reads=[B_Gg, B_Rr], writes=[B_X5])
                k.op('dve', lambda e: e.tensor_tensor(out=x5(2), in0=bc(sm[:, 3, :]), in1=Rr[:], op=ALU.subtract),
                     reads=[B_Rr, B_sm], writes=[B_X5])
                k.op('dve', lambda e: e.tensor_tensor(out=x5(3), in0=lastb(Rr), in1=Rr[:], op=ALU.subtract),
                     reads=[B_Rr], writes=[B_X5])
                k.op('dve', lambda e: e.tensor_tensor(out=x5(4), in0=NB[:], in1=Rr[:], op=ALU.subtract),
                     reads=[B_NB, B_Rr], writes=[B_X5])
                k.op('act', lambda e: e.activation(out=X5[:], in_=X5[:], func=AF.Exp), writes=[B_X5])
                if d == 0:
                    Ysrc, B_Ysrc = X5, B_X5
                else:
                    k.op('dve', lambda e: e.tensor_copy(out=Y5[:], in_=X5[:, :, ::-1]), reads=[B_X5], writes=[B_Y5])
                    Ysrc, B_Ysrc = Y5, B_Y5
                for bk, (pb_, B_pb) in enumerate((pA, pB)):
                    qs = list(range(bk * 10, bk * 10 + 10))
                    fns = []
                    for q in qs:
                        for ti in range(2):
                            o = ((q - bk * 10) * 2 + ti) * NP
                            fns.append(lambda e, q=q, ti=ti, o=o: e.transpose(out=pb_[:, o:o + NP],
                                                                             in_=Ysrc[:, q, ti * 128:(ti + 1) * 128], identity=idn))
                    k.op('pe', fns, reads=[B_Ysrc, B_identF], writes=[B_pb])
                    k.op('dve', lambda e: e.tensor_copy(
                        out=COL[:, qs[0]:qs[-1] + 1, :, :],
                        in_=pb_[:, 0:20 * NP].rearrange("p (a b c) -> p a b c", b=2, c=NP)),
                        writes=[B_pb, B_COL])
                k.op('dve', lambda e: e.tensor_copy(out=EKB[:], in_=COL[:, 0:4, :, :]), reads=[B_COL], writes=[B_EKB])
                k.op('dve', lambda e: e.tensor_copy(out=ABF[:], in_=COL[:, 4:8, :, :]), reads=[B_COL], writes=[B_ABF])
                for h in range(4):
                    k.op('dve', lambda e: e.tensor_scalar(out=tmpd[:], in0=idn, scalar1=sm[:, 4, h:h + 1], scalar2=None,
                                                         op0=ALU.mult), reads=[B_sm, B_identF], writes=[B_tmpd])
                    k.op('pe', lambda e: e.matmul(pC[0][:, h * NP:(h + 1) * NP], lhsT=onesF[:], rhs=tmpd[:], start=True,
                                                  stop=True), reads=[B_ones, B_tmpd], writes=[pC[1]])
                k.op('dve', lambda e: e.tensor_copy(out=DECB[:], in_=pC[0][:, 0:4 * NP].rearrange("p (a c) -> p a c", c=NP)),
                     writes=[pC[1], B_DECB])
                k.barrier(release=loc)

        def acquire(lk, name):
            while lk.get(name):
                yield
            lk[name] = True

        def run_pipelined(make_gen, n, depth, skew):
            active = []
            nxt = 0
            tick = 0
            last_start = -10 ** 9
            while nxt < n or active:
                if nxt < n and len(active) < depth and tick - last_start >= skew:
                    active.append(make_gen(nxt))
                    nxt += 1
                    last_start = tick
                for g in list(active):
                    try:
                        next(g)
                    except StopIteration:
                        active.remove(g)
                tick += 1

        def run_interleaved(gens):
            active = list(gens)
            while active:
                for g in list(active):
                    try:
                        next(g)
                    except StopIteration:
                        active.remove(g)

        def phase_scan_all():
            with ExitStack() as es:
                loc = []
                G1 = []
                for d in range(2):
                    COL, B_COL = cx.tile(es, [128, 20, 2, NBK], F32, "COL", loc)
                    DECB, B_DECB = cx.tile(es, [128, 4, NBK], F32, "DECB", loc)
                    ABF, B_ABF = cx.tile(es, [128, 4, 2, NBK], BF16, "ABF", loc)
                    EKB, B_EKB = cx.tile(es, [128, 4, 2, NBK], BF16, "EKB", loc)
                    msk, B_msk = cx.tile(es, [128, 128], F32, "msk", loc)
                    G1.append((COL, B_COL, DECB, B_DECB, ABF, B_ABF, EKB, B_EKB, msk, B_msk))
                for d in range(2):
                    COL, B_COL, DECB, B_DECB, ABF, B_ABF, EKB, B_EKB, msk, B_msk = G1[d]
                    phase_g1(d, COL, B_COL, DECB, B_DECB, ABF, B_ABF, EKB, B_EKB)
                    k.dma('sp', msk[:], masks_d[:, d, :], B_msk, writes=[B_msk])
                for hs0 in (0, 2):
                    with ExitStack() as es2:
                        loc2 = []
                        gens = [scan_dir(es2, loc2, d, hs0, G1[d]) for d in range(2)]
                        run_interleaved(gens)
                        k.barrier(release=loc2)
                k.barrier(release=loc)

        def scan_dir(es, loc, d, hs0, g1, nblocks=NBK):
            COL, B_COL, DECB, B_DECB, ABF, B_ABF, EKB, B_EKB, msk, B_msk = g1
            NHP = 2
            HW = NHP * 512
            C_, _ = cx.tile(es, [128, NHP * 4, 512], F32, "C", loc)
            Cb_, _ = cx.tile(es, [128, NHP * 4, 512], BF16, "Cb", loc)
            n_, B_n = cx.tile(es, [128, NHP * 4], F32, "n", loc)
            nb_, B_nb = cx.tile(es, [128, NHP * 4], BF16, "nb", loc)
            BC = [Buf("C_%d" % i) for i in range(NHP * 4)]
            BCb = [Buf("Cb_%d" % i) for i in range(NHP * 4)]
            k.op('pool', lambda e: e.memset(C_[:], 0.0), writes=BC)
            k.op('pool', lambda e: e.memset(Cb_[:], 0.0), writes=BCb)
            k.op('pool', lambda e: e.memset(n_[:], 0.0), writes=[B_n])
            k.op('pool', lambda e: e.memset(nb_[:], 0.0), writes=[B_nb])
            kbs = [[cx.tile(es, [128, HW], BF16, "kb", loc) for _ in range(2)] for _ in range(2)]
            kTs = [[cx.tile(es, [128, NHP * 4, 128], BF16, "kTt", loc) for _ in range(2)] for _ in range(2)]
            vws = [[cx.tile(es, [128, HW], BF16, "vw", loc) for _ in range(2)] for _ in range(2)]
            vas = [[cx.tile(es, [128, HW], BF16, "va", loc) for _ in range(2)] for _ in range(2)]
            qTs = [[cx.tile(es, [128, NHP * 4, 128], BF16, "qTt", loc) for _ in range(2)] for _ in range(2)]
            vts = [cx.tile(es, [128, HW], F32, "vt", loc) for _ in range(2)]
            PTs = [cx.tile(es, [128, NHP, 128], BF16, "PT", loc) for _ in range(3)]
            HOs = [cx.tile(es, [128, 512], F32, "HO", loc) for _ in range(3)]
            Tqs = [cx.tile(es, [128, 512], F32, "Tq", loc) for _ in range(3)]
            smS = [cx.tile(es, [128, 40], F32, "smS", loc) for _ in range(2)]
            pks = cx.psum(es, [128, 512], F32, "pks")
            bank3 = [cx.psum(es, [128, 512], F32, "pb3") for _ in range(3)]
            SM0 = 256
            cnt = dict(vt=0, pt=0, ho=0, tq=0, ss=0, bk=0)
            yield

            def loads(j):
                c = cmapB(d, j)
                lat = c >= 1
                tl = [2 * c, 2 * c + 1] if d == 0 else [2 * c + 1, 2 * c]
                for i, u in enumerate(tl):
                    kb, B_kb = kbs[j % 2][i]
                    k.dma('sp', kb[:], Ktm_d[u * 128:(u + 1) * 128, hs0 * 512:hs0 * 512 + HW], B_kb, reads=[DB("Ktm")], writes=[B_kb])
                    if lat:
                        kT, B_kT = kTs[j % 2][i]
                        qT, B_qT = qTs[j % 2][i]
                        k.dma('sp', kT[:], KT_d[u][:, hs0 * 512:hs0 * 512 + NHP * 512].rearrange("p (a t) -> p a t", t=128),
                              B_kT, reads=[DB("KT")], writes=[B_kT])
                        k.dma('sp', qT[:], QT_d[u][:, hs0 * 512:hs0 * 512 + NHP * 512].rearrange("p (a t) -> p a t", t=128),
                              B_qT, reads=[DB("QT")], writes=[B_qT])

            loads(0)
            yield
            for j in range(nblocks):
                c = cmapB(d, j)
                lat = c >= 1
                tl = [2 * c, 2 * c + 1] if d == 0 else [2 * c + 1, 2 * c]
                col = lambda q, h, ti: COL[:, q * 4 + h, ti, j:j + 1]
                for i, u in enumerate(tl):
                    ti = u - 2 * c
                    vt, B_vt = vts[cnt['vt'] % 2]
                    cnt['vt'] += 1
                    vw, B_vw = vws[j % 2][i]
                    va, B_va = vas[j % 2][i]
                    k.dma('sp', vt[:], Vtm_d[u * 128:(u + 1) * 128, hs0 * 512:hs0 * 512 + HW], B_vt, reads=[DB("Vtm")], writes=[B_vt])
                    for hl in range(NHP):
                        h = hs0 + hl
                        k.op('act', lambda e: e.activation(out=vw[:, hl * 512:(hl + 1) * 512], in_=vt[:, hl * 512:(hl + 1) * 512],
                                                           func=AF.Copy, scale=col(0, h, ti)),
                             reads=[B_vt, B_COL], writes=[B_vw])
                        if lat:
                            k.op('act', lambda e: e.activation(out=va[:, hl * 512:(hl + 1) * 512], in_=vt[:, hl * 512:(hl + 1) * 512],
                                                               func=AF.Copy, scale=col(1, h, ti)),
                                 reads=[B_vt, B_COL], writes=[B_va])
                    yield
                if j + 1 < nblocks:
                    loads(j + 1)
                if lat:
                    for i, u in enumerate(tl):
                        ti = u - 2 * c
                        qT, B_qT = qTs[j % 2][i]
                        sS, B_sS = smS[cnt['ss'] % 2]
                        cnt['ss'] += 1
                        PTl = []
                        for w in range(i + 1):
                            kT, B_kT = kTs[j % 2][w]
                            PT, B_PT = PTs[cnt['pt'] % 3]
                            cnt['pt'] += 1
                            PTl.append((PT, B_PT, w))
                            fns = []
                            for hl in range(NHP):
                                for kc in range(4):
                                    fns.append(lambda e, hl=hl, kc=kc: e.matmul(pks[0][:, hl * 128:(hl + 1) * 128],
                                                                               lhsT=kT[:, hl * 4 + kc, :], rhs=qT[:, hl * 4 + kc, :],
                                                                               start=(kc == 0), stop=(kc == 3)))
                            k.op('pe', fns, reads=[B_qT, B_kT], writes=[pks[1]])
                            yield
                            kqv = pks[0][:, 0:NHP * 128].rearrange("p (h t) -> p h t", h=NHP)
                            if w == i:
                                k.op('dve', lambda e: e.tensor_tensor(out=PT[:], in0=kqv,
                                                                     in1=msk[:].unsqueeze(1).to_broadcast([128, NHP, 128]), op=ALU.mult),
                                     reads=[B_msk], writes=[pks[1], B_PT])
                            else:
                                k.op('dve', lambda e: e.tensor_copy(out=PT[:], in_=kqv), writes=[pks[1], B_PT])
                            yield
                        fns = []
                        rds = [B_qT, B_nb, B_ABF]
                        for hl in range(NHP):
                            h = hs0 + hl
                            for wi, (PT, B_PT, w) in enumerate(PTl):
                                tw = tl[w] - 2 * c
                                fns.append(lambda e, hl=hl, h=h, PT=PT, tw=tw, wi=wi: e.matmul(
                                    pks[0][:, SM0 + 2 * hl:SM0 + 2 * hl + 1], lhsT=PT[:, hl, :], rhs=ABF[:, h, tw, j:j + 1],
                                    start=(wi == 0), stop=(wi == len(PTl) - 1)))
                                rds.append(B_PT)
                            for kc in range(4):
                                fns.append(lambda e, hl=hl, kc=kc: e.matmul(pks[0][:, SM0 + 2 * hl + 1:SM0 + 2 * hl + 2],
                                                                           lhsT=qT[:, hl * 4 + kc, :],
                                                                           rhs=nb_[:, hl * 4 + kc:hl * 4 + kc + 1],
                                                                           start=(kc == 0), stop=(kc == 3)))
                        k.op('pe', fns, reads=rds, writes=[pks[1]])
                        yield
                        NS = 2 * NHP
                        k.op('act', lambda e: e.copy(out=sS[:, 0:NS], in_=pks[0][:, SM0:SM0 + NS]), writes=[pks[1], B_sS])
                        yield
                        sv = lambda a_: sS[:, 8 * a_:8 * a_ + NHP]
                        ev = sS[:, 0:NS].rearrange("p (h two) -> p two h", two=2)
                        cq = lambda q: COL[:, q * 4 + hs0:q * 4 + hs0 + NHP, ti, j]
                        k.op('dve', lambda e: e.tensor_tensor(out=sv(1), in0=ev[:, 0, :], in1=cq(3), op=ALU.mult),
                             reads=[B_COL], writes=[B_sS])
                        k.op('dve', lambda e: e.tensor_tensor(out=sv(2), in0=ev[:, 1, :], in1=cq(2), op=ALU.mult),
                             reads=[B_COL], writes=[B_sS])
                        yield
                        k.op('dve', lambda e: e.tensor_tensor(out=sv(1), in0=sv(1), in1=sv(2), op=ALU.add), writes=[B_sS])
                        yield
                        k.op('dve', lambda e: e.tensor_scalar(out=sv(2), in0=sv(1), scalar1=-1.0, scalar2=None, op0=ALU.mult),
                             writes=[B_sS])
                        yield
                        k.op('dve', lambda e: e.tensor_tensor(out=sv(1), in0=sv(1), in1=sv(2), op=ALU.max), writes=[B_sS])
                        yield
                        k.op('dve', lambda e: e.tensor_tensor(out=sv(1), in0=sv(1), in1=cq(4), op=ALU.max),
                             reads=[B_COL], writes=[B_sS])
                        yield
                        k.op('dve', lambda e: e.reciprocal(out=sv(1), in_=sv(1)), writes=[B_sS])
                        yield
                        k.op('dve', lambda e: e.tensor_tensor(out=sv(2), in0=sv(1), in1=cq(3), op=ALU.mult),
                             reads=[B_COL], writes=[B_sS])
                        k.op('dve', lambda e: e.tensor_tensor(out=sv(3), in0=sv(1), in1=cq(2), op=ALU.mult),
                             reads=[B_COL], writes=[B_sS])
                        yield
                        Hd, Hn = (HF_d, "HF") if d == 0 else (HB_d, "HB")
                        for hl in range(NHP):
                            h = hs0 + hl
                            Tq, B_Tq = Tqs[cnt['tq'] % 3]
                            cnt['tq'] += 1
                            HO, B_HO = HOs[cnt['ho'] % 3]
                            cnt['ho'] += 1
                            pnum_, B_pnum = bank3[cnt['bk'] % 3]
                            pqc_, B_pqc = bank3[(cnt['bk'] + 1) % 3]
                            cnt['bk'] += 2
                            fns = []
                            rds = []
                            for wi, (PT, B_PT, w) in enumerate(PTl):
                                va, B_va = vas[j % 2][w]
                                fns.append(lambda e, PT=PT, va=va, wi=wi: e.matmul(pnum_[:], lhsT=PT[:, hl, :],
                                                                                  rhs=va[:, hl * 512:(hl + 1) * 512],
                                                                                  start=(wi == 0), stop=(wi == len(PTl) - 1)))
                                rds += [B_PT, B_va]
                            k.op('pe', fns, reads=rds, writes=[B_pnum])
                            k.op('pe', [lambda e, kc=kc: e.matmul(pqc_[:], lhsT=qT[:, hl * 4 + kc, :], rhs=Cb_[:, hl * 4 + kc, :],
                                                                  start=(kc == 0), stop=(kc == 3)) for kc in range(4)],
                                 reads=[B_qT] + BCb[hl * 4:hl * 4 + 4], writes=[B_pqc])
                            yield
                            k.op('act', lambda e: e.activation(out=Tq[:], in_=pqc_[:], func=AF.Copy, scale=sS[:, 24 + hl:25 + hl]),
                                 reads=[B_sS], writes=[B_pqc, B_Tq])
                            yield
                            k.op('dve', lambda e: e.scalar_tensor_tensor(out=HO[:], in0=pnum_[:], scalar=sS[:, 16 + hl:17 + hl],
                                                                        in1=Tq[:], op0=ALU.mult, op1=ALU.add),
                                 reads=[B_sS, B_Tq], writes=[B_pnum, B_HO])
                            yield
                            r0 = u * 128 - TC
                            k.dma('sp', Hd[r0:r0 + 128, h * 512:(h + 1) * 512], HO[:], B_HO, reads=[B_HO], writes=[DB(Hn)])
                if j == nblocks - 1:
                    continue
                fns = []
                for hl in range(NHP):
                    h = hs0 + hl
                    for kc in range(4):
                        for i, u in enumerate(tl):
                            ti = u - 2 * c
                            kb, B_kb = kbs[j % 2][i]
                            fns.append(lambda e, hl=hl, h=h, kc=kc, kb=kb, ti=ti, i=i: e.matmul(
                                pks[0][:, SM0 + 32 + hl * 4 + kc:SM0 + 33 + hl * 4 + kc],
                                lhsT=kb[:, hl * 512 + kc * 128:hl * 512 + (kc + 1) * 128], rhs=EKB[:, h, ti, j:j + 1],
                                start=(i == 0), stop=(i == 1)))
                k.op('pe', fns, reads=[B_EKB, kbs[j % 2][0][1], kbs[j % 2][1][1]], writes=[pks[1]])
                yield
                k.op('dve', lambda e: e.tensor_tensor(out=n_[:].rearrange("p (h c) -> p h c", h=NHP),
                                                     in0=n_[:].rearrange("p (h c) -> p h c", h=NHP),
                                                     in1=DECB[:, hs0:hs0 + NHP, j:j + 1].to_broadcast([128, NHP, 4]), op=ALU.mult),
                     reads=[B_DECB, B_nb], writes=[B_n])
                yield
                k.op('dve', lambda e: e.tensor_tensor(out=n_[:], in0=n_[:], in1=pks[0][:, SM0 + 32:SM0 + 32 + NHP * 4], op=ALU.add),
                     writes=[pks[1], B_n])
                yield
                k.op('dve', lambda e: e.tensor_copy(out=nb_[:], in_=n_[:]), reads=[B_n], writes=[B_nb])
                for hl in range(NHP):
                    h = hs0 + hl
                    for kc in range(4):
                        ii = hl * 4 + kc
                        pdc_, B_pdc = bank3[cnt['bk'] % 3]
                        cnt['bk'] += 1
                        fns = []
                        rds = []
                        for i in range(2):
                            kb, B_kb = kbs[j % 2][i]
                            vw, B_vw = vws[j % 2][i]
                            fns.append(lambda e, kb=kb, vw=vw, i=i: e.matmul(pdc_[:], lhsT=kb[:, hl * 512 + kc * 128:hl * 512 + (kc + 1) * 128],
                                                                           rhs=vw[:, hl * 512:(hl + 1) * 512], start=(i == 0), stop=(i == 1)))
                            rds += [B_kb, B_vw]
                        k.op('pe', fns, reads=rds, writes=[B_pdc])
                        k.op('act', lambda e: e.activation(out=C_[:, ii, :], in_=C_[:, ii, :], func=AF.Copy,
                                                           scale=DECB[:, h, j:j + 1]),
                             reads=[B_DECB, BCb[ii]], writes=[BC[ii]])
                        yield
                        k.op('dve', lambda e: e.tensor_tensor(out=C_[:, ii, :], in0=C_[:, ii, :], in1=pdc_[:], op=ALU.add),
                             writes=[B_pdc, BC[ii]])
                        yield
                        k.op('dve', lambda e: e.tensor_copy(out=Cb_[:, ii, :], in_=C_[:, ii, :]), reads=[BC[ii]], writes=[BCb[ii]])
                        yield

        def phase_c1(ntiles=T // 128):
            with ExitStack() as es:
                loc = []
                mwout, B_mwout = cx.tile(es, [128, 16, D], BF16, "mwout", loc)
                k.dma('pool', mwout[:], ml_wout_d.rearrange("(ec p) n -> p ec n", p=128), B_mwout, writes=[B_mwout])
                mlv, B_mlv = cx.tile(es, [128, 16, 6], F32, "mlv", loc)
                k.dma('sp', mlv[:], mlv_d[:, :, :], B_mlv, writes=[B_mlv])
                gt, B_gt = cx.tile(es, [128, D], F32, "gm1", loc)
                k.dma('sp', gt[:], gbc_d[4], B_gt, reads=[DB("gbc4")], writes=[B_gt])
                lng, B_lng = cx.tile(es, [128, D], F32, "lng", loc)
                lnb, B_lnb = cx.tile(es, [128, D], F32, "lnb", loc)
                k.dma('sp', lng[:], lng_d[:, 2, :], B_lng, writes=[B_lng])
                k.dma('sp', lnb[:], lnb_d[:, 2, :], B_lnb, writes=[B_lnb])
                ND = 3
                hfs = [cx.tile(es, [128, E], F32, "hf", loc) for _ in range(ND)]
                hbs = [cx.tile(es, [128, E], F32, "hb", loc) for _ in range(ND)]
                hns = [cx.tile(es, [128, E], BF16, "hn", loc) for _ in range(ND)]
                p1s = [cx.tile(es, [128, 16, 128], F32, "p1t", loc) for _ in range(ND)]
                p2s = [cx.tile(es, [128, 16, 128], F32, "p2t", loc) for _ in range(ND)]
                yTs = [cx.tile(es, [128, 16, 128], BF16, "yT", loc) for _ in range(ND)]
                sts = [cx.tile(es, [128, 48], F32, "stc", loc) for _ in range(ND)]
                xts = [cx.tile(es, [128, D], F32, "xt", loc) for _ in range(ND)]
                t1s = [cx.tile(es, [128, D], F32, "t1", loc) for _ in range(ND)]
                sms = [cx.tile(es, [128, 16], F32, "sm", loc) for _ in range(ND)]
                pths = [cx.psum(es, [128, 1024], BF16, "pth") for _ in range(4)]
                pos = [cx.psum(es, [128, 512], F32, "po") for _ in range(4)]
                def c1_loads(iq):
                    tq0 = iq * 128
                    k.dma('sp', hfs[iq % ND][0][:], HF_d[tq0:tq0 + 128, :], hfs[iq % ND][1], reads=[DB("HF")], writes=[hfs[iq % ND][1]])
                    k.dma('sp', hbs[iq % ND][0][:], HB_d[tq0:tq0 + 128, :], hbs[iq % ND][1], reads=[DB("HB")], writes=[hbs[iq % ND][1]])
                    k.dma('sp', p1s[iq % ND][0][:], P1_d[:, :, tq0:tq0 + 128], p1s[iq % ND][1], reads=[DB("P1")], writes=[p1s[iq % ND][1]])
                    k.dma('sp', p2s[iq % ND][0][:], P2_d[:, :, tq0:tq0 + 128], p2s[iq % ND][1], reads=[DB("P2")], writes=[p2s[iq % ND][1]])
                    k.dma('sp', xts[iq % ND][0][:], H2_d[TC + tq0:TC + tq0 + 128, :], xts[iq % ND][1], reads=[DB("H2")], writes=[xts[iq % ND][1]])

                def c1_tile(i):
                    t0 = i * 128
                    hf, B_hf = hfs[i % ND]
                    hb, B_hb = hbs[i % ND]
                    hn, B_hn = hns[i % ND]
                    p1, B_p1 = p1s[i % ND]
                    p2, B_p2 = p2s[i % ND]
                    yT, B_yT = yTs[i % ND]
                    st, B_st = sts[i % ND]
                    xt, B_xt = xts[i % ND]
                    t1, B_t1 = t1s[i % ND]
                    sm, B_sm = sms[i % ND]
                    c1_loads(i)
                    yield
                    k.op('dve', lambda e: e.tensor_tensor(out=hf[:], in0=hf[:], in1=hb[:], op=ALU.add), reads=[B_hb], writes=[B_hf])
                    yield
                    k.op('dve', [lambda e, h=h: e.bn_stats(out=st[:, h * 6:(h + 1) * 6], in_=hf[:, h * 512:(h + 1) * 512])
                                 for h in range(4)], reads=[B_hf], writes=[B_st])
                    yield
                    for h in range(4):
                        k.op('dve', lambda e: e.bn_aggr(out=st[:, 24 + 2 * h:26 + 2 * h], in_=st[:, h * 6:(h + 1) * 6]), writes=[B_st])
                        yield
                    mvv = st[:, 24:32].rearrange("p (h two) -> p two h", two=2)
                    k.op('act', lambda e: e.activation(out=st[:, 32:36], in_=mvv[:, 1, :], func=AF.Sqrt, bias=epsT[:, 0:1], scale=1.0),
                         reads=[B_eps], writes=[B_st])
                    yield
                    k.op('dve', lambda e: e.reciprocal(out=st[:, 32:36], in_=st[:, 32:36]), writes=[B_st])
                    yield
                    k.op('dve', lambda e: e.scalar_tensor_tensor(out=st[:, 36:40], in0=mvv[:, 0, :], scalar=-1.0, in1=st[:, 32:36],
                                                                op0=ALU.mult, op1=ALU.mult), writes=[B_st])
                    yield
                    for h in range(4):
                        k.op('act', lambda e: e.activation(out=hn[:, h * 512:(h + 1) * 512], in_=hf[:, h * 512:(h + 1) * 512],
                                                           func=AF.Identity, scale=st[:, 32 + h:33 + h], bias=st[:, 36 + h:37 + h]),
                             reads=[B_st, B_hf], writes=[B_hn])
                        yield
                    for half in range(2):
                        pth, B_pth = pths[(2 * i + half) % 4]
                        k.op('pe', [lambda e, q=q: e.transpose(out=pth[:, q * 128:(q + 1) * 128],
                                                              in_=hn[:, (half * 8 + q) * 128:(half * 8 + q + 1) * 128],
                                                              identity=identB[:]) for q in range(8)],
                             reads=[B_hn, B_identB], writes=[B_pth])
                        yield
                        for q in range(8):
                            ec = half * 8 + q
                            k.op('dve', lambda e: e.scalar_tensor_tensor(out=p1[:, ec, :], in0=pth[:, q * 128:(q + 1) * 128],
                                                                        scalar=mlv[:, ec, 5:6], in1=p1[:, ec, :],
                                                                        op0=ALU.mult, op1=ALU.add),
                                 reads=[B_mlv], writes=[B_pth, B_p1])
                            yield
                    k.op('dve', lambda e: e.tensor_tensor(out=yT[:], in0=p1[:], in1=p2[:], op=ALU.mult),
                         reads=[B_p1, B_p2], writes=[B_yT])
                    yield
                    po0, po1 = pos[(2 * i) % 4], pos[(2 * i + 1) % 4]
                    for hfx, (po, B_po) in enumerate((po0, po1)):
                        k.op('pe', [lambda e, ec=ec: e.matmul(po[:], lhsT=yT[:, ec, :], rhs=mwout[:, ec, hfx * 512:(hfx + 1) * 512],
                                                              start=(ec == 0), stop=(ec == 15)) for ec in range(16)],
                             reads=[B_yT, B_mwout], writes=[B_po])
                        yield
                    yield from post_norm_g(po0[0][:], po0[1], po1[0][:], po1[1], xt[:], B_xt,
                                           gt, B_gt, lng[:], B_lng, lnb[:], B_lnb, t1, B_t1, sm, B_sm,
                                           H3_d[t0:t0 + 128, :], DB("H3"))

                run_pipelined(c1_tile, ntiles, depth=ND, skew=18)
                k.barrier(release=loc)

        phase_mods()
        if dbg.get('full'):
            layer0_mixer(L0_GROUPS)
            phase_ffn(0, H1_d, "H1", H2_d, "H2", [(i * 256, i * 256, 1 if i == 0 else 0) for i in range(NT // 256)])
            phase_a1a(L0_GROUPS)
            phase_a1b([(i * 256, 1 if i == 0 else 0) for i in range(NT // 256)])
            phase_scan_all()
            phase_c1()
            phase_ffn(1, H3_d, "H3", out_d, "out", [(i * 256, i * 256, 0) for i in range(T // 256)])
        if dbg.get('a1'):
            ng = dbg.get('ngroups', 9)
            phase_a1a(L0_GROUPS[:ng])
            nt256 = (256 + (ng - 1) * 512) // 256
            phase_a1b([(i * 256, 1 if i == 0 else 0) for i in range(nt256)])
        if dbg.get('scan'):
            phase_scan_all()
        if dbg.get('c1'):
            phase_c1(dbg.get('c1tiles', T // 128))
        if dbg.get('l0mix'):
            layer0_mixer(L0_GROUPS[:dbg.get('ngroups', 9)])
        if dbg.get('ffn0'):
            groups = [(i * 256, i * 256, 1 if i == 0 else 0) for i in range(dbg.get('ngroups', 17))]
            phase_ffn(0, H1_d, "H1", H2_d, "H2", groups)
        k.barrier()
    return nc


POOL_W = (2, 4, 8, 16)
POOL_D = {0: list(range(-1, 4)), 1: list(range(-1, 5)), 2: list(range(-2, 6)), 3: list(range(-4, 8))}

def pool_consts():
    mats = []
    for gi, w in enumerate(POOL_W):
        h = w // 2
        for dl in POOL_D[gi]:
            m = np.zeros((2, 64, 8, 64), np.float32)
            for rr in range(2):
                srow = 2 * dl + rr
                for rho in range(8):
                    if rho - h <= srow <= rho + h - 1:
                        for c in range(64):
                            lo = max(c - h, 0); hi = min(c + h - 1, 63)
                            m[rr, lo:hi + 1, rho, c] = 1.0
            mats.append(m.reshape(128, 512))
    poolm = np.ascontiguousarray(np.stack(mats, 1))
    pc = np.zeros((128, 8, 256), np.float32)
    for gi, w in enumerate(POOL_W):
        h = w // 2
        for st in range(2):
            for p in range(128):
                n_src = st * 128 + p
                for n in range(256):
                    if n - h <= n_src <= n + h - 1:
                        pc[p, gi * 2 + st, n] = 1.0
    def cnt(L, w):
        t = np.arange(L)
        lo = np.clip(t - w // 2, 0, L); hi = np.clip(t + w - w // 2, 0, L)
        return (hi - lo).astype(np.float32)
    invc = np.zeros((4, 64, 64), np.float32); invcc = np.zeros((4, 256), np.float32)
    for gi, w in enumerate(POOL_W):
        c64 = cnt(64, w)
        invc[gi] = (np.float32(1.0) / c64)[:, None] * (np.float32(1.0) / c64)[None, :]
        invcc[gi] = np.float32(1.0) / cnt(256, w)
    invc = np.ascontiguousarray(np.broadcast_to(invc.reshape(1, 4, 4096), (128, 4, 4096)))
    invcc = np.ascontiguousarray(np.broadcast_to(invcc.reshape(1, 4, 256), (128, 4, 256)))
    return poolm, pc, invc, invcc

def abvec(inp):
    v = np.zeros((128, 4, 36), np.float32)
    v[:, :, 0:31] = inp['ab_conv_w'][0].reshape(31, 4, 128).transpose(2, 1, 0)
    v[:, :, 31] = inp['ab_conv_b'][0].reshape(4, 128).T
    v[:, :, 32] = inp['ab_norm_g'][0].reshape(4, 128).T
    v[:, :, 33] = inp['ab_norm_b'][0].reshape(4, 128).T
    v[:, :, 34] = inp['ab_pool_b'][0].reshape(4, 128).T
    v[:, :, 35] = inp['ab_pool_scale'][0].reshape(4, 128).T
    return v

def ml_consts(inp):
    wgt = inp['ml_w_gate'][0]
    wg = np.ascontiguousarray(wgt.reshape(2, 3, 16, 128, 8).transpose(3, 1, 2, 0, 4).reshape(128, 3, 16, 16)).astype(np.float32)
    bg = np.ascontiguousarray(np.broadcast_to(inp['ml_b_gate'][0].reshape(1, 16), (128, 16))).astype(np.float32)
    mlv = np.zeros((128, 16, 6), np.float32)
    for i in range(3):
        mlv[:, :, i] = inp['ml_conv_w'][0][i].reshape(16, 128).T
    mlv[:, :, 3] = inp['ml_conv_b'][0].reshape(16, 128).T
    mlv[:, :, 4] = inp['ml_skip'][0].reshape(16, 128).T
    mlv[:, :, 5] = inp['ml_norm_g'][0].reshape(16, 128).T
    masks = np.zeros((128, 2, 128), np.float32)
    s = np.arange(128)[:, None]; t = np.arange(128)[None, :]
    masks[:, 0, :] = (s <= t)
    masks[:, 1, :] = (s >= t)
    return dict(wg=wg, bg_bc=bg, mlvec=mlv, masks=masks, ml_w_in=inp['ml_w_in'][0], ml_wq=inp['ml_wq'][0], ml_wk=inp['ml_wk'][0],
                ml_wv=inp['ml_wv'][0], ml_w_out=inp['ml_w_out'][0])


def host_inputs(inp, b):
    c = inp['c'][b]
    cc = inp['c_ctx']
    cvec = np.stack([c.reshape(8, 128).T, cc.reshape(8, 128).T], axis=-1).astype(np.float32)
    mb = inp['mod_b']
    modb_col = np.ascontiguousarray(np.broadcast_to(mb.reshape(2, 48, 128).transpose(2, 0, 1)[..., None], (128, 2, 48, 2))).astype(np.float32)
    modb_bc = np.ascontiguousarray(np.broadcast_to(mb[None], (128, 2, 6144))).astype(np.float32)
    lng = np.ascontiguousarray(np.broadcast_to(inp['ln_g'].reshape(1, 4, 1024), (128, 4, 1024))).astype(np.float32)
    lnb = np.ascontiguousarray(np.broadcast_to(inp['ln_b'].reshape(1, 4, 1024), (128, 4, 1024))).astype(np.float32)
    return dict(x=np.ascontiguousarray(inp['x'][b]), ctx=np.ascontiguousarray(inp['ctx'][b]), cvec=np.ascontiguousarray(cvec),
                modb_col=modb_col, modb_bc=modb_bc, lng_bc=lng, lnb_bc=lnb)


_NC_CACHE = {}


def kernel(**inputs):
    inp = {k_: np.asarray(v, dtype=np.float32) for k_, v in inputs.items()}
    if 'nc' not in _NC_CACHE:
        _NC_CACHE['nc'] = build(dict(full=True))
    nc = _NC_CACHE['nc']
    poolm, pc, invc, invcc = pool_consts()
    shared = dict(mod_w=inp['mod_w'], ident=np.eye(128, dtype=np.float32), ffn_w_in=inp['ffn_w_in'], ffn_w_out=inp['ffn_w_out'],
                  ab_w_in=inp['ab_w_in'][0], ab_w_out=inp['ab_w_out'][0], ab_pool_w=inp['ab_pool_w'][0],
                  poolm=poolm, poolc=pc, invc=invc, invcc=invcc, abvec=abvec(inp))
    shared.update(ml_consts(inp))
    shared = {k_: np.ascontiguousarray(v, dtype=np.float32) for k_, v in shared.items()}
    in_maps = []
    for b in range(8):
        m = dict(shared)
        m.update(host_inputs(inp, b))
        in_maps.append(m)
    res = run_bass_kernel_spmd(nc, in_maps, core_ids=list(range(8)))
    return np.stack([np.asarray(r["out"], dtype=np.float32) for r in res.results], axis=0)
```
